# Optimizing a Trainium2 kernel written in Bass

```python
import math
import jax
import jax.numpy as jnp
from jax import lax
import numpy as np

D_MODEL = 4096
BATCH = 4
SEQ = 2048
DEPTH = 2
DEC_BATCH = 8
DEC_SEQ = 4
PAST_LEN = 16384
PAGE_SIZE = 128

N_EVEN = (DEPTH + 1) // 2
N_ODD = DEPTH // 2
N_MEM = 256
EPS = 1e-6
CHUNK = 64

HG_DK = 128
HG_DV = 128
HG_HEADS = D_MODEL // 2 // HG_DV
HG_W = HG_HEADS * HG_DV

NSA_HD = 128
NSA_HEADS = D_MODEL // 2 // NSA_HD
NSA_KVH = 4
NSA_G = NSA_HEADS // NSA_KVH
NSA_W = NSA_HEADS * NSA_HD
NSA_KV_W = NSA_KVH * NSA_HD
CMP_BLOCK = 32
CMP_STRIDE = 16
SEL_BLOCK = 64
N_SEL = 16
WINDOW = 512
Q_BLOCK = 128
SEL_Q_BLOCK = 32

ML_HEADS = D_MODEL // 512
ML_DK = D_MODEL // 2 // ML_HEADS
ML_DV = D_MODEL // ML_HEADS
ML_QK_W = ML_HEADS * ML_DK
ML_V_W = ML_HEADS * ML_DV

MEM_HEADS = 4
MEM_HD = 128
MEM_W = MEM_HEADS * MEM_HD

REL_BUCKETS = 32
REL_MAX_DIST = 128

EVEN_SPLITS = (HG_W, HG_W, HG_W, HG_W, NSA_W, 6 * NSA_KV_W, 3 * NSA_HEADS, NSA_W, MEM_W)
ODD_SPLITS = (ML_QK_W, ML_QK_W, ML_V_W, ML_V_W, ML_HEADS, ML_HEADS, ML_V_W, MEM_W)
F32 = jnp.float32

kernel_name = 'hybrid_hgrn2_nsa_mlstm_step'


def _rmsnorm(x, w):
    xf = x.astype(F32)
    y = xf * lax.rsqrt(jnp.mean(xf * xf, axis=-1, keepdims=True) + EPS)
    return (y * w.astype(F32)).astype(x.dtype)


def _split(a, sizes):
    offs = np.cumsum(sizes)[:-1].tolist()
    return jnp.split(a, offs, axis=-1)


def _masked_softmax(s, mask):
    s = jnp.where(mask, s, -jnp.inf)
    m = jnp.max(s, axis=-1, keepdims=True)
    m = jnp.where(jnp.isfinite(m), m, 0.0)
    p = jnp.exp(s - m)
    return p / jnp.maximum(p.sum(axis=-1, keepdims=True), jnp.finfo(F32).tiny)


def _rel_bucket(dist):
    n = jnp.maximum(dist, 0)
    exact = REL_BUCKETS // 2
    nf = jnp.maximum(n, 1).astype(F32)
    large = exact + (jnp.log(nf / exact) / math.log(REL_MAX_DIST / exact) * (REL_BUCKETS - exact)).astype(jnp.int32)
    return jnp.where(n < exact, n, jnp.minimum(large, REL_BUCKETS - 1))


def _rel_bias(rel_bias, dist):
    b = rel_bias[_rel_bucket(dist)]
    b = b.reshape(b.shape[:-1] + (NSA_KVH, NSA_G))
    return jnp.moveaxis(b, (-2, -1), (-4, -3)).astype(F32)


def _chunks(a, L):
    B, T = a.shape[:2]
    return a.reshape((B, T // L, L) + a.shape[2:]).swapaxes(0, 1)


def _unchunk(a):
    n, B, L = a.shape[:3]
    return a.swapaxes(0, 1).reshape((B, n * L) + a.shape[3:])


def _chunk_len(T):
    return CHUNK if T % CHUNK == 0 else T


def _hgrn_inputs(qa, fa, ia, lb):
    B, T = qa.shape[:2]
    shp = (B, T, HG_HEADS, HG_DK)
    q = jax.nn.silu(qa.astype(F32)).reshape(shp)
    lb = lb.reshape(HG_HEADS, HG_DK)
    logf = jnp.logaddexp(jnp.log(lb), jnp.log1p(-lb) + jax.nn.log_sigmoid(fa.astype(F32).reshape(shp)))
    k = -jnp.expm1(logf)
    v = ia.astype(F32).reshape(B, T, HG_HEADS, HG_DV)
    return q, k, v, logf


def _hgrn2_scan(q, k, v, logf, S0):
    L = _chunk_len(q.shape[1])
    causal = jnp.tril(jnp.ones((L, L), bool))

    def step(S, inp):
        qc, kc, vc, gc = inp
        Bc = jnp.cumsum(gc, axis=1)
        inter = jnp.einsum('blhk,bhkv->blhv', qc * jnp.exp(Bc), S)
        diff = Bc[:, :, None] - Bc[:, None]
        decay = jnp.exp(jnp.where(causal[None, :, :, None, None], diff, -jnp.inf))
        att = jnp.einsum('bthk,bshk,btshk->bhts', qc, kc, decay)
        intra = jnp.einsum('bhts,bshv->bthv', att, vc)
        Bl = Bc[:, -1]
        S = jnp.exp(Bl)[..., None] * S + jnp.einsum('bshk,bshv->bhkv', kc * jnp.exp(Bl[:, None] - Bc), vc)
        return S, inter + intra

    S, o = lax.scan(step, S0.astype(F32), (_chunks(q, L), _chunks(k, L), _chunks(v, L), _chunks(logf, L)))
    return _unchunk(o), S


def _mlstm_scan(q, k, v, log_i, log_f, C0, n0, m0):
    L = _chunk_len(q.shape[1])
    causal = jnp.tril(jnp.ones((L, L), bool))

    def step(carry, inp):
        C, n, m = carry
        qc, kc, vc, ic, fc = inp
        b = jnp.cumsum(fc, axis=1)
        dmat = jnp.where(causal[None, :, :, None], b[:, :, None] - b[:, None] + ic[:, None], -jnp.inf)
        inter = b + m[:, None]
        mt = jnp.maximum(inter, dmat.max(axis=2))
        w_in = jnp.exp(dmat - mt[:, :, None])
        w_x = jnp.exp(inter - mt)
        sw = jnp.einsum('bthk,bshk->btsh', qc, kc) * w_in
        num = w_x[..., None] * jnp.einsum('bthk,bhvk->bthv', qc, C) + jnp.einsum('btsh,bshv->bthv', sw, vc)
        den = w_x * jnp.einsum('bthk,bhk->bth', qc, n) + sw.sum(axis=2)
        h = num / jnp.maximum(jnp.abs(den), jnp.exp(-mt))[..., None]
        mL = mt[:, -1]
        w_end = jnp.exp(b[:, -1:] - b + ic - mL[:, None])
        dC = jnp.exp(b[:, -1] + m - mL)
        C = dC[..., None, None] * C + jnp.einsum('bsh,bshv,bshk->bhvk', w_end, vc, kc)
        n = dC[..., None] * n + jnp.einsum('bsh,bshk->bhk', w_end, kc)
        return (C, n, mL), h

    init = (C0.astype(F32), n0.astype(F32), m0.astype(F32))
    xs = (_chunks(q, L), _chunks(k, L), _chunks(v, L), _chunks(log_i, L), _chunks(log_f, L))
    (C, n, m), h = lax.scan(step, init, xs)
    return _unchunk(h), C, n, m


def _compress(rows, w1, b1, w2, pe):
    B, T = rows.shape[:2]
    r = CMP_BLOCK // CMP_STRIDE
    nch = T // CMP_STRIDE
    n_cmp = nch - r + 1
    rc = rows[:, :nch * CMP_STRIDE].reshape(B, nch, CMP_STRIDE, NSA_KVH, NSA_HD)
    pe_c = pe.reshape(r, CMP_STRIDE, NSA_HD)
    w1_c = w1.reshape(r, CMP_STRIDE, NSA_HD, NSA_HD)
    h = b1
    for j in range(r):
        h = h + jnp.einsum('bnskd,sde->bnke', rc[:, j:j + n_cmp] + pe_c[j][:, None, :], w1_c[j])
    return jnp.einsum('bnke,ed->bnkd', jax.nn.gelu(h), w2)


def _cmp_attn(q, qpos, kc, vc, rel_bias):
    kend = jnp.arange(kc.shape[1]) * CMP_STRIDE + CMP_BLOCK - 1
    dist = qpos[:, None] - kend[None, :]
    s = jnp.einsum('bqkgd,bnkd->bkgqn', q, kc).astype(F32) * NSA_HD ** -0.5 + _rel_bias(rel_bias, dist)
    p = _masked_softmax(s, dist >= 0)
    return jnp.einsum('bkgqn,bnkd->bqkgd', p.astype(vc.dtype), vc), p


def _cmp_to_slc(p, n_slc):
    r = SEL_BLOCK // CMP_STRIDE
    c = CMP_BLOCK // CMP_STRIDE
    front = c - 1
    back = max(r * n_slc + r - p.shape[-1], 0)
    pp = jnp.pad(p, [(0, 0)] * (p.ndim - 1) + [(front, back)])
    terms = [lax.slice_in_dim(pp, front + m - n, front + m - n + r * (n_slc - 1) + 1, stride=r, axis=p.ndim - 1)
             for m in range(r) for n in range(c)]
    return sum(terms[1:], terms[0])


def _select(p, qpos, n_slc):
    ps = _cmp_to_slc(p.sum(axis=2), n_slc)
    blk = jnp.arange(n_slc)
    cur = (qpos // SEL_BLOCK)[:, None]
    forced = (blk == 0) | (blk == cur) | (blk == cur - 1)
    score = jnp.where(forced, jnp.inf, ps)
    score = jnp.where(blk > cur, -jnp.inf, score)
    _, idx = lax.top_k(score, min(N_SEL, n_slc))
    idx = jnp.moveaxis(idx, 1, 2)
    valid = idx <= (qpos // SEL_BLOCK)[None, :, None, None]
    return idx, valid


def _sel_attn(q, qpos, idx, valid, ks, vs, rel_bias):
    B, Tq = q.shape[:2]
    nk = idx.shape[-1]
    kpos = idx[..., None] * SEL_BLOCK + jnp.arange(SEL_BLOCK)
    dist = qpos[None, :, None, None, None] - kpos
    mask = (valid[..., None] & (dist >= 0)).reshape(B, Tq, NSA_KVH, 1, nk * SEL_BLOCK)
    tbl = rel_bias.reshape(REL_BUCKETS, NSA_KVH, NSA_G)
    bias = tbl[_rel_bucket(dist), jnp.arange(NSA_KVH)[:, None, None]]
    s = jnp.einsum('bqkgd,bqknsd->bqkgns', q, ks).astype(F32) * NSA_HD ** -0.5 + jnp.moveaxis(bias, -1, 3).astype(F32)
    p = _masked_softmax(s.reshape(B, Tq, NSA_KVH, NSA_G, nk * SEL_BLOCK), mask)
    return jnp.einsum('bqkgm,bqkmd->bqkgd', p.astype(vs.dtype), vs.reshape(B, Tq, NSA_KVH, nk * SEL_BLOCK, NSA_HD))


def _sel_prompt(q, qpos, idx, valid, k, v, rel_bias):
    B, T = q.shape[:2]
    nb = T // SEL_Q_BLOCK

    def blocks(a):
        return a.reshape(B, T // SEL_BLOCK, SEL_BLOCK, NSA_KVH, NSA_HD).transpose(0, 3, 1, 2, 4)

    kt, vt = blocks(k), blocks(v)
    bi = jnp.arange(B)[:, None, None, None]
    hi = jnp.arange(NSA_KVH)[None, None, :, None]

    def qsplit(a):
        return a.reshape((B, nb, SEL_Q_BLOCK) + a.shape[2:]).swapaxes(0, 1)

    def step(args):
        qb, pb, ib, vb = args
        return _sel_attn(qb, pb, ib, vb, kt[bi, hi, ib], vt[bi, hi, ib], rel_bias)

    out = lax.map(step, (qsplit(q), qpos.reshape(nb, SEL_Q_BLOCK), qsplit(idx), qsplit(valid)))
    return out.swapaxes(0, 1).reshape(q.shape)


def _gather_pages(pool, page_table):
    g = pool[page_table]
    return g.reshape((g.shape[0], -1) + g.shape[3:])


def _gather_sel_sample(pool, new, page_table, idx):
    B = idx.shape[0]
    n_pages = PAST_LEN // PAGE_SIZE
    pos = idx[..., None] * SEL_BLOCK + jnp.arange(SEL_BLOCK)
    bi = jnp.arange(B)[:, None, None, None, None]
    hi = jnp.arange(NSA_KVH)[None, None, :, None, None]
    page = page_table[bi, jnp.clip(pos // PAGE_SIZE, 0, n_pages - 1)]
    past_rows = pool[page, pos % PAGE_SIZE, hi]
    new_rows = new[bi, jnp.clip(pos - PAST_LEN, 0, new.shape[1] - 1), hi]
    return jnp.where((pos < PAST_LEN)[..., None], past_rows, new_rows)


def _local_attn(q, k, v, qpos, kpos, rel_bias):
    dist = qpos[..., :, None] - kpos[..., None, :]
    mask = (dist >= 0) & (dist < WINDOW) & (kpos[..., None, :] >= 0)
    s = jnp.einsum('b...qkgd,b...skd->b...kgqs', q, k).astype(F32) * NSA_HD ** -0.5 + _rel_bias(rel_bias, dist)
    p = _masked_softmax(s, mask[..., None, None, :, :])
    return jnp.einsum('b...kgqs,b...skd->b...qkgd', p.astype(v.dtype), v)


def _window_prompt(q, k, v, rel_bias):
    B, T = q.shape[:2]
    nq = T // Q_BLOCK
    nk = WINDOW // Q_BLOCK + 1
    pad = [(0, 0), (WINDOW, 0), (0, 0), (0, 0)]

    def band(a):
        ab = jnp.pad(a, pad).reshape(B, nq + nk - 1, Q_BLOCK, NSA_KVH, NSA_HD)
        return jnp.stack([ab[:, j:j + nq] for j in range(nk)], axis=2).reshape(B, nq, nk * Q_BLOCK, NSA_KVH, NSA_HD)

    qpos = jnp.arange(T).reshape(nq, Q_BLOCK)
    kpos = (jnp.arange(nq) * Q_BLOCK - WINDOW)[:, None] + jnp.arange(nk * Q_BLOCK)[None, :]
    qb = q.reshape(B, nq, Q_BLOCK, NSA_KVH, NSA_G, NSA_HD)
    return _local_attn(qb, band(k), band(v), qpos, kpos, rel_bias).reshape(q.shape)


def _nsa_heads(qb, kvb):
    B, T = qb.shape[:2]
    q = qb.reshape(B, T, NSA_KVH, NSA_G, NSA_HD)
    kv = kvb.reshape(B, T, 6, NSA_KVH, NSA_HD)
    return q, [kv[:, :, j] for j in range(6)]


def _mem_attn(qm, mk, mv):
    B, T = qm.shape[:2]
    q = qm.reshape(B, T, MEM_HEADS, MEM_HD)
    s = jnp.einsum('bqhd,bmhd->bhqm', q, mk).astype(F32) * MEM_HD ** -0.5
    p = jax.nn.softmax(s, axis=-1)
    return jnp.einsum('bhqm,bmhd->bqhd', p.astype(mv.dtype), mv).reshape(B, T, MEM_W)


def _even_out(hn, oa, za, o3, gb, b_gate, zb, qm, mk, mv, g_norm, w_out):
    B, T = hn.shape[:2]
    oa = _rmsnorm(oa, g_norm).reshape(B, T, HG_W) * jax.nn.silu(za)
    g = jax.nn.sigmoid((gb + b_gate).astype(F32)).reshape(B, T, 3, NSA_KVH, NSA_G, 1)
    ob = g[:, :, 0] * o3[0] + g[:, :, 1] * o3[1] + g[:, :, 2] * o3[2]
    ob = ob.reshape(B, T, NSA_W) * jax.nn.silu(zb)
    om = _mem_attn(qm, mk, mv)
    y = jnp.einsum('bte,ed->btd', jnp.concatenate([oa, ob, om], axis=-1), w_out)
    return y.astype(hn.dtype)


def _even_prompt(hn, mk, mv, w_in, b_gate, w1, b1, w2, pe, lb, g_norm, w_out, rel_bias):
    B, T = hn.shape[:2]
    qa, fa, ia, za, qb, kvb, gb, zb, qm = _split(jnp.einsum('btd,de->bte', hn, w_in), EVEN_SPLITS)
    hq, hk, hv, hf = _hgrn_inputs(qa, fa, ia, lb)
    oa, S = _hgrn2_scan(hq, hk, hv, hf, jnp.zeros((B, HG_HEADS, HG_DK, HG_DV), F32))
    q, (kc, vc, ks, vs, kw, vw) = _nsa_heads(qb, kvb)
    qpos = jnp.arange(T)
    kcmp = _compress(kc, w1[0], b1[0], w2[0], pe[0])
    vcmp = _compress(vc, w1[1], b1[1], w2[1], pe[1])
    o_cmp, p = _cmp_attn(q, qpos, kcmp, vcmp, rel_bias)
    idx, valid = _select(p, qpos, T // SEL_BLOCK)
    o_sel = _sel_prompt(q, qpos, idx, valid, ks, vs, rel_bias)
    o_win = _window_prompt(q, kw, vw, rel_bias)
    y = _even_out(hn, oa, za, (o_cmp, o_sel, o_win), gb, b_gate, zb, qm, mk, mv, g_norm, w_out)
    wb = min(WINDOW, T)
    return y, (kc, vc, ks, vs, kw[:, -wb:], vw[:, -wb:], S)


def _even_sample(hn, mk, mv, page_table, pk_cmp, pv_cmp, pk_sel, pv_sel, wk, wv, S0,
                 w_in, b_gate, w1, b1, w2, pe, lb, g_norm, w_out, rel_bias):
    B, T = hn.shape[:2]
    qa, fa, ia, za, qb, kvb, gb, zb, qm = _split(jnp.einsum('btd,de->bte', hn, w_in), EVEN_SPLITS)
    hq, hk, hv, hf = _hgrn_inputs(qa, fa, ia, lb)
    oa, S = _hgrn2_scan(hq, hk, hv, hf, S0)
    q, (kc, vc, ks, vs, kw, vw) = _nsa_heads(qb, kvb)
    qpos = PAST_LEN + jnp.arange(T)
    kcmp = _compress(jnp.concatenate([_gather_pages(pk_cmp, page_table), kc], axis=1), w1[0], b1[0], w2[0], pe[0])
    vcmp = _compress(jnp.concatenate([_gather_pages(pv_cmp, page_table), vc], axis=1), w1[1], b1[1], w2[1], pe[1])
    o_cmp, p = _cmp_attn(q, qpos, kcmp, vcmp, rel_bias)
    idx, valid = _select(p, qpos, -(-(PAST_LEN + T) // SEL_BLOCK))
    o_sel = _sel_attn(q, qpos, idx, valid, _gather_sel_sample(pk_sel, ks, page_table, idx),
                      _gather_sel_sample(pv_sel, vs, page_table, idx), rel_bias)
    wb = wk.shape[1]
    kk = jnp.concatenate([wk, kw], axis=1)
    vv = jnp.concatenate([wv, vw], axis=1)
    o_win = _local_attn(q, kk, vv, qpos, PAST_LEN - wb + jnp.arange(wb + T), rel_bias)
    y = _even_out(hn, oa, za, (o_cmp, o_sel, o_win), gb, b_gate, zb, qm, mk, mv, g_norm, w_out)
    return y, (kc, vc, ks, vs, kk[:, -wb:], vv[:, -wb:], S)


def _odd_mix(hn, mk, mv, C0, n0, m0, w_in, b_if, g_norm, w_out):
    B, T = hn.shape[:2]
    q, k, v, og, ig, fg, z, qm = _split(jnp.einsum('btd,de->bte', hn, w_in), ODD_SPLITS)
    q = q.astype(F32).reshape(B, T, ML_HEADS, ML_DK)
    k = k.astype(F32).reshape(B, T, ML_HEADS, ML_DK) * ML_DK ** -0.5
    v = v.astype(F32).reshape(B, T, ML_HEADS, ML_DV)
    log_i = ig.astype(F32) + b_if[0]
    log_f = jax.nn.log_sigmoid(fg.astype(F32) + b_if[1])
    h, C, n, m = _mlstm_scan(q, k, v, log_i, log_f, C0, n0, m0)
    h = _rmsnorm(h, g_norm) * jax.nn.sigmoid(og.astype(F32)).reshape(B, T, ML_HEADS, ML_DV)
    h = h.reshape(B, T, ML_V_W) * jax.nn.silu(z)
    om = _mem_attn(qm, mk, mv)
    y = jnp.einsum('bte,ed->btd', jnp.concatenate([h, om], axis=-1), w_out)
    return y.astype(hn.dtype), (C, n, m)


def _stack(lst, i):
    return jnp.stack([t[i] for t in lst])


def setup_inputs(seed: int = 0) -> dict:
    key = jax.random.key(seed)
    ks = iter(jax.random.split(key, 48))

    def nrm(shape, scale=1.0):
        return scale * jax.random.normal(next(ks), shape, F32)

    n_pages = PAST_LEN // PAGE_SIZE
    n_used = DEC_BATCH * n_pages
    n_pool = n_used + (n_used + 3) // 4
    win_buf = min(WINDOW, PAST_LEN)
    even_in = sum(EVEN_SPLITS)
    odd_in = sum(ODD_SPLITS)
    even_out = HG_W + NSA_W + MEM_W
    odd_out = ML_V_W + MEM_W
    page_table = jax.random.permutation(next(ks), n_pool)[:n_used].reshape(DEC_BATCH, n_pages).astype(jnp.int32)
    forget_bias = jnp.linspace(3.0, 6.0, ML_HEADS, dtype=F32)
    return {
        'x_prompt': nrm((BATCH, SEQ, D_MODEL)),
        'x_sample': nrm((DEC_BATCH, DEC_SEQ, D_MODEL)),
        'cache_mem_k': nrm((DEPTH, DEC_BATCH, N_MEM, MEM_HEADS, MEM_HD)),
        'cache_mem_v': nrm((DEPTH, DEC_BATCH, N_MEM, MEM_HEADS, MEM_HD)),
        'cache_cmp_k': nrm((N_EVEN, n_pool, PAGE_SIZE, NSA_KVH, NSA_HD)),
        'cache_cmp_v': nrm((N_EVEN, n_pool, PAGE_SIZE, NSA_KVH, NSA_HD)),
        'cache_sel_k': nrm((N_EVEN, n_pool, PAGE_SIZE, NSA_KVH, NSA_HD)),
        'cache_sel_v': nrm((N_EVEN, n_pool, PAGE_SIZE, NSA_KVH, NSA_HD)),
        'cache_win_k': nrm((N_EVEN, DEC_BATCH, win_buf, NSA_KVH, NSA_HD)),
        'cache_win_v': nrm((N_EVEN, DEC_BATCH, win_buf, NSA_KVH, NSA_HD)),
        'state_hgrn': nrm((N_EVEN, DEC_BATCH, HG_HEADS, HG_DK, HG_DV), 0.5),
        'state_mlstm_c': nrm((N_ODD, DEC_BATCH, ML_HEADS, ML_DV, ML_DK), 0.3),
        'state_mlstm_n': nrm((N_ODD, DEC_BATCH, ML_HEADS, ML_DK), 0.3),
        'state_mlstm_m': nrm((N_ODD, DEC_BATCH, ML_HEADS)),
        'page_table': page_table,
        'mem_prompt': nrm((BATCH, N_MEM, D_MODEL)),
        'norm_w': 1.0 + nrm((DEPTH, D_MODEL), 0.02),
        'mem_norm_w': 1.0 + nrm((DEPTH, D_MODEL), 0.02),
        'final_norm_w': 1.0 + nrm((D_MODEL,), 0.02),
        'rel_bias': nrm((REL_BUCKETS, NSA_HEADS), 0.5),
        'w_mem_kv': nrm((DEPTH, D_MODEL, 2 * MEM_W), D_MODEL ** -0.5),
        'w_in_even': nrm((N_EVEN, D_MODEL, even_in), D_MODEL ** -0.5),
        'b_nsa_gate': nrm((N_EVEN, 3 * NSA_HEADS), 0.1),
        'w_cmp1': nrm((N_EVEN, 2, CMP_BLOCK, NSA_HD, NSA_HD), (CMP_BLOCK * NSA_HD) ** -0.5),
        'b_cmp1': nrm((N_EVEN, 2, NSA_HD), 0.02),
        'w_cmp2': nrm((N_EVEN, 2, NSA_HD, NSA_HD), NSA_HD ** -0.5),
        'pe_cmp': nrm((N_EVEN, 2, CMP_BLOCK, NSA_HD), 0.1),
        'hgrn_lb_logits': nrm((DEPTH + 1, HG_W), 0.5),
        'hgrn_norm_w': 1.0 + nrm((N_EVEN, HG_DV), 0.02),
        'w_out_even': nrm((N_EVEN, even_out, D_MODEL), even_out ** -0.5),
        'w_in_odd': nrm((N_ODD, D_MODEL, odd_in), D_MODEL ** -0.5),
        'b_mlstm_if': jnp.stack([nrm((N_ODD, ML_HEADS), 0.1), forget_bias + nrm((N_ODD, ML_HEADS), 0.1)], axis=1),
        'mlstm_norm_w': 1.0 + nrm((N_ODD, ML_HEADS, ML_DV), 0.02),
        'w_out_odd': nrm((N_ODD, odd_out, D_MODEL), odd_out ** -0.5),
    }


def reference(x_prompt, x_sample, cache_mem_k, cache_mem_v, cache_cmp_k, cache_cmp_v, cache_sel_k, cache_sel_v,
              cache_win_k, cache_win_v, state_hgrn, state_mlstm_c, state_mlstm_n, state_mlstm_m, page_table,
              mem_prompt, norm_w, mem_norm_w, final_norm_w, rel_bias, w_mem_kv, w_in_even, b_nsa_gate,
              w_cmp1, b_cmp1, w_cmp2, pe_cmp, hgrn_lb_logits, hgrn_norm_w, w_out_even, w_in_odd, b_mlstm_if,
              mlstm_norm_w, w_out_odd):
    lbs = jnp.cumsum(jax.nn.softmax(hgrn_lb_logits.astype(F32), axis=0), axis=0)
    hp, hs = x_prompt, x_sample
    mem_new, even_p, even_s, odd_p, odd_s = [], [], [], [], []
    for l in range(DEPTH):
        npre = _rmsnorm(hp, norm_w[l])
        nsam = _rmsnorm(hs, norm_w[l])
        mkv = jnp.einsum('bmd,de->bme', _rmsnorm(mem_prompt, mem_norm_w[l]), w_mem_kv[l])
        mk_p, mv_p = [a.reshape(a.shape[0], a.shape[1], MEM_HEADS, MEM_HD) for a in _split(mkv, (MEM_W, MEM_W))]
        mem_new.append((mk_p, mv_p))
        mk_s, mv_s = cache_mem_k[l], cache_mem_v[l]
        if l % 2 == 0:
            e = l // 2
            wts = (w_in_even[e], b_nsa_gate[e], w_cmp1[e], b_cmp1[e], w_cmp2[e], pe_cmp[e], lbs[l],
                   hgrn_norm_w[e], w_out_even[e], rel_bias)
            yp, st_p = _even_prompt(npre, mk_p, mv_p, *wts)
            ys, st_s = _even_sample(nsam, mk_s, mv_s, page_table, cache_cmp_k[e], cache_cmp_v[e], cache_sel_k[e],
                                    cache_sel_v[e], cache_win_k[e], cache_win_v[e], state_hgrn[e], *wts)
            even_p.append(st_p)
            even_s.append(st_s)
        else:
            o = l // 2
            wts = (w_in_odd[o], b_mlstm_if[o], mlstm_norm_w[o], w_out_odd[o])
            bp = hp.shape[0]
            yp, st_p = _odd_mix(npre, mk_p, mv_p, jnp.zeros((bp, ML_HEADS, ML_DV, ML_DK), F32),
                                jnp.zeros((bp, ML_HEADS, ML_DK), F32), jnp.zeros((bp, ML_HEADS), F32), *wts)
            ys, st_s = _odd_mix(nsam, mk_s, mv_s, state_mlstm_c[o], state_mlstm_n[o], state_mlstm_m[o], *wts)
            odd_p.append(st_p)
            odd_s.append(st_s)
        hp = hp + yp
        hs = hs + ys
    y_prompt = _rmsnorm(hp, final_norm_w)
    y_sample = _rmsnorm(hs, final_norm_w)
    return (y_prompt, y_sample,
            _stack(mem_new, 0), _stack(mem_new, 1),
            _stack(even_p, 0), _stack(even_p, 1), _stack(even_p, 2), _stack(even_p, 3),
            _stack(even_p, 4), _stack(even_p, 5), _stack(even_p, 6),
            _stack(odd_p, 0), _stack(odd_p, 1), _stack(odd_p, 2),
            _stack(even_s, 0), _stack(even_s, 1), _stack(even_s, 2), _stack(even_s, 3),
            _stack(even_s, 4), _stack(even_s, 5), _stack(even_s, 6),
            _stack(odd_s, 0), _stack(odd_s, 1), _stack(odd_s, 2))
```

```python
import numpy as np
import concourse.bass as bass
import concourse.mybir as mybir

F32 = mybir.dt.float32
BF16 = mybir.dt.bfloat16
I32 = mybir.dt.int32
U32 = mybir.dt.uint32
AF = mybir.ActivationFunctionType
ALU = mybir.AluOpType
AX = mybir.AxisListType

ENGS = ("tensor", "vector", "scalar", "gpsimd", "sync")


class Buf:
    def __init__(self, prog, handle, name):
        self.prog = prog
        self.h = handle
        self.name = name
        self.last_w = None
        self.readers = []
        self.sem = None
        self.ndma = 0

    def __getitem__(self, idx):
        return V(self, self.h[idx])

    def ap(self):
        return V(self, self.h[:] if not isinstance(self.h, bass.AP) else self.h)


class V:
    def __init__(self, buf, ap):
        self.bufs = tuple(buf) if isinstance(buf, (tuple, list)) else (buf,)
        self.ap = ap

    @property
    def buf(self):
        return self.bufs[0]

    def __getitem__(self, idx):
        return V(self.bufs, self.ap[idx])

    def rearrange(self, s, **kw):
        return V(self.bufs, self.ap.rearrange(s, **kw))

    def to_broadcast(self, shape):
        return V(self.bufs, self.ap.to_broadcast(shape))

    def broadcast_to(self, shape):
        return V(self.bufs, self.ap.broadcast_to(shape))

    def partition_broadcast(self, n):
        return V(self.bufs, self.ap.partition_broadcast(n))

    def bitcast(self, dt):
        return V(self.bufs, self.ap.bitcast(dt))

    def unsqueeze(self, a):
        return V(self.bufs, self.ap.unsqueeze(a))

    def with_bufs(self, bufs):
        return V(bufs, self.ap)

    @property
    def shape(self):
        return self.ap.shape


class Op:
    __slots__ = ("eng", "fn", "deps", "idx", "eidx", "is_dma", "sem", "semval", "signal", "waits")

    def __init__(self, eng, fn, is_dma):
        self.eng = eng
        self.fn = fn
        self.is_dma = is_dma
        self.deps = []
        self.signal = False
        self.sem = None
        self.semval = None
        self.waits = None


WRITE_KEYS = ("out", "out_max", "out_indices", "accum_out")


class Prog:
    def __init__(self, nc, stack):
        self.nc = nc
        self.stack = stack
        self.ops = []
        self.eng_ops = {e: [] for e in ENGS}
        self.nbuf = 0
        self.out_dma_ops = []
        self.gstack = stack
        self.sem_free = []
        self.phase_bufs = []
        self.all_dma_bufs = []

    def begin_phase(self):
        from contextlib import ExitStack as _ES
        self.phase_stack = _ES()
        self.stack = self.phase_stack
        self.phase_bufs = []

    def end_phase(self):
        self.barrier()
        for b in self.phase_bufs:
            if b.sem is not None:
                self.sem_free.append([b.sem, b.ndma])
                b.sem = None
        self.phase_bufs = []
        self.phase_stack.close()
        self.stack = self.gstack

    def barrier(self):
        last = [ops[-1] for ops in self.eng_ops.values() if ops]
        dmas = [b.last_dma for b in self.all_dma_bufs if getattr(b, "last_dma", None) is not None]
        deps = last + dmas
        for e in ENGS:
            op = Op(e, lambda eng: eng.nop(), False)
            op.deps = [d for d in deps]
            op.idx = len(self.ops)
            op.eidx = len(self.eng_ops[e])
            self.ops.append(op)
            self.eng_ops[e].append(op)

    def sbuf(self, shape, dtype, name=None):
        self.nbuf += 1
        name = f"{name or 'sb'}_{self.nbuf}"
        h = self.stack.enter_context(self.nc.sbuf_tensor(name, list(shape), dtype))
        b = Buf(self, h, name)
        self.phase_bufs.append(b)
        return b

    def psum(self, shape, dtype=F32, name=None):
        self.nbuf += 1
        name = f"{name or 'ps'}_{self.nbuf}"
        h = self.stack.enter_context(self.nc.psum_tensor(name, list(shape), dtype))
        return Buf(self, h, name)

    def dram(self, name, shape, dtype, kind="Internal"):
        h = self.nc.dram_tensor(name, list(shape), dtype, kind=kind)
        return Buf(self, h.ap(), name)

    def _record(self, eng, fn, reads, writes, is_dma=False, dma_buf=None):
        op = Op(eng, fn, is_dma)
        deps = []
        for b in reads:
            if b.last_w is not None:
                deps.append(b.last_w)
        for b in writes:
            if b.last_w is not None:
                deps.append(b.last_w)
            deps.extend(b.readers)
        if is_dma:
            if dma_buf.sem is None:
                if self.sem_free:
                    dma_buf.sem, dma_buf.ndma = self.sem_free.pop()
                else:
                    dma_buf.sem = self.gstack.enter_context(self.nc.semaphore(f"d_{dma_buf.name}"))
                    dma_buf.ndma = 0
                dma_buf.last_dma = None
                self.all_dma_bufs.append(dma_buf)
            if dma_buf.last_dma is not None:
                deps.append(dma_buf.last_dma)
            dma_buf.ndma += 1
            dma_buf.last_dma = op
            op.sem = dma_buf.sem
            op.semval = 16 * dma_buf.ndma
        seen = set()
        for d in deps:
            if d is op or id(d) in seen:
                continue
            seen.add(id(d))
            op.deps.append(d)
        for b in reads:
            if b not in writes:
                b.readers.append(op)
        for b in writes:
            b.last_w = op
            b.readers = []
        op.idx = len(self.ops)
        op.eidx = len(self.eng_ops[eng])
        self.ops.append(op)
        self.eng_ops[eng].append(op)
        return op

    def op(self, eng, method, *args, extra_reads=(), extra_writes=(), **kw):
        reads, writes = list(b for b in extra_reads), list(b for b in extra_writes)
        real_kw = {}
        for k, v in kw.items():
            if isinstance(v, V):
                real_kw[k] = v.ap
                for vb in v.bufs:
                    if k in WRITE_KEYS:
                        if vb not in writes:
                            writes.append(vb)
                    else:
                        if vb not in reads:
                            reads.append(vb)
            else:
                real_kw[k] = v
        real_args = []
        for a in args:
            if isinstance(a, V):
                raise ValueError("pass V's as kwargs")
            real_args.append(a)

        def fn(e, method=method, real_args=real_args, real_kw=real_kw):
            return getattr(e, method)(*real_args, **real_kw)

        return self._record(eng, fn, reads, writes)

    def dma(self, out, in_, eng="sync", **kw):
        def is_dram(v):
            return isinstance(v.buf.h, bass.AP)
        dma_buf = out.buf if not is_dram(out) else in_.buf
        oap, iap = out.ap, in_.ap

        def fn(e, oap=oap, iap=iap, kw=kw):
            return e.dma_start(out=oap, in_=iap, **kw)

        op = self._record(eng, fn, list(in_.bufs), list(out.bufs), is_dma=True, dma_buf=dma_buf)
        return op

    def idma(self, out, in_, offs, eng="gpsimd"):
        oap, iap, fap = out.ap, in_.ap, offs.ap

        def fn(e):
            return e.indirect_dma_start(out=oap, out_offset=None, in_=iap,
                                        in_offset=bass.IndirectOffsetOnAxis(ap=fap, axis=0))

        return self._record(eng, fn, list(in_.bufs) + list(offs.bufs), list(out.bufs), is_dma=True, dma_buf=out.buf)

    def mm(self, out, lhsT, rhs, start=True, stop=True, **kw):
        return self.op("tensor", "matmul", out=out, lhsT=lhsT, rhs=rhs, start=start, stop=stop, **kw)

    def transpose(self, out, in_, identity):
        return self.op("tensor", "transpose", out=out, in_=in_, identity=identity)

    def act(self, out, in_, func, eng="scalar", **kw):
        return self.op(eng, "activation", out=out, in_=in_, func=func, **kw)

    def tt(self, out, in0, in1, op, eng="vector"):
        return self.op(eng, "tensor_tensor", out=out, in0=in0, in1=in1, op=op)

    def ts(self, out, in0, scalar1, op0, scalar2=None, op1=None, eng="vector", **kw):
        if op1 is None:
            return self.op(eng, "tensor_scalar", out=out, in0=in0, scalar1=scalar1, scalar2=scalar2, op0=op0, **kw)
        return self.op(eng, "tensor_scalar", out=out, in0=in0, scalar1=scalar1, scalar2=scalar2, op0=op0, op1=op1, **kw)

    def stt(self, out, in0, scalar, in1, op0, op1, eng="vector", **kw):
        return self.op(eng, "scalar_tensor_tensor", out=out, in0=in0, scalar=scalar, in1=in1, op0=op0, op1=op1, **kw)

    def copy(self, out, in_, eng="vector"):
        if eng == "scalar":
            return self.op(eng, "copy", out=out, in_=in_)
        return self.op(eng, "tensor_copy", out=out, in_=in_)

    def memset(self, out, val, eng="vector"):
        return self.op(eng, "memset", ap=None, constant=val, extra_writes=[out.buf]) if False else self._memset(out, val, eng)

    def _memset(self, out, val, eng):
        oap = out.ap

        def fn(e):
            return e.memset(oap, val)

        return self._record(eng, fn, [], list(out.bufs))

    def emit(self, final_wait_ops=None):
        nc = self.nc
        sig_count = {e: 0 for e in ENGS}
        known = {e: {e2: -1 for e2 in ENGS} for e in ENGS}
        known_dma = {e: {} for e in ENGS}
        for op in self.ops:
            e = op.eng
            waits_c = {}
            waits_d = {}
            for d in op.deps:
                if d.is_dma:
                    key = id(d.sem)
                    if known_dma[e].get(key, 0) >= d.semval:
                        continue
                    cur = waits_d.get(key)
                    if cur is None or cur[1] < d.semval:
                        waits_d[key] = (d.sem, d.semval)
                else:
                    if d.eng == e and e == "tensor":
                        continue
                    if known[e][d.eng] >= d.eidx:
                        continue
                    if waits_c.get(d.eng, -1) < d.eidx:
                        waits_c[d.eng] = d.eidx
            op.waits = (waits_c, waits_d)
            for e2, ei in waits_c.items():
                self.eng_ops[e2][ei].signal = True
                known[e][e2] = ei
            for key, (s, v) in waits_d.items():
                known_dma[e][key] = v
        for e in ENGS:
            c = 0
            for op in self.eng_ops[e]:
                if op.is_dma:
                    continue
                if op.signal:
                    c += 1
                    op.semval = c
        import sys
        print("FW stats: ops", {e: len(self.eng_ops[e]) for e in ENGS}, "signals", {e: max([op.semval or 0 for op in self.eng_ops[e] if not op.is_dma] + [0]) for e in ENGS},
              "max dma sem", max([op.semval for op in self.ops if op.is_dma] + [0]), "n dma bufs", len(self.all_dma_bufs), file=sys.stderr)
        eng_sem = {}
        for e in ENGS:
            eng_sem[e] = self.stack.enter_context(nc.semaphore(f"e_{e}"))
        final_ops = list(final_wait_ops or [])
        with nc.Block() as block:
            def make(e):
                def body(eng):
                    for op in self.eng_ops[e]:
                        wc, wd = op.waits
                        for e2, ei in wc.items():
                            d = self.eng_ops[e2][ei]
                            eng.wait_ge(eng_sem[e2], d.semval)
                        for key, (s, v) in wd.items():
                            eng.wait_ge(s, v)
                        ins = op.fn(eng)
                        if op.is_dma:
                            ins.then_inc(op.sem, 16)
                        elif op.signal:
                            ins.then_inc(eng_sem[e], 1)
                    if e == "sync":
                        seen = {}
                        for op in final_ops:
                            k = id(op.sem)
                            if k not in seen or seen[k][1] < op.semval:
                                seen[k] = (op.sem, op.semval)
                        for s, v in seen.values():
                            eng.wait_ge(s, v)
                return body
            block.tensor(make("tensor"))
            block.vector(make("vector"))
            block.scalar(make("scalar"))
            block.gpsimd(make("gpsimd"))
            block.sync(make("sync"))

import os
from contextlib import ExitStack
from concourse.bass_utils import run_bass_kernel_spmd

D = 4096
T = 2048
NT = T // 128
EPS = 1e-6
EVEN_IN = 15920
ODD_IN = 16912
STAGE = int(os.environ.get("MK_STAGE", "99"))
NEG = -30000.0
HG_OLD = os.environ.get("MK_HG_OLD", "1") == "1"


class Pool:
    def __init__(self, P, n, shape, dtype, name, psum=False):
        self.bufs = [(P.psum if psum else P.sbuf)(shape, dtype, f"{name}{i}") for i in range(n)]
        self.i = 0

    def next(self):
        b = self.bufs[self.i % len(self.bufs)]
        self.i += 1
        return b


def host_consts():
    c = {}
    ident = np.eye(128, dtype=np.float32)
    s = np.arange(128)[:, None]
    t = np.arange(128)[None, :]
    tri = ((s // 64 == t // 64) & (s <= t)).astype(np.float32)
    sel2 = np.zeros((128, 2), np.float32)
    sel2[:64, 0] = 1
    sel2[64:, 1] = 1
    n = np.arange(128)
    nf = np.maximum(n, 1).astype(np.float32)
    large = 16 + (np.log(nf / np.float32(16)) / np.float32(np.log(8.0)) * np.float32(16)).astype(np.int32)
    bucket = np.where(n < 16, n, np.minimum(large, 31))
    oh = np.zeros((32, 128), np.float32)
    oh[bucket, n] = 1
    c["c_oh"] = oh
    J = np.zeros((128, 128), np.float32)
    J[np.arange(128), 127 - np.arange(128)] = 1
    c["c_J"] = J
    J127 = np.zeros((128, 128), np.float32)
    J127[np.arange(127), 126 - np.arange(127)] = 1
    c["c_J127"] = J127
    wc = np.zeros((128, 32), np.float32)
    for j in range(32):
        for nn, wgt in ((4 * j - 1, 1), (4 * j, 2), (4 * j + 1, 2), (4 * j + 2, 2), (4 * j + 3, 1)):
            if 0 <= nn < 127:
                wc[nn, j] += wgt
    c["c_wc"] = wc
    forced = np.zeros((128, 16, 32), np.float32)
    for qt in range(16):
        cur = (qt * 128 + np.arange(128)) // 64
        jj = np.arange(32)[None, :]
        f = np.zeros((128, 32), np.float32)
        f[(jj == 0) | (jj == cur[:, None]) | (jj == cur[:, None] - 1)] = 1e4
        f[jj > cur[:, None]] = -1e4
        forced[:, qt, :] = f
    c["c_forced"] = forced
    E = np.zeros((32, 16, 128), np.float32)
    for kt in range(16):
        kk = kt * 128 + np.arange(128)
        E[kk // 64, kt, np.arange(128)] = 1
    c["c_E"] = E
    s64 = np.arange(64)
    c["c_tri64"] = (s64[:, None] <= s64[None, :]).astype(np.float32)
    sl = np.zeros((64, 128), np.float32); sl[63, :] = 1
    c["c_sellast"] = sl
    cm = np.where(s64[None, :] <= s64[:, None], 0.0, -1e9).astype(np.float32)
    c["c_cmask"] = np.tile(cm[:, None, :], (1, 8, 1)).reshape(64, 512)
    cmT = np.where(s64[:, None] <= s64[None, :], 0.0, -1e9).astype(np.float32)
    c["c_cmaskT"] = np.tile(cmT[:, None, :], (1, 8, 1)).reshape(64, 512)
    s4 = np.arange(4)
    sl4 = np.zeros((4, 128), np.float32); sl4[3, :] = 1
    c["c_sellast4"] = sl4
    cm4 = np.where(s4[None, :] <= s4[:, None], 0.0, -1e9).astype(np.float32)
    c["c_cmask4"] = np.tile(cm4[:, None, :], (1, 8, 1)).reshape(4, 32)
    cmT4 = np.where(s4[:, None] <= s4[None, :], 0.0, -1e9).astype(np.float32)
    c["c_cmaskT4"] = np.tile(cmT4[:, None, :], (1, 8, 1)).reshape(4, 32)
    c["c_iota"] = np.arange(128, dtype=np.float32).reshape(128, 1)
    wcs = np.zeros((1024, 257), np.float32)
    for j in range(257):
        for nn, wgt in ((4 * j - 1, 1), (4 * j, 2), (4 * j + 1, 2), (4 * j + 2, 2), (4 * j + 3, 1)):
            if 0 <= nn < 1023:
                wcs[nn, j] += wgt
    c["c_wcs"] = np.ascontiguousarray(wcs.reshape(8, 128, 257).transpose(1, 0, 2))
    fs = np.zeros((4, 257), np.float32); fs[:, [0, 255, 256]] = 1e4
    c["c_forced_s"] = fs
    sg = np.zeros((16, 4), np.float32)
    for g in range(4):
        for t in range(4):
            sg[g * 4 + t, t] = 1
    c["c_sumg"] = sg
    hm = np.zeros((2, 128), np.float32); hm[0, :64] = 1; hm[1, 64:] = 1
    c["c_hm"] = hm
    c["c_ident"] = ident
    c["c_tri"] = tri
    c["c_sel2"] = sel2
    return c


def build_program():
    nc = bass.Bass("TRN2", target_bir_lowering=False)
    st = ExitStack()
    with st:
        P = Prog(nc, st)
        outs = []

        def din(name, shape, dt=F32):
            return P.dram(name, shape, dt, kind="ExternalInput")

        def dout(name, shape, dt=F32):
            return P.dram(name, shape, dt, kind="ExternalOutput")

        x_p = din("x_p", [T, D])
        mem_p = din("mem_p", [256, D])
        norm_w = din("norm_w", [2, D])
        mem_norm_w = din("mem_norm_w", [2, D])
        final_norm_w = din("final_norm_w", [1, D])
        w_mem_kv = din("w_mem_kv", [2, D, 1024])
        w_in_even = din("w_in_even", [D, EVEN_IN])
        rel_bias = din("rel_bias", [32, 16])
        b_gate = din("b_gate", [1, 48])
        w_cmp1 = din("w_cmp1", [2, 32, 128, 128])
        b_cmp1 = din("b_cmp1", [2, 128])
        w_cmp2 = din("w_cmp2", [2, 128, 128])
        pe_cmp = din("pe_cmp", [64, 128])
        w_out_even = din("w_out_even", [4608, D])
        page_tab = din("page_tab", [1, 128], I32)
        pool_ck = din("pool_ck", [1280 * 128, 512])
        pool_cv = din("pool_cv", [1280 * 128, 512])
        pool_sk = din("pool_sk", [1280 * 128, 512])
        pool_sv = din("pool_sv", [1280 * 128, 512])
        memk_s = din("memk_s", [2, 256, 512])
        memv_s = din("memv_s", [2, 256, 512])
        c_iota = din("c_iota", [128, 1])
        c_wcs = din("c_wcs", [128, 8, 257])
        c_forced_s = din("c_forced_s", [4, 257])
        c_sumg = din("c_sumg", [16, 4])
        c_hm = din("c_hm", [2, 128])
        st_c = din("st_c", [8, 512, 256])
        st_n = din("st_n", [8, 256])
        st_m = din("st_m", [1, 8])
        c_sellast4 = din("c_sellast4", [4, 128])
        c_cmask4 = din("c_cmask4", [4, 32])
        c_cmaskT4 = din("c_cmaskT4", [4, 32])
        x_s = din("x_s", [4, D])
        st_hgrn = din("st_hgrn", [16, 128, 128])
        win_k_in = din("win_k_in", [512, 512])
        win_v_in = din("win_v_in", [512, 512])
        w_in_odd = din("w_in_odd", [D, ODD_IN])
        w_out_odd = din("w_out_odd", [4608, D])
        b_if = din("b_if", [2, 8])
        ml_nw = din("ml_nw", [1, 4096])
        c_tri64 = din("c_tri64", [64, 64])
        c_sellast = din("c_sellast", [64, 128])
        c_cmask = din("c_cmask", [64, 512])
        c_cmaskT = din("c_cmaskT", [64, 512])
        c_oh = din("c_oh", [32, 128])
        c_J = din("c_J", [128, 128])
        c_J127 = din("c_J127", [128, 128])
        c_wc = din("c_wc", [128, 32])
        c_forced = din("c_forced", [128, 16, 32])
        c_E = din("c_E", [32, 16, 128])
        hgrn_lb = din("hgrn_lb", [3, 2048])
        hgrn_nw = din("hgrn_nw", [1, 128])
        c_ident = din("c_ident", [128, 128])
        c_tri = din("c_tri", [128, 128])
        c_sel2 = din("c_sel2", [128, 2])

        o_memk = dout("o_memk", [2, 256, 512])
        o_memv = dout("o_memv", [2, 256, 512])
        o_kv = []
        for j in range(6):
            hh = nc.dram_tensor(f"o_kv{j}", [T, 512], F32, kind="ExternalOutput").ap()
            o_kv.append([Buf(P, hh[i * 128:(i + 1) * 128, :], f"o_kv{j}_{i}") for i in range(NT)])

        o_hgrn = dout("o_hgrn", [16, 128, 128])
        o_y = dout("o_y", [T, D])
        o_ys = dout("o_ys", [4, D])
        o_mc_s = dout("o_mc_s", [8, 512, 256])
        o_mn_s = dout("o_mn_s", [8, 256])
        o_mm_s = dout("o_mm_s", [1, 8])
        o_kvs = [dout(f"o_kvs{j}", [4, 512]) for j in range(4)]
        o_wins = [dout(f"o_wins{j}", [512, 512]) for j in range(2)]
        o_hgrn_s = dout("o_hgrn_s", [16, 128, 128])
        s_nm = P.dram("s_nm", [4, 4, 258], F32)
        s_gs = P.dram("s_gs", [4, 48], F32)
        s_ps0 = P.dram("s_ps0", [4, EVEN_IN], F32)
        s_ps1 = P.dram("s_ps1", [4, ODD_IN], F32)
        s_cat_s = P.dram("s_cat_s", [4, 4608], BF16, kind=("ExternalOutput" if os.environ.get("MK_DEBUG", "0") == "1" else "Internal"))
        s_cat1_s = P.dram("s_cat1_s", [4, 4608], BF16)
        s_h1s = P.dram("s_h1s", [4, D], F32, kind=("ExternalOutput" if os.environ.get("MK_DEBUG", "0") == "1" else "Internal"))
        s_h2s = P.dram("s_h2s", [4, D], F32)
        o_mc = dout("o_mc", [8, 512, 256])
        o_mn = dout("o_mn", [8, 256])
        o_mm = dout("o_mm", [1, 8])
        DEBUG = os.environ.get("MK_DEBUG", "0") == "1"
        def scratch_tiles(name, cols, dt=F32, dbg=False):
            h = nc.dram_tensor(name, [T, cols], dt, kind=("ExternalOutput" if (dbg and DEBUG) else "Internal")).ap()
            return [Buf(P, h[i * 128:(i + 1) * 128, :], f"{name}_{i}") for i in range(NT)]

        s_hg = scratch_tiles("s_hg", 8192)
        s_zb = scratch_tiles("s_zb", 2048)
        s_h1 = scratch_tiles("s_h1", D, dbg=True)
        s_bias_h = nc.dram_tensor("s_bias", [16, 4400], BF16, kind="Internal")
        s_bias = Buf(P, s_bias_h.ap(), "s_bias")
        s_cat = scratch_tiles("s_cat", 4608, BF16, dbg=True)
        s_cat1 = scratch_tiles("s_cat1", 4608, BF16, dbg=True)
        s_h2 = scratch_tiles("s_h2", D)
        s_k1 = scratch_tiles("s_k1", 2048, BF16)
        s_v1 = scratch_tiles("s_v1", 4096, BF16)
        s_og = scratch_tiles("s_og", 4096)
        s_z1 = scratch_tiles("s_z1", 4096)
        s_if_h = nc.dram_tensor("s_if", [T, 16], F32, kind="Internal").ap()
        s_if = Buf(P, s_if_h, "s_if")
        q1T_h = nc.dram_tensor("s_q1T", [2048, T], BF16, kind="Internal").ap()
        s_q1T = [[Buf(P, q1T_h[cb * 128:(cb + 1) * 128, pp * 1024:(pp + 1) * 1024], f"s_q1T_{cb}_{pp}") for pp in range(2)] for cb in range(16)]
        qm1T_h = nc.dram_tensor("s_qm1T", [512, T], BF16, kind="Internal").ap()
        s_qm1T = [[Buf(P, qm1T_h[cb * 128:(cb + 1) * 128, pp * 1024:(pp + 1) * 1024], f"s_qm1T_{cb}_{pp}") for pp in range(2)] for cb in range(4)]
        dbg_o = [scratch_tiles(f"dbg_o{br}", 2048, F32, dbg=True) for br in range(3)] if DEBUG else None
        s_gb = scratch_tiles("s_gb", 48)
        qT_h = nc.dram_tensor("s_qT", [2048, T], BF16, kind="Internal").ap()
        s_qT = [[Buf(P, qT_h[cb * 128:(cb + 1) * 128, pp * 1024:(pp + 1) * 1024], f"s_qT_{cb}_{pp}") for pp in range(2)] for cb in range(16)]
        qmT_h = nc.dram_tensor("s_qmT", [512, T], BF16, kind="Internal").ap()
        s_qmT = [[Buf(P, qmT_h[cb * 128:(cb + 1) * 128, pp * 1024:(pp + 1) * 1024], f"s_qmT_{cb}_{pp}") for pp in range(2)] for cb in range(4)]

        ident_f = P.sbuf([128, 128], F32, "ident_f")
        ident_b = P.sbuf([128, 128], BF16, "ident_b")
        P.dma(ident_f[:, :], c_ident[:, :])
        P.copy(ident_b[:, :], ident_f[:, :])
        SC = 128 ** -0.5

        def proj_env():
            class E_: pass
            E = E_()
            E.after_load = None
            P.begin_phase()
            wbc = P.sbuf([128, D], F32, "wbc")
            hnT = P.sbuf([128, 32, 1024], BF16, "hnT")
            xpool = Pool(P, 2, [128, D], F32, "xt")
            ybf = Pool(P, 1, [128, D], BF16, "ybf")
            junk = P.sbuf([128, D], BF16, "junk")
            small = Pool(P, 8, [128, 8], F32, "small")
            wpool = Pool(P, 2, [128, 32, 512], BF16, "wch")
            stage = Pool(P, 3, [128, 512], F32, "stg")
            stage_b = Pool(P, 4, [128, 512], BF16, "stgb")
            psA = Pool(P, 4, [128, 512], F32, "psA", psum=True)
            psT = Pool(P, 2, [128, 4, 128], BF16, "psT", psum=True)

            def rmsnorm_to_T(x_rows, dstT, col0, nrows=128):
                xt = xpool.next()
                P.dma(xt[:nrows, :], x_rows)
                ss = small.next()
                P.memset(ss[:, 0:1], 0.0)
                P.act(junk[:nrows, :], xt[:nrows, :], AF.Square, accum_out=ss[:nrows, 0:1])
                P.ts(ss[:nrows, 1:2], ss[:nrows, 0:1], 1.0 / D, ALU.mult, EPS, ALU.add)
                P.act(ss[:nrows, 3:4], ss[:nrows, 1:2], AF.Sqrt)
                P.op("vector", "reciprocal", out=ss[:nrows, 2:3], in_=ss[:nrows, 3:4])
                yb = ybf.next()
                P.stt(yb[:nrows, :], xt[:nrows, :], ss[:nrows, 2:3], wbc[:nrows, :], ALU.mult, ALU.mult)
                for g in range(8):
                    pt = psT.next()
                    for j in range(4):
                        k = g * 4 + j
                        P.transpose(pt[:, j, :nrows], yb[:nrows, k * 128:(k + 1) * 128], ident_b[:nrows, :nrows])
                    P.copy(dstT[:, g * 4:(g + 1) * 4, col0:col0 + nrows], pt[:, :, :nrows], eng="scalar" if g % 2 else "vector")

            def load_w(wdram, c0, ncols):
                wt = wpool.next()
                wv = wdram if isinstance(wdram, V) else wdram[:, :]
                src = wv.rearrange("(k p) c -> p k c", p=128)[:, :, c0:c0 + ncols]
                P.dma(wt[:, :, :ncols], src, eng="gpsimd")
                if E.after_load is not None:
                    E.after_load(wt, c0, ncols)
                return wt

            def proj_tok(wt, ncols, ntok_tiles, sink):
                for tt in range(ntok_tiles):
                    ps = psA.next()
                    for k in range(32):
                        P.mm(ps[:, :ncols], lhsT=hnT[:, k, tt * 128:(tt + 1) * 128], rhs=wt[:, k, :ncols], start=(k == 0), stop=(k == 31))
                    sink(tt, ps)

            def proj_feat(wt, ncb, nhalves, sink):
                for cb in range(ncb):
                    for hf in range(nhalves):
                        ps = psA.next()
                        for k in range(32):
                            P.mm(ps[:, :], lhsT=wt[:, k, cb * 128:(cb + 1) * 128], rhs=hnT[:, k, hf * 512:(hf + 1) * 512], start=(k == 0), stop=(k == 31))
                        sink(cb, hf, ps)


            hnTs = P.sbuf([128, 32, 4], BF16, "hnTs")
            E.hnTs = hnTs
            def sample_proj(dst):
                def f(wt, c0, ncols):
                    ps = psA.next()
                    for k in range(32):
                        P.mm(ps[0:4, :ncols], lhsT=hnTs[:, k, 0:4], rhs=wt[:, k, :ncols], start=(k == 0), stop=(k == 31))
                    sg = stage.next()
                    P.copy(sg[0:4, :ncols], ps[0:4, :ncols])
                    P.dma(dst[0:4, c0:c0 + ncols], sg[0:4, :ncols])
                return f
            E.sample_proj = sample_proj
            E.wbc, E.hnT, E.small, E.stage, E.stage_b, E.psA = wbc, hnT, small, stage, stage_b, psA
            E.rmsnorm_to_T, E.load_w, E.proj_tok, E.proj_feat = rmsnorm_to_T, load_w, proj_tok, proj_feat
            return E

        E = proj_env()
        wbc, hnT, stage, stage_b, psA = E.wbc, E.hnT, E.stage, E.stage_b, E.psA
        rmsnorm_to_T, load_w, proj_tok, proj_feat = E.rmsnorm_to_T, E.load_w, E.proj_tok, E.proj_feat

        memT = hnT
        for l in range(2):
            P.dma(wbc[:, :], mem_norm_w[l:l + 1, :].partition_broadcast(128))
            for mt in range(2):
                rmsnorm_to_T(mem_p[mt * 128:(mt + 1) * 128, :], memT, mt * 128)
            for ch in range(2):
                wt = load_w(w_mem_kv[l], ch * 512, 512)
                for mt in range(2):
                    ps = psA.next()
                    for k in range(32):
                        P.mm(ps[:, :], lhsT=memT[:, k, mt * 128:(mt + 1) * 128], rhs=wt[:, k, :], start=(k == 0), stop=(k == 31))
                    sg = stage.next()
                    P.copy(sg[:, :], ps[:, :])
                    dst = (o_memk if ch == 0 else o_memv)
                    outs.append(P.dma(dst[l, mt * 128:(mt + 1) * 128, :], sg[:, :]))

        P.dma(wbc[:, :], norm_w[0:1, :].partition_broadcast(128))
        rmsnorm_to_T(x_s[0:4, :], E.hnTs, 0, nrows=4)
        for pp in range(2):
            E.after_load = E.sample_proj(s_ps0) if pp == 0 else None
            for tt in range(8):
                t0 = pp * 1024 + tt * 128
                rmsnorm_to_T(x_p[t0:t0 + 128, :], hnT, tt * 128)
            def sink_hg(cbase):
                def f(tt, ps):
                    sg = stage.next()
                    P.copy(sg[:, :], ps[:, :], eng="scalar" if tt % 2 else "vector")
                    P.dma(s_hg[pp * 8 + tt][:, cbase:cbase + 512], sg[:, :])
                return f
            for ci in range(16):
                wt = load_w(w_in_even, ci * 512, 512)
                proj_tok(wt, 512, 8, sink_hg(ci * 512))
            for ci in range(4):
                wt = load_w(w_in_even, 8192 + ci * 512, 512)
                def sink_q(cb, hf, ps, ci=ci):
                    sg = stage_b.next()
                    P.act(sg[:, :], ps[:, :], AF.Copy, scale=SC)
                    P.dma(s_qT[ci * 4 + cb][pp][:, hf * 512:(hf + 1) * 512], sg[:, :])
                proj_feat(wt, 4, 2, sink_q)
            for j in range(6):
                wt = load_w(w_in_even, 10240 + j * 512, 512)
                def sink_kv(tt, ps, j=j):
                    sg = stage.next()
                    P.copy(sg[:, :], ps[:, :], eng="scalar" if tt % 2 else "vector")
                    t0 = pp * 1024 + tt * 128
                    outs.append(P.dma(o_kv[j][pp * 8 + tt][:, :], sg[:, :]))
                proj_tok(wt, 512, 8, sink_kv)
            wt = load_w(w_in_even, 13312, 48)
            def sink_gb(tt, ps):
                sg = stage.next()
                P.copy(sg[:, :48], ps[:, :48])
                P.dma(s_gb[pp * 8 + tt][:, :], sg[:, :48])
            proj_tok(wt, 48, 8, sink_gb)
            for ci in range(4):
                wt = load_w(w_in_even, 13360 + ci * 512, 512)
                def sink_zb(tt, ps, ci=ci):
                    sg = stage.next()
                    P.copy(sg[:, :], ps[:, :], eng="scalar" if tt % 2 else "vector")
                    P.dma(s_zb[pp * 8 + tt][:, ci * 512:(ci + 1) * 512], sg[:, :])
                proj_tok(wt, 512, 8, sink_zb)
            wt = load_w(w_in_even, 15408, 512)
            def sink_qm(cb, hf, ps):
                sg = stage_b.next()
                P.act(sg[:, :], ps[:, :], AF.Copy, scale=SC)
                P.dma(s_qmT[cb][pp][:, hf * 512:(hf + 1) * 512], sg[:, :])
            proj_feat(wt, 4, 2, sink_qm)

        E.after_load = None
        for j in range(4):
            outs.append(P.dma(o_kvs[j][0:4, :], s_ps0[0:4, 10240 + j * 512:10240 + (j + 1) * 512]))
        for j, win_in in enumerate((win_k_in, win_v_in)):
            outs.append(P.dma(o_wins[j][0:508, :], win_in[4:512, :]))
            outs.append(P.dma(o_wins[j][508:512, :], s_ps0[0:4, 10240 + (4 + j) * 512:10240 + (5 + j) * 512]))
        P.end_phase()

        class PSB:
            def __init__(self, name):
                self.h = P.stack.enter_context(nc.psum_tensor(name, [128, 8, 512], F32))
                self.b = [Buf(P, None, f"{name}_b{i}") for i in range(8)]
            def f32(self, b0, n=1):
                return V(self.b[b0:b0 + n], self.h[:, b0:b0 + n, :].rearrange("p a b -> p (a b)"))
            def bf16(self, b0):
                return V(self.b[b0:b0 + 1], self.h[:, b0, :].bitcast(BF16))

        if STAGE >= 2:
            P.begin_phase()
            PS = PSB("psb")
            tri = P.sbuf([128, 128], F32, "tri")
            sel2 = P.sbuf([128, 2], F32, "sel2")
            P.dma(tri[:, :], c_tri[:, :])
            P.dma(sel2[:, :], c_sel2[:, :])
            lbbc = P.sbuf([128, 2048], F32, "lbbc")
            omlbc = P.sbuf([128, 2048], F32, "omlbc")
            gnbc = P.sbuf([128, 128], F32, "gnbc")
            P.dma(gnbc[:, :], hgrn_nw[0:1, :].partition_broadcast(128))
            w1 = P.sbuf([128, 2048], F32, "w1")
            w2 = P.sbuf([128, 2048], F32, "w2")
            w3 = P.sbuf([128, 2048], F32, "w3")
            w4 = P.sbuf([128, 2048], F32, "w4")
            for r, dst in enumerate((w1, w2, w3)):
                P.dma(dst[:, :], hgrn_lb[r:r + 1, :].partition_broadcast(128))
                P.act(dst[:, :], dst[:, :], AF.Exp)
            P.tt(w4[:, :], w1[:, :], w2[:, :], ALU.add)
            P.tt(w4[:, :], w4[:, :], w3[:, :], ALU.add)
            P.op("vector", "reciprocal", out=w4[:, :], in_=w4[:, :])
            P.tt(lbbc[:, :], w1[:, :], w4[:, :], ALU.mult)
            P.ts(omlbc[:, :], lbbc[:, :], -1.0, ALU.mult, 1.0, ALU.add)

            hgp = Pool(P, 2, [128, 8192], F32, "hgt")
            qgb = P.sbuf([128, 2048], BF16, "qgb")
            kgb = P.sbuf([128, 2048], BF16, "kgb")
            vb = P.sbuf([128, 2048], BF16, "vb")
            qgT = P.sbuf([128, 16, 128], BF16, "qgT")
            kgT = P.sbuf([128, 16, 128], BF16, "kgT")
            qgT_lo = P.sbuf([128, 16, 128], BF16, "qgT_lo")
            qgT_hi = P.sbuf([128, 16, 128], BF16, "qgT_hi")
            mlo = P.sbuf([128, 128], BF16, "mlo")
            mhi = P.sbuf([128, 128], BF16, "mhi")
            P.memset(mlo[:, 0:64], 1.0)
            P.memset(mlo[:, 64:128], 0.0)
            P.memset(mhi[:, 0:64], 0.0)
            P.memset(mhi[:, 64:128], 1.0)
            attb = P.sbuf([128, 16, 128], BF16, "attb")
            eBl = P.sbuf([128, 16, 2], F32, "eBl")
            S = P.sbuf([128, 16, 128], F32, "S")
            Sb = P.sbuf([128, 16, 128], BF16, "Sb")
            oab = Pool(P, 2, [128, 2048], BF16, "oab")
            sm = Pool(P, 4, [128, 64], F32, "smB")
            P.memset(S[:, :, :], 0.0)
            P.memset(Sb[:, :, :], 0.0)
            def hgrn_tile(nr, chunks, hg_src, triV, selV, S, Sb, cat_dst):
                R = slice(0, nr)
                nch = len(chunks)
                hg = hgp.next()
                P.dma(hg[R, :], hg_src)
                qa, fa, ia, za = (hg[R, j * 2048:(j + 1) * 2048] for j in range(4))
                P.act(w1[R, :], fa, AF.Sigmoid)
                P.tt(w1[R, :], w1[R, :], omlbc[R, :], ALU.mult)
                P.tt(w1[R, :], w1[R, :], lbbc[R, :], ALU.add, eng="gpsimd")
                P.act(w2[R, :], w1[R, :], AF.Ln)
                P.ts(w1[R, :], w1[R, :], -1.0, ALU.mult, 1.0, ALU.add)
                for q4 in range(4):
                    P.mm(PS.f32(q4)[R, :], lhsT=triV, rhs=w2[R, q4 * 512:(q4 + 1) * 512])
                bc = PS.f32(0, 4)[R, :]
                P.act(w3[R, :], bc, AF.Exp)
                P.act(w4[R, :], bc, AF.Exp, scale=-1.0)
                P.tt(kgb[R, :], w1[R, :], w4[R, :], ALU.mult)
                P.act(w1[R, :], qa, AF.Silu)
                P.tt(qgb[R, :], w1[R, :], w3[R, :], ALU.mult)
                P.copy(vb[R, :], ia, eng="gpsimd")
                blp = PS.f32(6)
                for h in range(16):
                    P.mm(blp[:, h * nch:(h + 1) * nch], lhsT=w2[R, h * 128:(h + 1) * 128], rhs=selV)
                P.act(eBl[:, :, 0:nch], blp[:, 0:16 * nch].rearrange("p (a b) -> p a b", a=16), AF.Exp)
                for src_t, dst_t in ((qgb, qgT), (kgb, kgT)):
                    for g in range(4):
                        pt = PS.bf16(4 + g % 2)
                        for j in range(4):
                            h = g * 4 + j
                            P.transpose(pt[:, j * 128:j * 128 + nr], src_t[R, h * 128:(h + 1) * 128], ident_b[R, R])
                        P.copy(dst_t[:, g * 4:(g + 1) * 4, 0:nr], pt[:, 0:512].rearrange("p (a b) -> p a b", a=4)[:, :, 0:nr], eng="scalar" if g % 2 else "vector")
                for g in range(4):
                    ap_ = PS.f32(g)
                    for j in range(4):
                        h = g * 4 + j
                        P.mm(ap_[R, j * 128:j * 128 + nr], lhsT=kgT[:, h, 0:nr], rhs=qgT[:, h, 0:nr])
                    P.tt(attb[R, g * 4:(g + 1) * 4, 0:nr], ap_[R, :].rearrange("p (a b) -> p a b", a=4)[:, :, 0:nr], triV.unsqueeze(1).to_broadcast([nr, 4, nr]), ALU.mult)
                for h in range(16):
                    P.mm(PS.f32(4 + h // 4)[R, (h % 4) * 128:(h % 4 + 1) * 128], lhsT=attb[R, h, 0:nr], rhs=vb[R, h * 128:(h + 1) * 128], start=(h % 4 == 0), stop=False, skip_group_check=True)
                for j, (r0, rl) in enumerate(chunks):
                    for h in range(16):
                        P.mm(PS.f32(4 + h // 4)[r0:r0 + rl, (h % 4) * 128:(h % 4 + 1) * 128], lhsT=qgT[:, h, r0:r0 + rl], rhs=Sb[:, h, :], start=False, stop=(j == nch - 1), skip_group_check=True)
                    for h in range(16):
                        P.mm(PS.f32(h // 4)[:, (h % 4) * 128:(h % 4 + 1) * 128], lhsT=kgb[r0:r0 + rl, h * 128:(h + 1) * 128], rhs=vb[r0:r0 + rl, h * 128:(h + 1) * 128])
                    S2 = S[:, :, :].rearrange("p a b -> p (a b)")
                    P.tt(w3[:, :], PS.f32(0, 4), S2, ALU.add)
                    P.tt(S[:, :, :], w3[:, :].rearrange("p (a b) -> p a b", a=16), eBl[:, :, j:j + 1].to_broadcast([128, 16, 128]), ALU.mult)
                    P.copy(Sb[:, :, :], S[:, :, :], eng="scalar")
                ops_ = PS.f32(4, 4)[R, :]
                s8 = sm.next()
                P.act(w4[R, :], ops_, AF.Square)
                P.op("vector", "tensor_reduce", out=s8[R, 0:16], in_=w4[R, :].rearrange("p (a b) -> p a b", a=16), axis=AX.X, op=ALU.add)
                P.ts(s8[R, 16:32], s8[R, 0:16], 1.0 / 128, ALU.mult, EPS, ALU.add)
                P.act(s8[R, 32:48], s8[R, 16:32], AF.Sqrt)
                P.op("vector", "reciprocal", out=s8[R, 48:64], in_=s8[R, 32:48])
                P.tt(w3[R, :].rearrange("p (a b) -> p a b", a=16), ops_.rearrange("p (a b) -> p a b", a=16), s8[R, 48:64].unsqueeze(2).to_broadcast([nr, 16, 128]), ALU.mult)
                P.tt(w3[R, :].rearrange("p (a b) -> p a b", a=16), w3[R, :].rearrange("p (a b) -> p a b", a=16), gnbc[R, :].unsqueeze(1).to_broadcast([nr, 16, 128]), ALU.mult, eng="gpsimd")
                P.act(w4[R, :], za, AF.Silu)
                ob_ = oab.next()
                P.tt(ob_[R, :], w3[R, :], w4[R, :], ALU.mult)
                P.dma(cat_dst, ob_[R, :])

            for i in range(NT):
                hgrn_tile(128, [(0, 64), (64, 64)], s_hg[i][:, :], tri[:, :], sel2[:, :], S, Sb, s_cat[i][:, 0:2048])
            outs.append(P.dma(o_hgrn[:, :, :].rearrange("h k v -> k h v"), S[:, :, :]))
            P.dma(S[:, :, :], st_hgrn[:, :, :].rearrange("h k v -> k h v"))
            P.copy(Sb[:, :, :], S[:, :, :])
            tri4 = tri[0:4, 0:4]
            ones4 = P.sbuf([4, 1], F32, "ones4")
            P.memset(ones4[:, :], 1.0)
            hgrn_tile(4, [(0, 4)], s_ps0[0:4, 0:8192], tri4, ones4[:, :], S, Sb, s_cat_s[0:4, 0:2048])
            outs.append(P.dma(o_hgrn_s[:, :, :].rearrange("h k v -> k h v"), S[:, :, :]))
            P.end_phase()

        if STAGE >= 3:
            kcmpT = P.sbuf([128, 4, 128], BF16, "kcmpT")
            vcmpA = P.sbuf([128, 4, 161], BF16, "vcmpA")
            Jb = P.sbuf([128, 128], BF16, "Jb")
            J127b = P.sbuf([128, 128], BF16, "J127b")
            P.begin_phase()
            PS = PSB("psc1")
            ldf = Pool(P, 2, [128, 512], F32, "ldf")
            ldb = Pool(P, 2, [128, 512], BF16, "ldb")
            tmpf = P.sbuf([128, 128], F32, "tmpf")
            P.dma(tmpf[:, :], c_J[:, :]); P.copy(Jb[:, :], tmpf[:, :])
            tmpf2 = P.sbuf([128, 128], F32, "tmpf2")
            P.dma(tmpf2[:, :], c_J127[:, :]); P.copy(J127b[:, :], tmpf2[:, :])
            wcf = P.sbuf([128, 32], F32, "wcf")
            P.dma(wcf[:, :], c_wc[:, :])
            w1b = P.sbuf([128, 2, 32, 128], BF16, "w1b")
            P.dma(w1b[:, :, :, :], w_cmp1[:, :, :, :].rearrange("a s d e -> d a s e"), eng="gpsimd")
            w2b = P.sbuf([128, 2, 128], BF16, "w2b")
            P.dma(w2b[:, :, :], w_cmp2[:, :, :].rearrange("a e d -> e a d"), eng="gpsimd")
            pef = P.sbuf([64, 128], F32, "pef")
            P.dma(pef[:, :], pe_cmp[:, :])
            peT = P.sbuf([128, 64], BF16, "peT")
            pp_ = PS.f32(7)
            P.transpose(pp_[:, 0:64], pef[:, :], ident_f[0:64, 0:64])
            P.copy(peT[:, :], pp_[:, 0:64])
            b1f = P.sbuf([128, 2], F32, "b1f")
            P.dma(b1f[:, :], b_cmp1[:, :].rearrange("a e -> e a"), allow_slow_non_contiguous=True)
            b1p = P.sbuf([128, 2], F32, "b1p")
            for kv in range(2):
                bp = PS.f32(6)
                for s in range(32):
                    P.mm(bp[:, kv:kv + 1], lhsT=w1b[:, kv, s, :], rhs=peT[:, kv * 32 + s:kv * 32 + s + 1], start=(s == 0), stop=(s == 31))
                P.tt(b1p[:, kv:kv + 1], bp[:, kv:kv + 1], b1f[:, kv:kv + 1], ALU.add)
            KT = [P.sbuf([128, 4, T], BF16, f"KT{j}") for j in range(2)]
            for j in range(2):
                for i in range(NT):
                    lf = ldf.next()
                    P.dma(lf[:, :], o_kv[j][i][:, :])
                    lb_ = ldb.next()
                    P.copy(lb_[:, :], lf[:, :], eng="gpsimd")
                    pt = PS.bf16(i % 2)
                    for kvh in range(4):
                        P.transpose(pt[:, kvh * 128:(kvh + 1) * 128], lb_[:, kvh * 128:(kvh + 1) * 128], ident_b[:, :])
                    P.copy(KT[j][:, :, i * 128:(i + 1) * 128], pt[:, 0:512].rearrange("p (a b) -> p a b", a=4), eng="scalar" if i % 2 else "vector")
            gT = Pool(P, 2, [128, 128], BF16, "gT")
            P.memset(vcmpA[:, :, :], 0.0)
            P.memset(kcmpT[:, :, :], 0.0)
            for kv in range(2):
                for kvh in range(4):
                    hps = PS.f32(2 + (kvh % 2))
                    for s in range(32):
                        P.mm(hps[:, 0:127], lhsT=w1b[:, kv, s, :], rhs=KT[kv][:, kvh, s:s + 2017:16], start=(s == 0), stop=(s == 31))
                    g_ = gT.next()
                    P.act(g_[:, 0:127], hps[:, 0:127], AF.Gelu_apprx_tanh, bias=b1p[:, kv:kv + 1])
                    p2 = PS.f32(4 + (kvh % 2))
                    if kv == 0:
                        P.mm(p2[:, 0:127], lhsT=w2b[:, 0, :], rhs=g_[:, 0:127])
                        P.copy(kcmpT[:, kvh, 0:127], p2[:, 0:127])
                    else:
                        P.mm(p2[0:127, 0:128], lhsT=g_[:, 0:127], rhs=w2b[:, 1, :])
                        P.copy(vcmpA[0:127, kvh, 0:128], p2[0:127, 0:128])
            for kvh in range(4):
                P.memset(vcmpA[:, kvh, 128:129], 1.0)
                P.copy(vcmpA[:, kvh, 129:161], wcf[:, :])
            P.end_phase()

        if STAGE >= 3:
            P.begin_phase()
            PS = PSB("psc2")
            rb = P.sbuf([32, 16], F32, "rb")
            ohf = P.sbuf([32, 128], F32, "ohf")
            P.dma(rb[:, :], rel_bias[:, :])
            P.dma(ohf[:, :], c_oh[:, :])
            tb = PS.f32(7)
            P.mm(tb[0:16, 0:128], lhsT=rb[:, :], rhs=ohf[:, :])
            Grow = P.sbuf([16, 4400], F32, "Grow")
            ch = P.sbuf([16, 1], F32, "ch")
            P.copy(ch[:, :], tb[0:16, 127:128])
            P.memset(Grow[:, :], 0.0)
            P.memset(Grow[:, 0:2047], NEG)
            P.copy(Grow[:, 2047:2175], tb[0:16, 0:128])
            P.ts(Grow[:, 2175:4223], Grow[:, 2175:4223], ch[:, 0:1], ALU.add)
            P.memset(Grow[:, 4223:4400], NEG)
            Gb = P.sbuf([16, 4400], BF16, "Gb")
            P.copy(Gb[:, :], Grow[:, :])
            P.dma(s_bias[:, :], Gb[:, :])

            def bias_src(h, off, pstep, np_, nfree):
                return V(s_bias, bass.AP(s_bias_h.ap().tensor, h * 4400 + off, [[pstep, np_], [1, nfree]]))

            ksT = P.sbuf([128, 4, T], BF16, "ksT")
            kwT = P.sbuf([128, 4, T], BF16, "kwT")
            vsA = P.sbuf([128, NT, 4, 129], BF16, "vsA")
            vwA = P.sbuf([128, NT, 4, 129], BF16, "vwA")
            ldf = Pool(P, 2, [128, 512], F32, "ldf2")
            ldb = Pool(P, 2, [128, 512], BF16, "ldb2")
            P.memset(vsA[:, :, :, 128:129], 1.0)
            P.memset(vwA[:, :, :, 128:129], 1.0)
            for j, dstT in ((2, ksT), (4, kwT)):
                for i in range(NT):
                    lf = ldf.next()
                    P.dma(lf[:, :], o_kv[j][i][:, :])
                    lb_ = ldb.next()
                    P.copy(lb_[:, :], lf[:, :], eng="gpsimd")
                    pt = PS.bf16(i % 2)
                    for kvh in range(4):
                        P.transpose(pt[:, kvh * 128:(kvh + 1) * 128], lb_[:, kvh * 128:(kvh + 1) * 128], ident_b[:, :])
                    P.copy(dstT[:, :, i * 128:(i + 1) * 128], pt[:, 0:512].rearrange("p (a b) -> p a b", a=4), eng="scalar" if i % 2 else "vector")
            for j, dstV in ((3, vsA), (5, vwA)):
                for i in range(NT):
                    lf = ldf.next()
                    P.dma(lf[:, :], o_kv[j][i][:, :])
                    P.copy(dstV[:, i, :, 0:128], lf[:, :].rearrange("p (a b) -> p a b", a=4), eng="scalar" if i % 2 else "vector")
            gates = P.sbuf([128, NT, 48], F32, "gates")
            bgbc = P.sbuf([128, 48], F32, "bgbc")
            P.dma(bgbc[:, :], b_gate[0:1, :].partition_broadcast(128))
            for i in range(NT):
                P.dma(gates[:, i, :], s_gb[i][:, :])
            P.tt(gates[:, :, :], gates[:, :, :], bgbc[:, :].unsqueeze(1).to_broadcast([128, NT, 48]), ALU.add)
            P.act(gates[:, :, :], gates[:, :, :], AF.Sigmoid)
            forced = P.sbuf([128, 16, 32], F32, "forced")
            P.dma(forced[:, :, :], c_forced[:, :, :])
            Ef = P.sbuf([32, 16, 128], F32, "Ef")
            P.dma(Ef[:, :, :], c_E[:, :, :])
            Eb = P.sbuf([32, 16, 128], BF16, "Eb")
            P.copy(Eb[:, :, :], Ef[:, :, :])

            Qp = Pool(P, 2, [128, 4, T], BF16, "Qk")
            CBp = P.sbuf([128, 4, T], BF16, "CBp")
            Bt = P.sbuf([128, 4, 4, 128], BF16, "Bt")
            PTp = Pool(P, 3, [128, 512], BF16, "PT")
            accp = Pool(P, 2, [128, 4, 128], F32, "acc")
            zbp = Pool(P, 2, [128, 512], F32, "zbt")
            obp = Pool(P, 2, [128, 512], BF16, "obt")
            smc = Pool(P, 6, [128, 16], F32, "smc")
            psl = P.sbuf([128, 32], F32, "psl")
            sc1 = P.sbuf([128, 32], F32, "sc1")
            sc2 = P.sbuf([128, 32], F32, "sc2")
            m8 = P.sbuf([128, 8], F32, "m8")
            nmb = P.sbuf([128, 32], BF16, "nmb")
            nmT = P.sbuf([32, 128], BF16, "nmT")
            P.memset(CBp[:, :, :], 0.0)

            def combine(acc, Obanks, width, gate_cols, first, qt, rs_keep=None):
                for hb in range(2):
                    Ov = Obanks[hb][:, 0:2 * width].rearrange("p (g c) -> p g c", g=2)
                    s_ = smc.next()
                    P.ts(s_[:, 0:2], Ov[:, :, 128:129].rearrange("p g c -> p (g c)"), 1e-30, ALU.max)
                    P.op("vector", "reciprocal", out=s_[:, 2:4], in_=s_[:, 0:2])
                    if rs_keep is not None:
                        P.copy(rs_keep[:, hb * 2:hb * 2 + 2], s_[:, 2:4])
                    if DEBUG:
                        td = smallbig.next()
                        P.tt(td[:, :, :], Ov[:, :, 0:128], s_[:, 2:4].unsqueeze(2).to_broadcast([128, 2, 128]), ALU.mult)
                        br = gate_cols // 16
                        kvh_ = (gate_cols % 16) // 4
                        P.dma(dbg_o[br][qt][:, (kvh_ * 4 + hb * 2) * 128:(kvh_ * 4 + hb * 2 + 2) * 128], td[:, :, :].rearrange("p a b -> p (a b)"))
                    P.tt(s_[:, 4:6], s_[:, 2:4], gates[:, qt, gate_cols + hb * 2:gate_cols + hb * 2 + 2], ALU.mult)
                    if first:
                        P.tt(acc[:, hb * 2:hb * 2 + 2, :], Ov[:, :, 0:128], s_[:, 4:6].unsqueeze(2).to_broadcast([128, 2, 128]), ALU.mult)
                    else:
                        tmp = smallbig.next()
                        P.tt(tmp[:, :, :], Ov[:, :, 0:128], s_[:, 4:6].unsqueeze(2).to_broadcast([128, 2, 128]), ALU.mult)
                        P.tt(acc[:, hb * 2:hb * 2 + 2, :], acc[:, hb * 2:hb * 2 + 2, :], tmp[:, :, :], ALU.add, eng="gpsimd")

            smallbig = Pool(P, 2, [128, 2, 128], F32, "sbg")
            rsk = P.sbuf([128, 4], F32, "rsk")

            for kvh in range(4):
                Q = Qp.next()
                for g in range(4):
                    h = kvh * 4 + g
                    for pp in range(2):
                        P.dma(Q[:, g, pp * 1024:(pp + 1) * 1024], s_qT[h][pp][:, :])
                for g in range(4):
                    h = kvh * 4 + g
                    P.dma(CBp[0:127, g, :], bias_src(h, 0, 16, 127, T))
                    P.dma(Bt[:, 0, g, :], bias_src(h, 1920, 1, 128, 128))
                    P.dma(Bt[:, 1, g, :], bias_src(h, 2048, 1, 128, 128))
                    P.dma(Bt[:, 2, g, :], bias_src(h, 2300, 1, 128, 128))
                    P.dma(Bt[:, 3, g, :], bias_src(h, 4096, 1, 128, 128))
                for qt in range(NT):
                    q0 = qt * 128
                    Qt = Q[:, :, q0:q0 + 128]
                    acc = accp.next()
                    sp = PS.f32(qt % 2)
                    P.mm(sp[0:127, :], lhsT=kcmpT[:, kvh, 0:127], rhs=Qt, start=True, stop=False)
                    P.mm(sp[0:127, :], lhsT=J127b[0:127, 0:127], rhs=CBp[0:127, :, q0:q0 + 128], start=False, stop=True)
                    pt_ = PTp.next()
                    P.act(pt_[0:127, :], sp[0:127, :], AF.Exp)
                    Oc = [PS.f32(2), PS.f32(3)]
                    for g in range(4):
                        P.mm(Oc[g // 2][:, (g % 2) * 161:(g % 2 + 1) * 161], lhsT=pt_[0:127, g * 128:(g + 1) * 128], rhs=vcmpA[0:127, kvh, :])
                    combine(acc, Oc, 161, 0 * 16 + kvh * 4, True, qt, rs_keep=rsk)
                    use_sel = qt >= 8
                    if use_sel:
                        for g in range(4):
                            Og = Oc[g // 2][:, (g % 2) * 161 + 129:(g % 2) * 161 + 161]
                            if g == 0:
                                P.ts(psl[:, :], Og, rsk[:, 0:1], ALU.mult)
                            else:
                                P.stt(psl[:, :], Og, rsk[:, g:g + 1], psl[:, :], ALU.mult, ALU.add)
                        P.tt(sc1[:, :], psl[:, :], forced[:, qt, :], ALU.add)
                        P.op("vector", "max", out=m8[:, :], in_=sc1[:, :])
                        P.op("vector", "match_replace", out=sc2[:, :], in_to_replace=m8[:, :], in_values=sc1[:, :], imm_value=-1e30)
                        P.op("vector", "max", out=m8[:, :], in_=sc2[:, :])
                        P.ts(sc2[:, :], sc1[:, :], m8[:, 7:8], ALU.is_ge)
                        P.ts(nmb[:, :], sc2[:, :], -NEG, ALU.mult, NEG, ALU.add)
                        ptn = PS.bf16(2)
                        P.transpose(ptn[0:32, 0:128], nmb[:, :], ident_b[:, :])
                        P.copy(nmT[:, :], ptn[0:32, 0:128])
                    Os = [PS.f32(4), PS.f32(5)]
                    for kt in range(qt + 1):
                        sp = PS.f32(kt % 2)
                        P.mm(sp[:, :], lhsT=ksT[:, kvh, kt * 128:(kt + 1) * 128], rhs=Qt, start=True, stop=False)
                        btype = 0 if kt == qt else (1 if kt == qt - 1 else 2)
                        P.mm(sp[:, :], lhsT=Jb[:, :], rhs=Bt[:, btype, :, :], start=False, stop=not use_sel)
                        if use_sel:
                            P.mm(sp[:, :], lhsT=Eb[:, kt, :], rhs=nmT[:, :].unsqueeze(1).to_broadcast([32, 4, 128]), start=False, stop=True)
                        pt_ = PTp.next()
                        P.act(pt_[:, :], sp[:, :], AF.Exp)
                        for g in range(4):
                            P.mm(Os[g // 2][:, (g % 2) * 129:(g % 2 + 1) * 129], lhsT=pt_[:, g * 128:(g + 1) * 128], rhs=vsA[:, kt, kvh, :], start=(kt == 0 and g % 2 == 0), stop=(kt == qt), skip_group_check=True)
                    combine(acc, Os, 129, 1 * 16 + kvh * 4, False, qt)
                    Ow = [PS.f32(6), PS.f32(7)]
                    k0 = max(0, qt - 4)
                    for kt in range(k0, qt + 1):
                        sp = PS.f32(kt % 2)
                        P.mm(sp[:, :], lhsT=kwT[:, kvh, kt * 128:(kt + 1) * 128], rhs=Qt, start=True, stop=False)
                        dlt = qt - kt
                        btype = 0 if dlt == 0 else (1 if dlt == 1 else (3 if dlt == 4 else 2))
                        P.mm(sp[:, :], lhsT=Jb[:, :], rhs=Bt[:, btype, :, :], start=False, stop=True)
                        pt_ = PTp.next()
                        P.act(pt_[:, :], sp[:, :], AF.Exp)
                        for g in range(4):
                            P.mm(Ow[g // 2][:, (g % 2) * 129:(g % 2 + 1) * 129], lhsT=pt_[:, g * 128:(g + 1) * 128], rhs=vwA[:, kt, kvh, :], start=(kt == k0 and g % 2 == 0), stop=(kt == qt), skip_group_check=True)
                    combine(acc, Ow, 129, 2 * 16 + kvh * 4, False, qt)
                    zt = zbp.next()
                    P.dma(zt[:, :], s_zb[qt][:, kvh * 512:(kvh + 1) * 512])
                    P.act(zt[:, :], zt[:, :], AF.Silu)
                    ot = obp.next()
                    P.tt(ot[:, :], acc[:, :, :].rearrange("p a b -> p (a b)"), zt[:, :], ALU.mult)
                    P.dma(s_cat[qt][:, 2048 + kvh * 512:2048 + (kvh + 1) * 512], ot[:, :])
            P.end_phase()

        def mem_attn_phase(l, qmT_bufs, cat_tiles, col0, smp=None):
            P.begin_phase()
            PS = PSB(f"psm{l}")
            mkT = P.sbuf([128, 4, 256], BF16, "mkT")
            mvA = P.sbuf([128, 2, 4, 129], BF16, "mvA")
            ldf = Pool(P, 2, [128, 512], F32, "ldfm")
            ldb = Pool(P, 2, [128, 512], BF16, "ldbm")
            P.memset(mvA[:, :, :, 128:129], 1.0)
            for mt in range(2):
                lf = ldf.next()
                P.dma(lf[:, :], o_memk[l, mt * 128:(mt + 1) * 128, :])
                lb_ = ldb.next()
                P.copy(lb_[:, :], lf[:, :])
                pt = PS.bf16(mt)
                for h in range(4):
                    P.transpose(pt[:, h * 128:(h + 1) * 128], lb_[:, h * 128:(h + 1) * 128], ident_b[:, :])
                P.copy(mkT[:, :, mt * 128:(mt + 1) * 128], pt[:, 0:512].rearrange("p (a b) -> p a b", a=4))
                lf = ldf.next()
                P.dma(lf[:, :], o_memv[l, mt * 128:(mt + 1) * 128, :])
                P.copy(mvA[:, mt, :, 0:128], lf[:, :].rearrange("p (a b) -> p a b", a=4))
            qm = P.sbuf([128, 4, T], BF16, "qm")
            for h in range(4):
                for pp in range(2):
                    P.dma(qm[:, h, pp * 1024:(pp + 1) * 1024], qmT_bufs[h][pp][:, :])
            PTm = [[P.sbuf([128, 512], BF16, f"PTm{h}{mt}") for mt in range(2)] for h in range(4)]
            omp = Pool(P, 2, [128, 4, 128], BF16, "omt")
            smm = Pool(P, 4, [128, 4], F32, "smm")
            for qg in range(4):
                for h in range(4):
                    for mt in range(2):
                        sp = PS.f32(mt)
                        P.mm(sp[:, :], lhsT=mkT[:, h, mt * 128:(mt + 1) * 128], rhs=qm[:, h, qg * 512:(qg + 1) * 512])
                        P.act(PTm[h][mt][:, :], sp[:, :], AF.Exp)
                for qs in range(4):
                    om_ = omp.next()
                    for h in range(4):
                        O = PS.f32(2 + h)
                        for mt in range(2):
                            P.mm(O[:, 0:129], lhsT=PTm[h][mt][:, qs * 128:(qs + 1) * 128], rhs=mvA[:, mt, h, :], start=(mt == 0), stop=(mt == 1))
                        s_ = smm.next()
                        P.op("vector", "reciprocal", out=s_[:, 0:1], in_=O[:, 128:129])
                        P.ts(om_[:, h, :], O[:, 0:128], s_[:, 0:1], ALU.mult)
                    P.dma(cat_tiles[qg * 4 + qs][:, col0:col0 + 512], om_[:, :, :].rearrange("p a b -> p (a b)"))
            if smp is not None:
                q_src, cat_s_dst = smp
                for mt in range(2):
                    lf = ldf.next()
                    P.dma(lf[:, :], memk_s[l, mt * 128:(mt + 1) * 128, :])
                    lb_ = ldb.next()
                    P.copy(lb_[:, :], lf[:, :])
                    pt = PS.bf16(mt)
                    for h in range(4):
                        P.transpose(pt[:, h * 128:(h + 1) * 128], lb_[:, h * 128:(h + 1) * 128], ident_b[:, :])
                    P.copy(mkT[:, :, mt * 128:(mt + 1) * 128], pt[:, 0:512].rearrange("p (a b) -> p a b", a=4))
                    lf = ldf.next()
                    P.dma(lf[:, :], memv_s[l, mt * 128:(mt + 1) * 128, :])
                    P.copy(mvA[:, mt, :, 0:128], lf[:, :].rearrange("p (a b) -> p a b", a=4))
                qf = ldf.next()
                P.dma(qf[0:4, :], q_src)
                qb_ = ldb.next()
                P.act(qb_[0:4, :], qf[0:4, :], AF.Copy, scale=SC)
                ptq = PS.bf16(0)
                for h in range(4):
                    P.transpose(ptq[:, h * 4:(h + 1) * 4], qb_[0:4, h * 128:(h + 1) * 128], ident_b[0:4, 0:4])
                qmTs = P.sbuf([128, 16], BF16, "qmTs")
                P.copy(qmTs[:, :], ptq[:, 0:16])
                om_ = omp.next()
                for h in range(4):
                    O = PS.f32(2 + h)
                    for mt in range(2):
                        sp = PS.f32(6 + mt)
                        P.mm(sp[:, 0:4], lhsT=mkT[:, h, mt * 128:(mt + 1) * 128], rhs=qmTs[:, h * 4:(h + 1) * 4])
                        P.act(PTm[h][mt][:, 0:4], sp[:, 0:4], AF.Exp)
                    for mt in range(2):
                        P.mm(O[0:4, 0:129], lhsT=PTm[h][mt][:, 0:4], rhs=mvA[:, mt, h, :], start=(mt == 0), stop=(mt == 1))
                    s_ = smm.next()
                    P.op("vector", "reciprocal", out=s_[0:4, 0:1], in_=O[0:4, 128:129])
                    P.ts(om_[0:4, h, :], O[0:4, 0:128], s_[0:4, 0:1], ALU.mult)
                P.dma(cat_s_dst, om_[0:4, :, :].rearrange("p a b -> p (a b)"))
            P.end_phase()

        def out_proj_phase(l, w_out, cat_tiles, res_src, dst_tiles, smp=None):
            P.begin_phase()
            PS = PSB(f"pso{l}")
            catT = P.sbuf([128, 36, 1024], BF16, "catT")
            wpo = Pool(P, 2, [128, 36, 512], BF16, "wpo")
            ldc = Pool(P, 2, [128, 4608], BF16, "ldc")
            xres = Pool(P, 3, [128, 512], F32, "xres")
            if smp is not None:
                cat_s_src, res_s, dst_s = smp
                lc = ldc.next()
                P.dma(lc[0:4, :], cat_s_src[0:4, :])
                catTs = P.sbuf([128, 36, 4], BF16, "catTs")
                for g in range(9):
                    pt = PS.bf16(g % 2)
                    for j in range(4):
                        k = g * 4 + j
                        P.transpose(pt[:, j * 4:(j + 1) * 4], lc[0:4, k * 128:(k + 1) * 128], ident_b[0:4, 0:4])
                    P.copy(catTs[:, g * 4:(g + 1) * 4, :].rearrange("p a b -> p (a b)"), pt[:, 0:16])
            for pp in range(2):
                for tt in range(8):
                    lc = ldc.next()
                    P.dma(lc[:, :], cat_tiles[pp * 8 + tt][:, :])
                    for g in range(9):
                        pt = PS.bf16(g % 2)
                        for j in range(4):
                            k = g * 4 + j
                            P.transpose(pt[:, j * 128:(j + 1) * 128], lc[:, k * 128:(k + 1) * 128], ident_b[:, :])
                        P.copy(catT[:, g * 4:(g + 1) * 4, tt * 128:(tt + 1) * 128], pt[:, 0:512].rearrange("p (a b) -> p a b", a=4), eng="scalar" if g % 2 else "vector")
                for cc in range(8):
                    wt = wpo.next()
                    P.dma(wt[:, :, :], w_out[:, :].rearrange("(k p) c -> p k c", p=128)[:, :, cc * 512:(cc + 1) * 512], eng="gpsimd")
                    if smp is not None and pp == 0:
                        ps = PS.f32(6)
                        for k in range(36):
                            P.mm(ps[0:4, :], lhsT=catTs[:, k, :], rhs=wt[:, k, :], start=(k == 0), stop=(k == 35))
                        xr = xres.next()
                        P.dma(xr[0:4, :], res_s[0:4, cc * 512:(cc + 1) * 512])
                        P.tt(xr[0:4, :], xr[0:4, :], ps[0:4, :], ALU.add)
                        P.dma(dst_s[0:4, cc * 512:(cc + 1) * 512], xr[0:4, :])
                    for tt in range(8):
                        ps = PS.f32(2 + tt % 4)
                        for k in range(36):
                            P.mm(ps[:, :], lhsT=catT[:, k, tt * 128:(tt + 1) * 128], rhs=wt[:, k, :], start=(k == 0), stop=(k == 35))
                        xr = xres.next()
                        P.dma(xr[:, :], res_src(pp * 8 + tt, cc * 512))
                        P.tt(xr[:, :], xr[:, :], ps[:, :], ALU.add)
                        P.dma(dst_tiles[pp * 8 + tt][:, cc * 512:(cc + 1) * 512], xr[:, :])
            P.end_phase()


        if STAGE >= 8:
            sup_ = ExitStack()
            P.stack = sup_
            kcmpTs = P.sbuf([128, 4, 1024], BF16, "kcmpTs")
            vcmpAs = P.sbuf([128, 8, 4, 129], BF16, "vcmpAs")
            offs_i = P.sbuf([128, 128], I32, "offs_i")
            P.stack = P.gstack
            P.begin_phase()
            PS = PSB("pssc1")
            ptb_i = P.sbuf([128, 128], I32, "ptb_i")
            P.dma(ptb_i[:, :], page_tab[0:1, :].partition_broadcast(128))
            ptb_f = P.sbuf([128, 128], F32, "ptb_f")
            P.copy(ptb_f[:, :], ptb_i[:, :])
            iot = P.sbuf([128, 1], F32, "iot")
            P.dma(iot[:, :], c_iota[:, :])
            P.ts(ptb_f[:, :], ptb_f[:, :], 128.0, ALU.mult, iot[:, 0:1], ALU.add)
            P.copy(offs_i[:, :], ptb_f[:, :])
            w1b = P.sbuf([128, 2, 32, 128], BF16, "w1bs")
            P.dma(w1b[:, :, :, :], w_cmp1[:, :, :, :].rearrange("a s d e -> d a s e"), eng="gpsimd")
            w2b = P.sbuf([128, 2, 128], BF16, "w2bs")
            P.dma(w2b[:, :, :], w_cmp2[:, :, :].rearrange("a e d -> e a d"), eng="gpsimd")
            pef = P.sbuf([64, 128], F32, "pefs")
            P.dma(pef[:, :], pe_cmp[:, :])
            peT = P.sbuf([128, 64], BF16, "peTs")
            pp_ = PS.f32(7)
            P.transpose(pp_[:, 0:64], pef[:, :], ident_f[0:64, 0:64])
            P.copy(peT[:, :], pp_[:, 0:64])
            b1f = P.sbuf([128, 2], F32, "b1fs")
            P.dma(b1f[:, :], b_cmp1[:, :].rearrange("a e -> e a"), allow_slow_non_contiguous=True)
            b1p = P.sbuf([128, 2], F32, "b1ps")
            for kv in range(2):
                bp = PS.f32(6)
                for s in range(32):
                    P.mm(bp[:, kv:kv + 1], lhsT=w1b[:, kv, s, :], rhs=peT[:, kv * 32 + s:kv * 32 + s + 1], start=(s == 0), stop=(s == 31))
                P.tt(b1p[:, kv:kv + 1], bp[:, kv:kv + 1], b1f[:, kv:kv + 1], ALU.add)
            RT = P.sbuf([128, 4, 16384], BF16, "RT")
            pgf = Pool(P, 3, [128, 512], F32, "pgf")
            pgb = Pool(P, 2, [128, 512], BF16, "pgb")
            gTs = Pool(P, 2, [128, 512], BF16, "gTs")
            P.memset(vcmpAs[:, :, :, :], 0.0)
            P.memset(kcmpTs[:, :, :], 0.0)
            for kv, pool_ in ((0, pool_ck), (1, pool_cv)):
                for p in range(128):
                    pf = pgf.next()
                    P.idma(pf[:, :], pool_[:, :], offs_i[:, p:p + 1])
                    pb_ = pgb.next()
                    P.copy(pb_[:, :], pf[:, :], eng="gpsimd" if p % 2 else "vector")
                    pt = PS.bf16(p % 2)
                    for kvh in range(4):
                        P.transpose(pt[:, kvh * 128:(kvh + 1) * 128], pb_[:, kvh * 128:(kvh + 1) * 128], ident_b[:, :])
                    P.copy(RT[:, :, p * 128:(p + 1) * 128], pt[:, 0:512].rearrange("p (a b) -> p a b", a=4), eng="scalar")
                for kvh in range(4):
                    for n0, ncnt in ((0, 512), (512, 511)):
                        hps = PS.f32(2 + (n0 // 512))
                        for s in range(32):
                            st_ = 16 * n0 + s
                            P.mm(hps[:, 0:ncnt], lhsT=w1b[:, kv, s, :], rhs=RT[:, kvh, st_:st_ + 16 * (ncnt - 1) + 1:16], start=(s == 0), stop=(s == 31))
                        g_ = gTs.next()
                        P.act(g_[:, 0:ncnt], hps[:, 0:ncnt], AF.Gelu_apprx_tanh, bias=b1p[:, kv:kv + 1])
                        if kv == 0:
                            p2 = PS.f32(4 + (n0 // 512))
                            P.mm(p2[:, 0:ncnt], lhsT=w2b[:, 0, :], rhs=g_[:, 0:ncnt])
                            P.copy(kcmpTs[:, kvh, n0:n0 + ncnt], p2[:, 0:ncnt])
                        else:
                            for blk in range(4):
                                nb = min(128, ncnt - blk * 128)
                                p2 = PS.f32(4 + blk % 2)
                                P.mm(p2[0:nb, 0:128], lhsT=g_[:, blk * 128:blk * 128 + nb], rhs=w2b[:, 1, :])
                                P.copy(vcmpAs[0:nb, n0 // 128 + blk, kvh, 0:128], p2[0:nb, 0:128])
            P.memset(vcmpAs[:, :, :, 128:129], 1.0)
            P.end_phase()

            P.begin_phase()
            PS = PSB("pssc2")
            def sb_src(off, pstep, np_):
                return V(s_bias, bass.AP(s_bias_h.ap().tensor, off, [[pstep, np_], [4400, 16], [1, 4]]))
            cb_all = P.sbuf([128, 16, 4], BF16, "cb_all"); P.dma(cb_all[:, :, :], sb_src(2300, 0, 128))
            tz_all = P.sbuf([128, 16, 4], BF16, "tz_all"); P.dma(tz_all[:, :, :], sb_src(2048, 1, 128))
            cbl = P.sbuf([128, 16, 4], BF16, "cbl"); P.dma(cbl[0:127, :, :], sb_src(2048, 16, 127))
            nb4 = P.sbuf([4, 16, 4], BF16, "nb4"); P.dma(nb4[:, :, :], sb_src(2044, 1, 4))
            w0b = P.sbuf([128, 16, 4], BF16, "w0b"); P.dma(w0b[:, :, :], sb_src(4096, 1, 128))
            q4f = P.sbuf([4, 2048], F32, "q4f")
            P.dma(q4f[:, :], s_ps0[0:4, 8192:10240])
            q4b = P.sbuf([4, 2048], BF16, "q4b")
            P.act(q4b[:, :], q4f[:, :], AF.Copy, scale=SC)
            qTs = P.sbuf([128, 16, 4], BF16, "qTs4")
            ptq = PS.bf16(7)
            for h in range(16):
                P.transpose(ptq[:, h * 4:(h + 1) * 4], q4b[:, h * 128:(h + 1) * 128], ident_b[0:4, 0:4])
            P.copy(qTs[:, :, :].rearrange("p a b -> p (a b)"), ptq[:, 0:64])
            wcsf = P.sbuf([128, 8, 257], F32, "wcsf"); P.dma(wcsf[:, :, :], c_wcs[:, :, :])
            wcsb = P.sbuf([128, 8, 257], BF16, "wcsb"); P.copy(wcsb[:, :, :], wcsf[:, :, :])
            forced_s = P.sbuf([4, 257], F32, "forced_s"); P.dma(forced_s[:, :], c_forced_s[:, :])
            sumg = P.sbuf([16, 4], F32, "sumg"); P.dma(sumg[:, :], c_sumg[:, :])
            hmf = P.sbuf([2, 128], F32, "hmf"); P.dma(hmf[:, :], c_hm[:, :])
            hmb = P.sbuf([2, 128], BF16, "hmb"); P.copy(hmb[:, :], hmf[:, :])
            g4 = P.sbuf([4, 48], F32, "g4")
            P.dma(g4[:, :], s_ps0[0:4, 13312:13360])
            bg4 = P.sbuf([4, 48], F32, "bg4")
            P.dma(bg4[:, :], b_gate[0:1, :].partition_broadcast(4))
            P.tt(g4[:, :], g4[:, :], bg4[:, :], ALU.add)
            P.act(g4[:, :], g4[:, :], AF.Sigmoid)
            P.dma(s_gs[:, :], g4[:, :])
            gsr = P.sbuf([16, 3, 4], F32, "gsr")
            zbr = P.sbuf([16, 4, 128], F32, "zbr")
            for g in range(4):
                P.dma(gsr[g * 4:(g + 1) * 4, :, :], V(s_gs, bass.AP(s_gs.h.tensor, g, [[48, 4], [16, 3], [4, 4]])), allow_slow_non_contiguous=True)
                P.dma(zbr[g * 4:(g + 1) * 4, :, :], V(s_ps0, bass.AP(s_ps0.h.tensor, 13360 + g * 128, [[EVEN_IN, 4], [512, 4], [1, 128]])))
            P.act(zbr[:, :, :], zbr[:, :, :], AF.Silu)
            acc = P.sbuf([16, 4, 128], F32, "acc16")
            sms = Pool(P, 6, [16, 8], F32, "sms")
            PTs = Pool(P, 3, [128, 64], BF16, "PTs")
            pn = P.sbuf([16, 257], F32, "pn")
            sc1 = P.sbuf([4, 258], F32, "sc1s")
            sc2 = P.sbuf([4, 258], F32, "sc2s")
            m8 = P.sbuf([4, 8], F32, "m8s")

            for kvh in range(4):
                O1 = PS.f32(2)[0:16, 0:129]
                O2 = PS.f32(3)[0:16, 0:257]
                qk = qTs[:, kvh * 4:(kvh + 1) * 4, :]
                for nt in range(8):
                    rows = 127 if nt == 7 else 128
                    sp = PS.f32(nt % 2)[0:rows, 0:16]
                    P.mm(sp, lhsT=kcmpTs[:, kvh, nt * 128:nt * 128 + rows], rhs=qk, start=True, stop=False)
                    if nt == 7:
                        P.mm(sp, lhsT=J127b[0:127, 0:127], rhs=cbl[0:127, kvh * 4:(kvh + 1) * 4, :], start=False, stop=True)
                    else:
                        P.mm(sp, lhsT=ident_b[:, :], rhs=cb_all[:, kvh * 4:(kvh + 1) * 4, :], start=False, stop=True)
                    pt_ = PTs.next()
                    P.act(pt_[0:rows, 0:16], sp, AF.Exp)
                    P.mm(O1, lhsT=pt_[0:rows, 0:16], rhs=vcmpAs[0:rows, nt, kvh, :], start=(nt == 0), stop=(nt == 7))
                    P.mm(O2, lhsT=pt_[0:rows, 0:16], rhs=wcsb[0:rows, nt, :], start=(nt == 0), stop=(nt == 7))
                s_ = sms.next()
                P.op("vector", "reciprocal", out=s_[:, 0:1], in_=O1[:, 128:129])
                P.tt(s_[:, 1:2], s_[:, 0:1], gsr[:, 0, kvh:kvh + 1], ALU.mult)
                P.ts(acc[:, kvh, :], O1[:, 0:128], s_[:, 1:2], ALU.mult)
                P.ts(pn[:, :], O2, s_[:, 0:1], ALU.mult)
                psl = PS.f32(6)[0:4, 0:257]
                P.mm(psl, lhsT=sumg[:, :], rhs=pn[:, :])
                P.memset(sc1[:, 256:258], -1e30)
                P.tt(sc1[:, 0:257], psl, forced_s[:, :], ALU.add)
                P.op("vector", "max", out=m8[:, :], in_=sc1[:, :])
                P.op("vector", "match_replace", out=sc2[:, :], in_to_replace=m8[:, :], in_values=sc1[:, :], imm_value=-1e30)
                P.op("vector", "max", out=m8[:, :], in_=sc2[:, :])
                P.ts(sc2[:, :], sc1[:, :], m8[:, 7:8], ALU.is_ge)
                P.ts(sc2[:, :], sc2[:, :], -NEG, ALU.mult, NEG, ALU.add)
                P.dma(s_nm[kvh, :, :], sc2[:, :])
            nmPf = P.sbuf([2, 4, 4, 129], F32, "nmPf")
            for kvh in range(4):
                for t_ in range(4):
                    P.dma(nmPf[:, kvh, t_, :], V(s_nm, bass.AP(s_nm.h.tensor, (kvh * 4 + t_) * 258, [[1, 2], [2, 129]])), allow_slow_non_contiguous=True)
            nmP = P.sbuf([2, 4, 4, 129], BF16, "nmP")
            P.copy(nmP[:, :, :, :], nmPf[:, :, :, :])

            kTp = Pool(P, 2, [128, 4, 128], BF16, "kTp")
            vAp = Pool(P, 2, [128, 4, 129], BF16, "vAp")
            for b_ in vAp.bufs:
                P.memset(b_[:, :, 128:129], 1.0)
            pgf = Pool(P, 4, [128, 512], F32, "pgf2")
            pgb = Pool(P, 2, [128, 512], BF16, "pgb2")

            def attend(tiles, Obanks, br):
                nt_ = len(tiles)
                for ti, tl in enumerate(tiles):
                    rows = tl["rows"]
                    kf = pgf.next(); tl["kload"](kf)
                    vf = pgf.next(); tl["vload"](vf)
                    kb = pgb.next()
                    P.copy(kb[0:rows, :], kf[0:rows, :], eng="gpsimd" if ti % 2 else "vector")
                    ptk = PS.bf16(6 + ti % 2)
                    for kvh in range(4):
                        P.transpose(ptk[:, kvh * 128:kvh * 128 + rows], kb[0:rows, kvh * 128:(kvh + 1) * 128], ident_b[0:rows, 0:rows])
                    kT_ = kTp.next()
                    P.copy(kT_[:, :, 0:rows], ptk[:, 0:512].rearrange("p (a b) -> p a b", a=4)[:, :, 0:rows], eng="scalar")
                    vA_ = vAp.next()
                    P.copy(vA_[0:rows, :, 0:128], vf[0:rows, :].rearrange("p (a b) -> p a b", a=4))
                    sp = PS.f32(ti % 2)
                    blhs, brhs = tl["bias"]
                    for kvh in range(4):
                        spk = sp[0:rows, kvh * 16:(kvh + 1) * 16]
                        P.mm(spk, lhsT=kT_[:, kvh, 0:rows], rhs=qTs[:, kvh * 4:(kvh + 1) * 4, :], start=True, stop=False)
                        has_m = tl.get("mask") is not None
                        P.mm(spk, lhsT=blhs, rhs=brhs[0:rows, kvh * 4:(kvh + 1) * 4, :], start=False, stop=not has_m)
                        if has_m:
                            P.mm(spk, lhsT=hmb[:, :], rhs=nmP[:, kvh, :, tl["mask"]].unsqueeze(1).to_broadcast([2, 4, 4]), start=False, stop=True)
                    pt_ = PTs.next()
                    P.act(pt_[0:rows, :], sp[0:rows, 0:64], AF.Exp)
                    for kvh in range(4):
                        Ob = Obanks[0][0:16, kvh * 129:(kvh + 1) * 129] if kvh < 3 else Obanks[1][0:16, 0:129]
                        P.mm(Ob, lhsT=pt_[0:rows, kvh * 16:(kvh + 1) * 16], rhs=vA_[0:rows, kvh, :], start=(ti == 0 and kvh in (0, 3)), stop=(ti == nt_ - 1), skip_group_check=True)
                for kvh in range(4):
                    Ob = Obanks[0][0:16, kvh * 129:(kvh + 1) * 129] if kvh < 3 else Obanks[1][0:16, 0:129]
                    s_ = sms.next()
                    P.op("vector", "reciprocal", out=s_[:, 0:1], in_=Ob[:, 128:129])
                    P.tt(s_[:, 1:2], s_[:, 0:1], gsr[:, br, kvh:kvh + 1], ALU.mult)
                    P.stt(acc[:, kvh, :], Ob[:, 0:128], s_[:, 1:2], acc[:, kvh, :], ALU.mult, ALU.add)

            def page_loader(pool_, p):
                return lambda dst: P.idma(dst[:, :], pool_[:, :], offs_i[:, p:p + 1])
            def new_loader(j):
                return lambda dst: P.dma(dst[0:4, :], s_ps0[0:4, 10240 + j * 512:10240 + (j + 1) * 512])
            def win_loader(src_, i):
                return lambda dst: P.dma(dst[:, :], src_[i * 128:(i + 1) * 128, :])

            J4 = Jb[0:4, 124:128]
            sel_tiles = []
            for p in range(128):
                sel_tiles.append(dict(kload=page_loader(pool_sk, p), vload=page_loader(pool_sv, p), rows=128,
                                      bias=((Jb[:, :], tz_all) if p == 127 else (ident_b[:, :], cb_all)), mask=p))
            sel_tiles.append(dict(kload=new_loader(2), vload=new_loader(3), rows=4, bias=(J4, nb4), mask=None))
            attend(sel_tiles, [PS.f32(2), PS.f32(3)], 1)
            win_tiles = []
            for i in range(4):
                bias = (Jb[:, :], w0b) if i == 0 else ((Jb[:, :], tz_all) if i == 3 else (ident_b[:, :], cb_all))
                win_tiles.append(dict(kload=win_loader(win_k_in, i), vload=win_loader(win_v_in, i), rows=128, bias=bias, mask=None))
            win_tiles.append(dict(kload=new_loader(4), vload=new_loader(5), rows=4, bias=(J4, nb4), mask=None))
            attend(win_tiles, [PS.f32(4), PS.f32(5)], 2)
            ob16 = P.sbuf([16, 4, 128], BF16, "ob16")
            P.tt(ob16[:, :, :], acc[:, :, :], zbr[:, :, :], ALU.mult)
            for g in range(4):
                P.dma(V(s_cat_s, bass.AP(s_cat_s.h.tensor, 2048 + g * 128, [[4608, 4], [512, 4], [1, 128]])), ob16[g * 4:(g + 1) * 4, :, :])
            P.end_phase()
            sup_.close()

        if STAGE >= 4:
            SMP = STAGE >= 8
            mem_attn_phase(0, s_qmT, s_cat, 4096, smp=(s_ps0[0:4, 15408:15920], s_cat_s[0:4, 4096:4608]) if SMP else None)
            out_proj_phase(0, w_out_even, s_cat, lambda i, c0: x_p[i * 128:(i + 1) * 128, c0:c0 + 512], s_h1,
                           smp=(s_cat_s, x_s, s_h1s) if SMP else None)

        if STAGE >= 5:
            E = proj_env()
            wbc, hnT, stage, stage_b = E.wbc, E.hnT, E.stage, E.stage_b
            P.dma(wbc[:, :], norm_w[1:2, :].partition_broadcast(128))
            if STAGE >= 8:
                E.rmsnorm_to_T(s_h1s[0:4, :], E.hnTs, 0, nrows=4)
            for pp in range(2):
                E.after_load = E.sample_proj(s_ps1) if (pp == 0 and STAGE >= 8) else None
                for tt in range(8):
                    E.rmsnorm_to_T(s_h1[pp * 8 + tt][:, :], hnT, tt * 128)
                def feat_sink(dst_bufs, cbase, scale):
                    def f(cb, hf, ps):
                        sg = stage_b.next()
                        P.act(sg[:, :], ps[:, :], AF.Copy, scale=scale)
                        P.dma(dst_bufs[cbase + cb][pp][:, hf * 512:(hf + 1) * 512], sg[:, :])
                    return f
                def tok_sink(dst_tiles, cbase, bf, scale=1.0):
                    def f(tt, ps):
                        sg = (stage_b if bf else stage).next()
                        if scale != 1.0 or tt % 2:
                            P.act(sg[:, :], ps[:, :], AF.Copy, scale=scale)
                        else:
                            P.copy(sg[:, :], ps[:, :])
                        P.dma(dst_tiles[pp * 8 + tt][:, cbase:cbase + 512], sg[:, :])
                    return f
                for ci in range(4):
                    wt = E.load_w(w_in_odd, ci * 512, 512)
                    E.proj_feat(wt, 4, 2, feat_sink(s_q1T, ci * 4, 1.0))
                for ci in range(4):
                    wt = E.load_w(w_in_odd, 2048 + ci * 512, 512)
                    E.proj_tok(wt, 512, 8, tok_sink(s_k1, ci * 512, True, 1.0 / 16))
                for ci in range(8):
                    wt = E.load_w(w_in_odd, 4096 + ci * 512, 512)
                    E.proj_tok(wt, 512, 8, tok_sink(s_v1, ci * 512, True))
                for ci in range(8):
                    wt = E.load_w(w_in_odd, 8192 + ci * 512, 512)
                    E.proj_tok(wt, 512, 8, tok_sink(s_og, ci * 512, False))
                wt = E.load_w(w_in_odd, 12288, 16)
                def sink_if(tt, ps):
                    sg = stage.next()
                    P.copy(sg[:, :16], ps[:, :16])
                    t0 = pp * 1024 + tt * 128
                    P.dma(s_if[t0:t0 + 128, :], sg[:, :16])
                E.proj_tok(wt, 16, 8, sink_if)
                for ci in range(8):
                    wt = E.load_w(w_in_odd, 12304 + ci * 512, 512)
                    E.proj_tok(wt, 512, 8, tok_sink(s_z1, ci * 512, False))
                wt = E.load_w(w_in_odd, 16400, 512)
                E.proj_feat(wt, 4, 2, feat_sink(s_qm1T, 0, SC))
            P.end_phase()

        if STAGE >= 6:
            P.begin_phase()
            PS = PSB("psf")
            NCH = T // 64
            tri64 = P.sbuf([64, 64], F32, "tri64"); P.dma(tri64[:, :], c_tri64[:, :])
            sellast = P.sbuf([64, 128], F32, "sellast"); P.dma(sellast[:, :], c_sellast[:, :])
            cmask = P.sbuf([64, 512], F32, "cmask"); P.dma(cmask[:, :], c_cmask[:, :])
            cmaskT = P.sbuf([64, 512], F32, "cmaskT"); P.dma(cmaskT[:, :], c_cmaskT[:, :])
            ones_f = P.sbuf([64, 128], F32, "ones_f"); P.memset(ones_f[:, :], 1.0)
            ones_b = P.sbuf([64, 1], BF16, "ones_b"); P.memset(ones_b[:, :], 1.0)
            id64 = ident_f[0:64, 0:64]
            gts = P.sbuf([64, NCH, 16], F32, "gts")
            P.dma(gts[:, :, :], s_if[:, :].rearrange("(c t) g -> t c g", t=64))
            bifb = P.sbuf([64, 16], F32, "bifb")
            P.dma(bifb[:, :], b_if[:, :].rearrange("a h -> (a h)").unsqueeze(0).partition_broadcast(64) if False else V(b_if, bass.AP(b_if.h.tensor, 0, [[0, 64], [1, 16]])))
            P.tt(gts[:, :, :], gts[:, :, :], bifb[:, :].unsqueeze(1).to_broadcast([64, NCH, 16]), ALU.add)
            logf = P.sbuf([64, NCH, 8], F32, "logf")
            P.act(logf[:, :, :], gts[:, :, 8:16], AF.Exp, scale=-1.0)
            P.act(logf[:, :, :], logf[:, :, :], AF.Ln, bias=1.0)
            P.ts(logf[:, :, :], logf[:, :, :], -1.0, ALU.mult)
            bcs = P.sbuf([64, NCH, 8], F32, "bcs")
            pb = PS.f32(0)
            P.mm(pb[0:64, 0:NCH * 8], lhsT=tri64[:, :], rhs=logf[:, :, :].rearrange("p a b -> p (a b)"))
            P.copy(bcs[:, :, :].rearrange("p a b -> p (a b)"), pb[0:64, 0:NCH * 8])
            av = P.sbuf([64, NCH, 8], F32, "av")
            P.tt(av[:, :, :], gts[:, :, 0:8], bcs[:, :, :], ALU.subtract)
            Blb = P.sbuf([128, NCH, 8], F32, "Blb")
            pb2 = PS.f32(1)
            P.mm(pb2[:, 0:NCH * 8], lhsT=sellast[:, :], rhs=bcs[:, :, :].rearrange("p a b -> p (a b)"))
            P.copy(Blb[:, :, :].rearrange("p a b -> p (a b)"), pb2[:, 0:NCH * 8])
            CT = P.sbuf([128, 8, 2, 512], F32, "CT")
            CTb = P.sbuf([128, 8, 2, 512], BF16, "CTb")
            nT = P.sbuf([128, 8, 2], F32, "nT")
            nTb = P.sbuf([128, 8, 2], BF16, "nTb")
            mprev = P.sbuf([128, 8], F32, "mprev")
            for t_ in (CT, CTb):
                P.memset(t_[:, :, :, :], 0.0)
            P.memset(nT[:, :, :], 0.0); P.memset(nTb[:, :, :], 0.0); P.memset(mprev[:, :], 0.0)
            nwb = P.sbuf([64, 4096], F32, "nwb")
            P.dma(nwb[:, :], ml_nw[0:1, :].partition_broadcast(64))
            qTp = Pool(P, 2, [128, 16, 128], BF16, "q1Tt")
            vp_ = Pool(P, 2, [64, 4096], BF16, "vv1")
            kp_ = Pool(P, 2, [64, 2048], BF16, "kk1")
            ogp = Pool(P, 1, [64, 4096], F32, "ogt")
            zp = ogp
            hob = Pool(P, 1, [64, 4096], BF16, "hob")
            kT = P.sbuf([128, 16, 64], BF16, "kT1")
            dg = Pool(P, 2, [64, 8, 64], F32, "dg")
            Dm = P.sbuf([64, 8, 64], F32, "Dm")
            winT = P.sbuf([64, 8, 64], F32, "winT")
            sf = Pool(P, 4, [128, 64], F32, "sf")
            qT = None
            hsP = Pool(P, 2, [64, 8, 512], F32, "hsP")
            qTsP = Pool(P, 2, [128, 16, 64], BF16, "qTsP")
            swTP = Pool(P, 2, [64, 8, 64], BF16, "swTP")
            kwP = Pool(P, 2, [64, 8, 256], BF16, "kwP")
            rmx = P.sbuf([64, NCH, 8], F32, "rmx")
            mtA = P.sbuf([64, NCH, 8], F32, "mtA")
            mpA = P.sbuf([128, NCH + 1, 8], F32, "mpA")
            wxA = P.sbuf([64, NCH, 8], F32, "wxA")
            c2A = P.sbuf([64, NCH, 8], F32, "c2A")
            emtA = P.sbuf([64, NCH, 8], F32, "emtA")
            BmL = P.sbuf([128, NCH, 8], F32, "BmL")
            wendA = P.sbuf([64, NCH, 8], F32, "wendA")
            dCA = P.sbuf([128, NCH, 8], F32, "dCA")

            def mlstm_run(L, nch, bT, aT, BlT, sell_, cm_, cmT_, load_fn, og_src_fn, z_src_fn, cat_dst_fn):
                idL = ident_f[0:L, 0:L]
                RL = slice(0, L)
                for c in range(nch):
                    d1 = dg.next(); d1c = d1[:, :, :].rearrange("p a b -> p (a b)")
                    P.tt(d1c[RL, 0:8 * L].rearrange("p (a b) -> p a b", a=8), idL.unsqueeze(1).to_broadcast([L, 8, L]), aT[:, c, :].unsqueeze(2).to_broadcast([L, 8, L]), ALU.mult)
                    pA = PS.f32(c % 2)
                    P.mm(pA[RL, 0:8 * L], lhsT=ones_f[RL, RL], rhs=d1c[RL, 0:8 * L], start=True, stop=False)
                    P.mm(pA[RL, 0:8 * L], lhsT=idL, rhs=cm_, start=False, stop=True)
                    P.tt(Dm[RL, :, RL], pA[RL, 0:8 * L].rearrange("p (a b) -> p a b", a=8), bT[:, c, :].unsqueeze(2).to_broadcast([L, 8, L]), ALU.add)
                    P.op("vector", "tensor_reduce", out=rmx[RL, c, :], in_=Dm[RL, :, RL], axis=AX.X, op=ALU.max)
                for c in range(nch):
                    s_ = sf.next()
                    P.tt(s_[RL, 0:8], bT[:, c, :], mpA[RL, c, :], ALU.add)
                    P.tt(mtA[RL, c, :], rmx[RL, c, :], s_[RL, 0:8], ALU.max)
                    pM = PS.f32(2 + c % 2)
                    P.mm(pM[:, 0:8], lhsT=sell_, rhs=mtA[RL, c, :])
                    P.copy(mpA[:, c + 1, :], pM[:, 0:8])
                C_ = slice(0, nch)
                P.tt(wxA[RL, C_, :], bT, mpA[RL, 0:nch, :], ALU.add)
                P.tt(wxA[RL, C_, :], wxA[RL, C_, :], mtA[RL, C_, :], ALU.subtract)
                P.act(wxA[RL, C_, :], wxA[RL, C_, :], AF.Exp)
                P.tt(c2A[RL, C_, :], bT, mtA[RL, C_, :], ALU.subtract)
                P.act(emtA[RL, C_, :], mtA[RL, C_, :], AF.Exp, scale=-1.0)
                P.tt(BmL[:, C_, :], BlT, mpA[:, 1:nch + 1, :], ALU.subtract)
                P.tt(wendA[RL, C_, :], aT, BmL[RL, C_, :], ALU.add)
                P.act(wendA[RL, C_, :], wendA[RL, C_, :], AF.Exp)
                P.tt(dCA[:, C_, :], BmL[:, C_, :], mpA[:, 0:nch, :], ALU.add)
                P.act(dCA[:, C_, :], dCA[:, C_, :], AF.Exp)

                def stage1(c):
                    qc, vv, kk = load_fn(c)
                    ptk = PS.bf16(7)
                    for blk in range(16):
                        P.transpose(ptk[:, blk * 64:blk * 64 + L], kk[:, blk * 128:(blk + 1) * 128], ident_b[RL, RL])
                    P.copy(kT[:, :, RL], ptk[:, 0:1024].rearrange("p (a b) -> p a b", a=16)[:, :, RL])
                    d2 = dg.next(); d2c = d2[:, :, :].rearrange("p a b -> p (a b)")
                    P.tt(d2c[RL, 0:8 * L].rearrange("p (a b) -> p a b", a=8), idL.unsqueeze(1).to_broadcast([L, 8, L]), c2A[RL, c, :].unsqueeze(2).to_broadcast([L, 8, L]), ALU.mult)
                    pC = PS.f32(1)
                    P.mm(pC[RL, 0:8 * L], lhsT=ones_f[RL, RL], rhs=d2c[RL, 0:8 * L], start=True, stop=False)
                    P.mm(pC[RL, 0:8 * L], lhsT=idL, rhs=cmT_, start=False, stop=True)
                    P.tt(winT[RL, :, RL], pC[RL, 0:8 * L].rearrange("p (a b) -> p a b", a=8), aT[:, c, :].unsqueeze(2).to_broadcast([L, 8, L]), ALU.add)
                    P.act(winT[RL, :, RL], winT[RL, :, RL], AF.Exp)
                    d3 = dg.next(); d3c = d3[:, :, :].rearrange("p a b -> p (a b)")
                    P.tt(d3c[RL, 0:8 * L].rearrange("p (a b) -> p a b", a=8), idL.unsqueeze(1).to_broadcast([L, 8, L]), wxA[RL, c, :].unsqueeze(2).to_broadcast([L, 8, L]), ALU.mult)
                    pW = PS.f32(2)
                    P.mm(pW[:, 0:8 * L], lhsT=ones_f[RL, :], rhs=d3c[RL, 0:8 * L])
                    qTs_ = qTsP.next()
                    for kc in range(2):
                        P.tt(qTs_[:, kc::2, RL], qc[:, kc::2, :], pW[:, 0:8 * L].rearrange("p (a b) -> p a b", a=8), ALU.mult)
                    pS = PS.f32(3)
                    for h in range(8):
                        for kc in range(2):
                            P.mm(pS[RL, h * L:(h + 1) * L], lhsT=kT[:, h * 2 + kc, RL], rhs=qc[:, h * 2 + kc, :], start=(kc == 0), stop=(kc == 1))
                    swT_ = swTP.next()
                    P.tt(swT_[RL, :, RL], pS[RL, 0:8 * L].rearrange("p (a b) -> p a b", a=8), winT[RL, :, RL], ALU.mult)
                    kw__ = kwP.next()
                    P.tt(kw__[RL, :, :], kk.rearrange("p (a b) -> p a b", a=8), wendA[RL, c, :].unsqueeze(2).to_broadcast([L, 8, 256]), ALU.mult, eng="gpsimd")
                    return (qTs_, swT_, kw__, vv)

                def stage2(c, st):
                    qTs_, swT_, kw__, vv = st
                    s_ = sf.next()
                    pD = PS.f32(0)
                    for h in range(8):
                        P.mm(pD[RL, h:h + 1], lhsT=swT_[RL, h, RL], rhs=ones_b[RL, :], start=True, stop=False)
                        for kc in range(2):
                            P.mm(pD[RL, h:h + 1], lhsT=qTs_[:, h * 2 + kc, RL], rhs=nTb[:, h, kc:kc + 1], start=False, stop=(kc == 1))
                    P.act(s_[RL, 40:48], pD[RL, 0:8], AF.Abs)
                    P.tt(s_[RL, 40:48], s_[RL, 40:48], emtA[RL, c, :], ALU.max)
                    P.op("vector", "reciprocal", out=s_[RL, 56:64], in_=s_[RL, 40:48])
                    hs_ = hsP.next()
                    for h in range(8):
                        pN = PS.f32(4 + h % 2)
                        P.mm(pN[RL, :], lhsT=swT_[RL, h, RL], rhs=vv[:, h * 512:(h + 1) * 512], start=True, stop=False)
                        for kc in range(2):
                            P.mm(pN[RL, :], lhsT=qTs_[:, h * 2 + kc, RL], rhs=CTb[:, h, kc, :], start=False, stop=(kc == 1))
                        P.act(hs_[RL, h, :], pN[RL, :], AF.Copy, scale=s_[RL, 56 + h:57 + h])
                    for h in range(8):
                        for kc in range(2):
                            pU = PS.f32(4 + (h * 2 + kc) % 2)
                            P.mm(pU[:, :], lhsT=kw__[RL, h, kc * 128:(kc + 1) * 128], rhs=vv[:, h * 512:(h + 1) * 512])
                            P.stt(CT[:, h, kc, :], CT[:, h, kc, :], dCA[:, c, h:h + 1], pU[:, :], ALU.mult, ALU.add)
                            P.copy(CTb[:, h, kc, :], CT[:, h, kc, :], eng="scalar")
                    pn = PS.f32(6)
                    for h in range(8):
                        for kc in range(2):
                            P.mm(pn[:, 16 + h * 2 + kc:17 + h * 2 + kc], lhsT=kw__[RL, h, kc * 128:(kc + 1) * 128], rhs=ones_b[RL, :])
                    P.tt(nT[:, :, :], nT[:, :, :], dCA[:, c, :].unsqueeze(2).to_broadcast([128, 8, 2]), ALU.mult)
                    P.tt(nT[:, :, :], nT[:, :, :], pn[:, 16:32].rearrange("p (a b) -> p a b", a=8), ALU.add)
                    P.copy(nTb[:, :, :], nT[:, :, :])
                    s2 = sf.next()
                    hs2 = ogp.next()
                    hsf = hs_[RL, :, :].rearrange("p a b -> p (a b)")
                    P.act(hs2[RL, :], hsf, AF.Square)
                    P.op("vector", "tensor_reduce", out=s2[RL, 32:40], in_=hs2[RL, :].rearrange("p (a b) -> p a b", a=8), axis=AX.X, op=ALU.add)
                    P.ts(s2[RL, 32:40], s2[RL, 32:40], 1.0 / 512, ALU.mult, EPS, ALU.add)
                    P.act(s2[RL, 40:48], s2[RL, 32:40], AF.Sqrt)
                    P.op("vector", "reciprocal", out=s2[RL, 48:56], in_=s2[RL, 40:48])
                    P.tt(hs_[RL, :, :], hs_[RL, :, :], s2[RL, 48:56].unsqueeze(2).to_broadcast([L, 8, 512]), ALU.mult)
                    P.tt(hsf, hsf, nwb[RL, :], ALU.mult, eng="gpsimd")
                    og_ = ogp.next()
                    P.dma(og_[RL, :], og_src_fn(c))
                    P.act(og_[RL, :], og_[RL, :], AF.Sigmoid)
                    P.tt(hsf, hsf, og_[RL, :], ALU.mult)
                    z_ = zp.next()
                    P.dma(z_[RL, :], z_src_fn(c))
                    P.act(z_[RL, :], z_[RL, :], AF.Silu)
                    ho = hob.next()
                    P.tt(ho[RL, :], hsf, z_[RL, :], ALU.mult, eng="gpsimd")
                    P.dma(cat_dst_fn(c), ho[RL, :])

                st = stage1(0)
                for c in range(nch):
                    nxt = stage1(c + 1) if c + 1 < nch else None
                    stage2(c, st)
                    st = nxt
                P.copy(mprev[:, :], mpA[:, nch, :])

            qT_state = {"qT": None}
            def load_prompt(c):
                t0 = c * 64
                if c % 2 == 0:
                    qT_state["qT"] = qTp.next()
                    pp, off = divmod(t0, 1024)
                    for blk in range(16):
                        P.dma(qT_state["qT"][:, blk, :], s_q1T[blk][pp][:, off:off + 128])
                qc = qT_state["qT"][:, :, (c % 2) * 64:(c % 2 + 1) * 64]
                vt = vp_.next()
                kt_ = kp_.next()
                ti, r0 = divmod(t0, 128)
                P.dma(vt[:, :], s_v1[ti][r0:r0 + 64, :])
                P.dma(kt_[:, :], s_k1[ti][r0:r0 + 64, :])
                return qc, vt[:, :], kt_[:, :]
            def tr_(c):
                ti, r0 = divmod(c * 64, 128)
                return ti, r0
            P.memset(mpA[:, 0, :], 0.0)
            mlstm_run(64, NCH, bcs[:, :, :], av[:, :, :], Blb[:, :, :], sellast[:, :], cmask[:, :], cmaskT[:, :], load_prompt,
                      lambda c: s_og[tr_(c)[0]][tr_(c)[1]:tr_(c)[1] + 64, :], lambda c: s_z1[tr_(c)[0]][tr_(c)[1]:tr_(c)[1] + 64, :],
                      lambda c: s_cat1[tr_(c)[0]][tr_(c)[1]:tr_(c)[1] + 64, 0:4096])
            trp = Pool(P, 1, [128, 4, 128], F32, "trp")
            def emit_states(o_mc_, o_mn_, o_mm_):
                for h in range(8):
                    for kc in range(2):
                        pt = PS.f32(kc)
                        for dvb in range(4):
                            P.transpose(pt[:, dvb * 128:(dvb + 1) * 128], CT[:, h, kc, dvb * 128:(dvb + 1) * 128], ident_f[:, :])
                        tr = trp.next()
                        P.copy(tr[:, :, :].rearrange("p a b -> p (a b)"), pt[:, :])
                        outs.append(P.dma(o_mc_[h, :, kc * 128:(kc + 1) * 128].rearrange("(a p) k -> p a k", p=128), tr[:, :, :]))
                outs.append(P.dma(o_mn_[:, :].rearrange("h (c p) -> p h c", p=128), nT[:, :, :], allow_slow_non_contiguous=True))
                outs.append(P.dma(o_mm_[0:1, :], mprev[0:1, :]))
            emit_states(o_mc, o_mn, o_mm)
            if STAGE >= 8:
                sell4 = P.sbuf([4, 128], F32, "sell4"); P.dma(sell4[:, :], c_sellast4[:, :])
                cm4 = P.sbuf([4, 32], F32, "cm4"); P.dma(cm4[:, :], c_cmask4[:, :])
                cmT4 = P.sbuf([4, 32], F32, "cmT4"); P.dma(cmT4[:, :], c_cmaskT4[:, :])
                g4s = P.sbuf([4, 16], F32, "g4s")
                P.dma(g4s[:, :], s_ps1[0:4, 12288:12304])
                P.tt(g4s[:, :], g4s[:, :], bifb[0:4, :], ALU.add)
                lf4 = P.sbuf([4, 8], F32, "lf4")
                P.act(lf4[:, :], g4s[:, 8:16], AF.Exp, scale=-1.0)
                P.act(lf4[:, :], lf4[:, :], AF.Ln, bias=1.0)
                P.ts(lf4[:, :], lf4[:, :], -1.0, ALU.mult)
                b4 = P.sbuf([4, 8], F32, "b4")
                pb = PS.f32(0)
                P.mm(pb[0:4, 0:8], lhsT=tri64[0:4, 0:4], rhs=lf4[:, :])
                P.copy(b4[:, :], pb[0:4, 0:8])
                a4 = P.sbuf([4, 8], F32, "a4")
                P.tt(a4[:, :], g4s[:, 0:8], b4[:, :], ALU.subtract)
                Bl4 = P.sbuf([128, 8], F32, "Bl4")
                pb2 = PS.f32(1)
                P.mm(pb2[:, 0:8], lhsT=sell4[:, :], rhs=b4[:, :])
                P.copy(Bl4[:, :], pb2[:, 0:8])
                cin = Pool(P, 2, [128, 256], F32, "cin")
                for h in range(8):
                    for dvb in range(4):
                        ci_ = cin.next()
                        P.dma(ci_[:, :], st_c[h, dvb * 128:(dvb + 1) * 128, :])
                        pt = PS.f32(2 + dvb % 2)
                        for kc in range(2):
                            P.transpose(pt[:, kc * 128:(kc + 1) * 128], ci_[:, kc * 128:(kc + 1) * 128], ident_f[:, :])
                        P.copy(CT[:, h, :, dvb * 128:(dvb + 1) * 128], pt[:, 0:256].rearrange("p (a b) -> p a b", a=2))
                P.copy(CTb[:, :, :, :], CT[:, :, :, :])
                P.dma(nT[:, :, :], st_n[:, :].rearrange("h (c p) -> p h c", p=128), allow_slow_non_contiguous=True)
                P.copy(nTb[:, :, :], nT[:, :, :])
                P.dma(mprev[:, :], st_m[0:1, :].partition_broadcast(128))
                vt4 = vp_.next()
                kt4 = kp_.next()
                qkf = ogp.next()
                P.dma(qkf[0:4, :], s_ps1[0:4, 0:4096])
                P.act(kt4[0:4, :], qkf[0:4, 2048:4096], AF.Copy, scale=1.0 / 16)
                q4b = hob.next()
                P.copy(q4b[0:4, 0:2048], qkf[0:4, 0:2048])
                ptq = PS.bf16(7)
                for blk in range(16):
                    P.transpose(ptq[:, blk * 4:(blk + 1) * 4], q4b[0:4, blk * 128:(blk + 1) * 128], ident_b[0:4, 0:4])
                q4T = P.sbuf([128, 16, 4], BF16, "q4T")
                P.copy(q4T[:, :, :].rearrange("p a b -> p (a b)"), ptq[:, 0:64])
                vf4 = ogp.next()
                P.dma(vf4[0:4, :], s_ps1[0:4, 4096:8192])
                P.copy(vt4[0:4, :], vf4[0:4, :])
                P.copy(mpA[:, 0, :], mprev[:, :])
                mlstm_run(4, 1, b4[:, :].unsqueeze(1), a4[:, :].unsqueeze(1), Bl4[:, :].unsqueeze(1), sell4[:, :], cm4[:, :], cmT4[:, :],
                          lambda c: (q4T[:, :, :], vt4[0:4, :], kt4[0:4, :]),
                          lambda c: s_ps1[0:4, 8192:12288], lambda c: s_ps1[0:4, 12304:16400], lambda c: s_cat1_s[0:4, 0:4096])
                emit_states(o_mc_s, o_mn_s, o_mm_s)
            P.end_phase()

        if STAGE >= 7:
            SMP = STAGE >= 8
            mem_attn_phase(1, s_qm1T, s_cat1, 4096, smp=(s_ps1[0:4, 16400:16912], s_cat1_s[0:4, 4096:4608]) if SMP else None)
            out_proj_phase(1, w_out_odd, s_cat1, lambda i, c0: s_h1[i][:, c0:c0 + 512], s_h2,
                           smp=(s_cat1_s, s_h1s, s_h2s) if SMP else None)
            P.begin_phase()
            fwb = P.sbuf([128, D], F32, "fwb")
            P.dma(fwb[:, :], final_norm_w[0:1, :].partition_broadcast(128))
            xp2 = Pool(P, 2, [128, D], F32, "xp2")
            yp2 = Pool(P, 2, [128, D], F32, "yp2")
            jk = P.sbuf([128, D], BF16, "jk2")
            sm2 = Pool(P, 4, [128, 8], F32, "sm2")
            for i in range(NT + (1 if SMP else 0)):
                if i == NT:
                    xt = xp2.next()
                    P.dma(xt[0:4, :], s_h2s[0:4, :])
                    ss = sm2.next()
                    P.memset(ss[:, 0:1], 0.0)
                    P.act(jk[0:4, :], xt[0:4, :], AF.Square, accum_out=ss[0:4, 0:1])
                    P.ts(ss[0:4, 1:2], ss[0:4, 0:1], 1.0 / D, ALU.mult, EPS, ALU.add)
                    P.act(ss[0:4, 3:4], ss[0:4, 1:2], AF.Sqrt)
                    P.op("vector", "reciprocal", out=ss[0:4, 2:3], in_=ss[0:4, 3:4])
                    yt = yp2.next()
                    P.stt(yt[0:4, :], xt[0:4, :], ss[0:4, 2:3], fwb[0:4, :], ALU.mult, ALU.mult)
                    outs.append(P.dma(o_ys[0:4, :], yt[0:4, :]))
                    continue
                xt = xp2.next()
                P.dma(xt[:, :], s_h2[i][:, :])
                ss = sm2.next()
                P.memset(ss[:, 0:1], 0.0)
                P.act(jk[:, :], xt[:, :], AF.Square, accum_out=ss[:, 0:1])
                P.ts(ss[:, 1:2], ss[:, 0:1], 1.0 / D, ALU.mult, EPS, ALU.add)
                P.act(ss[:, 3:4], ss[:, 1:2], AF.Sqrt)
                P.op("vector", "reciprocal", out=ss[:, 2:3], in_=ss[:, 3:4])
                yt = yp2.next()
                P.stt(yt[:, :], xt[:, :], ss[:, 2:3], fwb[:, :], ALU.mult, ALU.mult)
                outs.append(P.dma(o_y[i * 128:(i + 1) * 128, :], yt[:, :]))
            P.end_phase()

        P.emit(outs)
    return nc


_NC_CACHE = {}
DBG_SINK = None


NCORES = int(os.environ.get("MK_CORES", "8"))


def kernel(**inp):
    n = NCORES
    if "nc" not in _NC_CACHE:
        _NC_CACHE["nc"] = build_program()
    nc = _NC_CACHE["nc"]
    consts = host_consts()
    in_maps = []
    f32 = lambda a: np.ascontiguousarray(np.asarray(a, dtype=np.float32))
    shared = {
        "norm_w": f32(inp["norm_w"]), "mem_norm_w": f32(inp["mem_norm_w"]),
        "final_norm_w": f32(inp["final_norm_w"]).reshape(1, D),
        "w_mem_kv": f32(inp["w_mem_kv"]), "w_in_even": f32(inp["w_in_even"][0]),
        "hgrn_lb": f32(inp["hgrn_lb_logits"]), "hgrn_nw": f32(inp["hgrn_norm_w"]),
        "rel_bias": f32(inp["rel_bias"]), "b_gate": f32(inp["b_nsa_gate"]), "w_cmp1": f32(inp["w_cmp1"][0]),
        "b_cmp1": f32(inp["b_cmp1"][0]), "w_cmp2": f32(inp["w_cmp2"][0]), "pe_cmp": f32(inp["pe_cmp"][0]).reshape(64, 128),
        "w_out_even": f32(inp["w_out_even"][0]),
        "w_in_odd": f32(inp["w_in_odd"][0]), "w_out_odd": f32(inp["w_out_odd"][0]), "b_if": f32(inp["b_mlstm_if"][0]),
        "ml_nw": f32(inp["mlstm_norm_w"][0]).reshape(1, 4096),
    }
    for nm_, key in (("pool_ck", "cache_cmp_k"), ("pool_cv", "cache_cmp_v"), ("pool_sk", "cache_sel_k"), ("pool_sv", "cache_sel_v")):
        shared[nm_] = f32(inp[key][0]).reshape(1280 * 128, 512)
    shared.update(consts)
    for c in range(n):
        b = c % 4
        m = dict(shared)
        m["x_p"] = f32(inp["x_prompt"][b])
        m["x_s"] = f32(inp["x_sample"][c])
        m["st_c"] = f32(inp["state_mlstm_c"][0, c])
        m["st_n"] = f32(inp["state_mlstm_n"][0, c])
        m["st_m"] = f32(inp["state_mlstm_m"][0, c]).reshape(1, 8)
        m["page_tab"] = np.ascontiguousarray(np.asarray(inp["page_table"][c], dtype=np.int32).reshape(1, 128))
        m["memk_s"] = f32(inp["cache_mem_k"][:, c]).reshape(2, 256, 512)
        m["memv_s"] = f32(inp["cache_mem_v"][:, c]).reshape(2, 256, 512)
        m["st_hgrn"] = f32(inp["state_hgrn"][0, c])
        m["win_k_in"] = f32(inp["cache_win_k"][0, c]).reshape(512, 512)
        m["win_v_in"] = f32(inp["cache_win_v"][0, c]).reshape(512, 512)
        m["mem_p"] = f32(inp["mem_prompt"][b])
        in_maps.append(m)
    res = run_bass_kernel_spmd(nc, in_maps, core_ids=list(range(n)))
    R = list(res.results)
    while len(R) < 8:
        R.append(R[0])
    B = 4
    mem_k = np.stack([R[b]["o_memk"] for b in range(B)], axis=1).reshape(2, B, 256, 4, 128)
    mem_v = np.stack([R[b]["o_memv"] for b in range(B)], axis=1).reshape(2, B, 256, 4, 128)
    kv = [np.stack([R[b][f"o_kv{j}"] for b in range(B)], axis=0).reshape(1, B, T, 4, 128) for j in range(6)]
    if DBG_SINK is not None:
        DBG_SINK(R)
    z = lambda *s: np.zeros(s, np.float32)
    yp = np.stack([R[b]["o_y"] for b in range(B)], axis=0) if STAGE >= 7 else z(B, T, D)
    ys = np.stack([R[c]["o_ys"] for c in range(8)], axis=0) if STAGE >= 8 else z(8, 4, D)
    outs = (yp, ys, mem_k, mem_v, kv[0], kv[1], kv[2], kv[3],
            np.ascontiguousarray(kv[4][:, :, -512:]), np.ascontiguousarray(kv[5][:, :, -512:]),
            (np.stack([R[b]["o_hgrn"] for b in range(B)], axis=0)[None] if STAGE >= 2 else z(1, B, 16, 128, 128)), (np.stack([R[b]["o_mc"] for b in range(B)], axis=0)[None] if STAGE >= 6 else z(1, B, 8, 512, 256)),
            (np.stack([R[b]["o_mn"] for b in range(B)], axis=0)[None] if STAGE >= 6 else z(1, B, 8, 256)),
            (np.stack([R[b]["o_mm"].reshape(8) for b in range(B)], axis=0)[None] if STAGE >= 6 else z(1, B, 8)),
            *[np.stack([R[c][f"o_kvs{j}"] for c in range(8)], axis=0).reshape(1, 8, 4, 4, 128) for j in range(4)],
            *[np.stack([R[c][f"o_wins{j}"] for c in range(8)], axis=0).reshape(1, 8, 512, 4, 128) for j in range(2)],
            (np.stack([R[c]["o_hgrn_s"] for c in range(8)], axis=0)[None] if STAGE >= 2 else z(1, 8, 16, 128, 128)),
            (np.stack([R[c]["o_mc_s"] for c in range(8)], axis=0)[None] if STAGE >= 8 else z(1, 8, 8, 512, 256)),
            (np.stack([R[c]["o_mn_s"] for c in range(8)], axis=0)[None] if STAGE >= 8 else z(1, 8, 8, 256)),
            (np.stack([R[c]["o_mm_s"].reshape(8) for c in range(8)], axis=0)[None] if STAGE >= 8 else z(1, 8, 8)))
    return outs
```

```python
import numpy as np
import concourse.bass as bass
import concourse.mybir as mybir

F32 = mybir.dt.float32
BF16 = mybir.dt.bfloat16
I32 = mybir.dt.int32
U32 = mybir.dt.uint32
AF = mybir.ActivationFunctionType
ALU = mybir.AluOpType
AX = mybir.AxisListType

ENGS = ("tensor", "vector", "scalar", "gpsimd", "sync")


class Buf:
    def __init__(self, prog, handle, name):
        self.prog = prog
        self.h = handle
        self.name = name
        self.last_w = None
        self.readers = []
        self.sem = None
        self.ndma = 0

    def __getitem__(self, idx):
        return V(self, self.h[idx])

    def ap(self):
        return V(self, self.h[:] if not isinstance(self.h, bass.AP) else self.h)


class V:
    def __init__(self, buf, ap):
        self.bufs = tuple(buf) if isinstance(buf, (tuple, list)) else (buf,)
        self.ap = ap

    @property
    def buf(self):
        return self.bufs[0]

    def __getitem__(self, idx):
        return V(self.bufs, self.ap[idx])

    def rearrange(self, s, **kw):
        return V(self.bufs, self.ap.rearrange(s, **kw))

    def to_broadcast(self, shape):
        return V(self.bufs, self.ap.to_broadcast(shape))

    def broadcast_to(self, shape):
        return V(self.bufs, self.ap.broadcast_to(shape))

    def partition_broadcast(self, n):
        return V(self.bufs, self.ap.partition_broadcast(n))

    def bitcast(self, dt):
        return V(self.bufs, self.ap.bitcast(dt))

    def unsqueeze(self, a):
        return V(self.bufs, self.ap.unsqueeze(a))

    def with_bufs(self, bufs):
        return V(bufs, self.ap)

    @property
    def shape(self):
        return self.ap.shape


class Op:
    __slots__ = ("eng", "fn", "deps", "idx", "eidx", "is_dma", "sem", "semval", "signal", "waits")

    def __init__(self, eng, fn, is_dma):
        self.eng = eng
        self.fn = fn
        self.is_dma = is_dma
        self.deps = []
        self.signal = False
        self.sem = None
        self.semval = None
        self.waits = None


WRITE_KEYS = ("out", "out_max", "out_indices", "accum_out")


class Prog:
    def __init__(self, nc, stack):
        self.nc = nc
        self.stack = stack
        self.ops = []
        self.eng_ops = {e: [] for e in ENGS}
        self.nbuf = 0
        self.out_dma_ops = []
        self.gstack = stack
        self.sem_free = []
        self.phase_bufs = []
        self.all_dma_bufs = []

    def begin_phase(self):
        from contextlib import ExitStack as _ES
        self.phase_stack = _ES()
        self.stack = self.phase_stack
        self.phase_bufs = []

    def end_phase(self):
        self.barrier()
        for b in self.phase_bufs:
            if b.sem is not None:
                self.sem_free.append([b.sem, b.ndma])
                b.sem = None
        self.phase_bufs = []
        self.phase_stack.close()
        self.stack = self.gstack

    def barrier(self):
        last = [ops[-1] for ops in self.eng_ops.values() if ops]
        dmas = [b.last_dma for b in self.all_dma_bufs if getattr(b, "last_dma", None) is not None]
        deps = last + dmas
        for e in ENGS:
            op = Op(e, lambda eng: eng.nop(), False)
            op.deps = [d for d in deps]
            op.idx = len(self.ops)
            op.eidx = len(self.eng_ops[e])
            self.ops.append(op)
            self.eng_ops[e].append(op)

    def sbuf(self, shape, dtype, name=None):
        self.nbuf += 1
        name = f"{name or 'sb'}_{self.nbuf}"
        h = self.stack.enter_context(self.nc.sbuf_tensor(name, list(shape), dtype))
        b = Buf(self, h, name)
        self.phase_bufs.append(b)
        return b

    def psum(self, shape, dtype=F32, name=None):
        self.nbuf += 1
        name = f"{name or 'ps'}_{self.nbuf}"
        h = self.stack.enter_context(self.nc.psum_tensor(name, list(shape), dtype))
        return Buf(self, h, name)

    def dram(self, name, shape, dtype, kind="Internal"):
        h = self.nc.dram_tensor(name, list(shape), dtype, kind=kind)
        return Buf(self, h.ap(), name)

    def _record(self, eng, fn, reads, writes, is_dma=False, dma_buf=None):
        op = Op(eng, fn, is_dma)
        deps = []
        for b in reads:
            if b.last_w is not None:
                deps.append(b.last_w)
        for b in writes:
            if b.last_w is not None:
                deps.append(b.last_w)
            deps.extend(b.readers)
        if is_dma:
            if dma_buf.sem is None:
                if self.sem_free:
                    dma_buf.sem, dma_buf.ndma = self.sem_free.pop()
                else:
                    dma_buf.sem = self.gstack.enter_context(self.nc.semaphore(f"d_{dma_buf.name}"))
                    dma_buf.ndma = 0
                dma_buf.last_dma = None
                self.all_dma_bufs.append(dma_buf)
            if dma_buf.last_dma is not None:
                deps.append(dma_buf.last_dma)
            dma_buf.ndma += 1
            dma_buf.last_dma = op
            op.sem = dma_buf.sem
            op.semval = 16 * dma_buf.ndma
        seen = set()
        for d in deps:
            if d is op or id(d) in seen:
                continue
            seen.add(id(d))
            op.deps.append(d)
        for b in reads:
            if b not in writes:
                b.readers.append(op)
        for b in writes:
            b.last_w = op
            b.readers = []
        op.idx = len(self.ops)
        op.eidx = len(self.eng_ops[eng])
        self.ops.append(op)
        self.eng_ops[eng].append(op)
        return op

    def op(self, eng, method, *args, extra_reads=(), extra_writes=(), **kw):
        reads, writes = list(b for b in extra_reads), list(b for b in extra_writes)
        real_kw = {}
        for k, v in kw.items():
            if isinstance(v, V):
                real_kw[k] = v.ap
                for vb in v.bufs:
                    if k in WRITE_KEYS:
                        if vb not in writes:
                            writes.append(vb)
                    else:
                        if vb not in reads:
                            reads.append(vb)
            else:
                real_kw[k] = v
        real_args = []
        for a in args:
            if isinstance(a, V):
                raise ValueError("pass V's as kwargs")
            real_args.append(a)

        def fn(e, method=method, real_args=real_args, real_kw=real_kw):
            return getattr(e, method)(*real_args, **real_kw)

        return self._record(eng, fn, reads, writes)

    def dma(self, out, in_, eng="sync", **kw):
        def is_dram(v):
            return isinstance(v.buf.h, bass.AP)
        dma_buf = out.buf if not is_dram(out) else in_.buf
        oap, iap = out.ap, in_.ap

        def fn(e, oap=oap, iap=iap, kw=kw):
            return e.dma_start(out=oap, in_=iap, **kw)

        op = self._record(eng, fn, list(in_.bufs), list(out.bufs), is_dma=True, dma_buf=dma_buf)
        return op

    def idma(self, out, in_, offs, eng="gpsimd"):
        oap, iap, fap = out.ap, in_.ap, offs.ap

        def fn(e):
            return e.indirect_dma_start(out=oap, out_offset=None, in_=iap,
                                        in_offset=bass.IndirectOffsetOnAxis(ap=fap, axis=0))

        return self._record(eng, fn, list(in_.bufs) + list(offs.bufs), list(out.bufs), is_dma=True, dma_buf=out.buf)

    def mm(self, out, lhsT, rhs, start=True, stop=True, **kw):
        return self.op("tensor", "matmul", out=out, lhsT=lhsT, rhs=rhs, start=start, stop=stop, **kw)

    def transpose(self, out, in_, identity):
        return self.op("tensor", "transpose", out=out, in_=in_, identity=identity)

    def act(self, out, in_, func, eng="scalar", **kw):
        return self.op(eng, "activation", out=out, in_=in_, func=func, **kw)

    def tt(self, out, in0, in1, op, eng="vector"):
        return self.op(eng, "tensor_tensor", out=out, in0=in0, in1=in1, op=op)

    def ts(self, out, in0, scalar1, op0, scalar2=None, op1=None, eng="vector", **kw):
        if op1 is None:
            return self.op(eng, "tensor_scalar", out=out, in0=in0, scalar1=scalar1, scalar2=scalar2, op0=op0, **kw)
        return self.op(eng, "tensor_scalar", out=out, in0=in0, scalar1=scalar1, scalar2=scalar2, op0=op0, op1=op1, **kw)

    def stt(self, out, in0, scalar, in1, op0, op1, eng="vector", **kw):
        return self.op(eng, "scalar_tensor_tensor", out=out, in0=in0, scalar=scalar, in1=in1, op0=op0, op1=op1, **kw)

    def copy(self, out, in_, eng="vector"):
        if eng == "scalar":
            return self.op(eng, "copy", out=out, in_=in_)
        return self.op(eng, "tensor_copy", out=out, in_=in_)

    def memset(self, out, val, eng="vector"):
        return self.op(eng, "memset", ap=None, constant=val, extra_writes=[out.buf]) if False else self._memset(out, val, eng)

    def _memset(self, out, val, eng):
        oap = out.ap

        def fn(e):
            return e.memset(oap, val)

        return self._record(eng, fn, [], list(out.bufs))

    def emit(self, final_wait_ops=None):
        nc = self.nc
        sig_count = {e: 0 for e in ENGS}
        known = {e: {e2: -1 for e2 in ENGS} for e in ENGS}
        known_dma = {e: {} for e in ENGS}
        for op in self.ops:
            e = op.eng
            waits_c = {}
            waits_d = {}
            for d in op.deps:
                if d.is_dma:
                    key = id(d.sem)
                    if known_dma[e].get(key, 0) >= d.semval:
                        continue
                    cur = waits_d.get(key)
                    if cur is None or cur[1] < d.semval:
                        waits_d[key] = (d.sem, d.semval)
                else:
                    if d.eng == e and e == "tensor":
                        continue
                    if known[e][d.eng] >= d.eidx:
                        continue
                    if waits_c.get(d.eng, -1) < d.eidx:
                        waits_c[d.eng] = d.eidx
            op.waits = (waits_c, waits_d)
            for e2, ei in waits_c.items():
                self.eng_ops[e2][ei].signal = True
                known[e][e2] = ei
            for key, (s, v) in waits_d.items():
                known_dma[e][key] = v
        for e in ENGS:
            c = 0
            for op in self.eng_ops[e]:
                if op.is_dma:
                    continue
                if op.signal:
                    c += 1
                    op.semval = c
        import sys
        print("FW stats: ops", {e: len(self.eng_ops[e]) for e in ENGS}, "signals", {e: max([op.semval or 0 for op in self.eng_ops[e] if not op.is_dma] + [0]) for e in ENGS},
              "max dma sem", max([op.semval for op in self.ops if op.is_dma] + [0]), "n dma bufs", len(self.all_dma_bufs), file=sys.stderr)
        eng_sem = {}
        for e in ENGS:
            eng_sem[e] = self.stack.enter_context(nc.semaphore(f"e_{e}"))
        final_ops = list(final_wait_ops or [])
        with nc.Block() as block:
            def make(e):
                def body(eng):
                    for op in self.eng_ops[e]:
                        wc, wd = op.waits
                        for e2, ei in wc.items():
                            d = self.eng_ops[e2][ei]
                            eng.wait_ge(eng_sem[e2], d.semval)
                        for key, (s, v) in wd.items():
                            eng.wait_ge(s, v)
                        ins = op.fn(eng)
                        if op.is_dma:
                            ins.then_inc(op.sem, 16)
                        elif op.signal:
                            ins.then_inc(eng_sem[e], 1)
                    if e == "sync":
                        seen = {}
                        for op in final_ops:
                            k = id(op.sem)
                            if k not in seen or seen[k][1] < op.semval:
                                seen[k] = (op.sem, op.semval)
                        for s, v in seen.values():
                            eng.wait_ge(s, v)
                return body
            block.tensor(make("tensor"))
            block.vector(make("vector"))
            block.scalar(make("scalar"))
            block.gpsimd(make("gpsimd"))
            block.sync(make("sync"))

import os
from contextlib import ExitStack
from concourse.bass_utils import run_bass_kernel_spmd

D = 4096
T = 2048
NT = T // 128
EPS = 1e-6
EVEN_IN = 15920
ODD_IN = 16912
STAGE = int(os.environ.get("MK_STAGE", "99"))
NEG = -30000.0
HG_OLD = os.environ.get("MK_HG_OLD", "1") == "1"


class Pool:
    def __init__(self, P, n, shape, dtype, name, psum=False):
        self.bufs = [(P.psum if psum else P.sbuf)(shape, dtype, f"{name}{i}") for i in range(n)]
        self.i = 0

    def next(self):
        b = self.bufs[self.i % len(self.bufs)]
        self.i += 1
        return b


def host_consts():
    c = {}
    ident = np.eye(128, dtype=np.float32)
    s = np.arange(128)[:, None]
    t = np.arange(128)[None, :]
    tri = ((s // 64 == t // 64) & (s <= t)).astype(np.float32)
    sel2 = np.zeros((128, 2), np.float32)
    sel2[:64, 0] = 1
    sel2[64:, 1] = 1
    n = np.arange(128)
    nf = np.maximum(n, 1).astype(np.float32)
    large = 16 + (np.log(nf / np.float32(16)) / np.float32(np.log(8.0)) * np.float32(16)).astype(np.int32)
    bucket = np.where(n < 16, n, np.minimum(large, 31))
    oh = np.zeros((32, 128), np.float32)
    oh[bucket, n] = 1
    c["c_oh"] = oh
    J = np.zeros((128, 128), np.float32)
    J[np.arange(128), 127 - np.arange(128)] = 1
    c["c_J"] = J
    J127 = np.zeros((128, 128), np.float32)
    J127[np.arange(127), 126 - np.arange(127)] = 1
    c["c_J127"] = J127
    wc = np.zeros((128, 32), np.float32)
    for j in range(32):
        for nn, wgt in ((4 * j - 1, 1), (4 * j, 2), (4 * j + 1, 2), (4 * j + 2, 2), (4 * j + 3, 1)):
            if 0 <= nn < 127:
                wc[nn, j] += wgt
    c["c_wc"] = wc
    forced = np.zeros((128, 16, 32), np.float32)
    for qt in range(16):
        cur = (qt * 128 + np.arange(128)) // 64
        jj = np.arange(32)[None, :]
        f = np.zeros((128, 32), np.float32)
        f[(jj == 0) | (jj == cur[:, None]) | (jj == cur[:, None] - 1)] = 1e4
        f[jj > cur[:, None]] = -1e4
        forced[:, qt, :] = f
    c["c_forced"] = forced
    E = np.zeros((32, 16, 128), np.float32)
    for kt in range(16):
        kk = kt * 128 + np.arange(128)
        E[kk // 64, kt, np.arange(128)] = 1
    c["c_E"] = E
    s64 = np.arange(64)
    c["c_tri64"] = (s64[:, None] <= s64[None, :]).astype(np.float32)
    sl = np.zeros((64, 128), np.float32); sl[63, :] = 1
    c["c_sellast"] = sl
    cm = np.where(s64[None, :] <= s64[:, None], 0.0, -1e9).astype(np.float32)
    c["c_cmask"] = np.tile(cm[:, None, :], (1, 8, 1)).reshape(64, 512)
    cmT = np.where(s64[:, None] <= s64[None, :], 0.0, -1e9).astype(np.float32)
    c["c_cmaskT"] = np.tile(cmT[:, None, :], (1, 8, 1)).reshape(64, 512)
    s4 = np.arange(4)
    sl4 = np.zeros((4, 128), np.float32); sl4[3, :] = 1
    c["c_sellast4"] = sl4
    cm4 = np.where(s4[None, :] <= s4[:, None], 0.0, -1e9).astype(np.float32)
    c["c_cmask4"] = np.tile(cm4[:, None, :], (1, 8, 1)).reshape(4, 32)
    cmT4 = np.where(s4[:, None] <= s4[None, :], 0.0, -1e9).astype(np.float32)
    c["c_cmaskT4"] = np.tile(cmT4[:, None, :], (1, 8, 1)).reshape(4, 32)
    c["c_iota"] = np.arange(128, dtype=np.float32).reshape(128, 1)
    wcs = np.zeros((1024, 257), np.float32)
    for j in range(257):
        for nn, wgt in ((4 * j - 1, 1), (4 * j, 2), (4 * j + 1, 2), (4 * j + 2, 2), (4 * j + 3, 1)):
            if 0 <= nn < 1023:
                wcs[nn, j] += wgt
    c["c_wcs"] = np.ascontiguousarray(wcs.reshape(8, 128, 257).transpose(1, 0, 2))
    fs = np.zeros((4, 257), np.float32); fs[:, [0, 255, 256]] = 1e4
    c["c_forced_s"] = fs
    sg = np.zeros((16, 4), np.float32)
    for g in range(4):
        for t in range(4):
            sg[g * 4 + t, t] = 1
    c["c_sumg"] = sg
    hm = np.zeros((2, 128), np.float32); hm[0, :64] = 1; hm[1, 64:] = 1
    c["c_hm"] = hm
    c["c_ident"] = ident
    c["c_tri"] = tri
    c["c_sel2"] = sel2
    return c


def build_program():
    nc = bass.Bass("TRN2", target_bir_lowering=False)
    st = ExitStack()
    with st:
        P = Prog(nc, st)
        outs = []

        def din(name, shape, dt=F32):
            return P.dram(name, shape, dt, kind="ExternalInput")

        def dout(name, shape, dt=F32):
            return P.dram(name, shape, dt, kind="ExternalOutput")

        x_p = din("x_p", [T, D])
        mem_p = din("mem_p", [256, D])
        norm_w = din("norm_w", [2, D])
        mem_norm_w = din("mem_norm_w", [2, D])
        final_norm_w = din("final_norm_w", [1, D])
        w_mem_kv = din("w_mem_kv", [2, D, 1024])
        w_in_even = din("w_in_even", [D, EVEN_IN])
        rel_bias = din("rel_bias", [32, 16])
        b_gate = din("b_gate", [1, 48])
        w_cmp1 = din("w_cmp1", [2, 32, 128, 128])
        b_cmp1 = din("b_cmp1", [2, 128])
        w_cmp2 = din("w_cmp2", [2, 128, 128])
        pe_cmp = din("pe_cmp", [64, 128])
        w_out_even = din("w_out_even", [4608, D])
        page_tab = din("page_tab", [1, 128], I32)
        pool_ck = din("pool_ck", [1280 * 128, 512])
        pool_cv = din("pool_cv", [1280 * 128, 512])
        pool_sk = din("pool_sk", [1280 * 128, 512])
        pool_sv = din("pool_sv", [1280 * 128, 512])
        memk_s = din("memk_s", [2, 256, 512])
        memv_s = din("memv_s", [2, 256, 512])
        c_iota = din("c_iota", [128, 1])
        c_wcs = din("c_wcs", [128, 8, 257])
        c_forced_s = din("c_forced_s", [4, 257])
        c_sumg = din("c_sumg", [16, 4])
        c_hm = din("c_hm", [2, 128])
        st_c = din("st_c", [8, 512, 256])
        st_n = din("st_n", [8, 256])
        st_m = din("st_m", [1, 8])
        c_sellast4 = din("c_sellast4", [4, 128])
        c_cmask4 = din("c_cmask4", [4, 32])
        c_cmaskT4 = din("c_cmaskT4", [4, 32])
        x_s = din("x_s", [4, D])
        st_hgrn = din("st_hgrn", [16, 128, 128])
        win_k_in = din("win_k_in", [512, 512])
        win_v_in = din("win_v_in", [512, 512])
        w_in_odd = din("w_in_odd", [D, ODD_IN])
        w_out_odd = din("w_out_odd", [4608, D])
        b_if = din("b_if", [2, 8])
        ml_nw = din("ml_nw", [1, 4096])
        c_tri64 = din("c_tri64", [64, 64])
        c_sellast = din("c_sellast", [64, 128])
        c_cmask = din("c_cmask", [64, 512])
        c_cmaskT = din("c_cmaskT", [64, 512])
        c_oh = din("c_oh", [32, 128])
        c_J = din("c_J", [128, 128])
        c_J127 = din("c_J127", [128, 128])
        c_wc = din("c_wc", [128, 32])
        c_forced = din("c_forced", [128, 16, 32])
        c_E = din("c_E", [32, 16, 128])
        hgrn_lb = din("hgrn_lb", [3, 2048])
        hgrn_nw = din("hgrn_nw", [1, 128])
        c_ident = din("c_ident", [128, 128])
        c_tri = din("c_tri", [128, 128])
        c_sel2 = din("c_sel2", [128, 2])

        o_memk = dout("o_memk", [2, 256, 512])
        o_memv = dout("o_memv", [2, 256, 512])
        o_kv = []
        for j in range(6):
            hh = nc.dram_tensor(f"o_kv{j}", [T, 512], F32, kind="ExternalOutput").ap()
            o_kv.append([Buf(P, hh[i * 128:(i + 1) * 128, :], f"o_kv{j}_{i}") for i in range(NT)])

        o_hgrn = dout("o_hgrn", [16, 128, 128])
        o_y = dout("o_y", [T, D])
        o_ys = dout("o_ys", [4, D])
        o_mc_s = dout("o_mc_s", [8, 512, 256])
        o_mn_s = dout("o_mn_s", [8, 256])
        o_mm_s = dout("o_mm_s", [1, 8])
        o_kvs = [dout(f"o_kvs{j}", [4, 512]) for j in range(4)]
        o_wins = [dout(f"o_wins{j}", [512, 512]) for j in range(2)]
        o_hgrn_s = dout("o_hgrn_s", [16, 128, 128])
        s_nm = P.dram("s_nm", [4, 4, 258], F32)
        s_gs = P.dram("s_gs", [4, 48], F32)
        s_ps0 = P.dram("s_ps0", [4, EVEN_IN], F32)
        s_ps1 = P.dram("s_ps1", [4, ODD_IN], F32)
        s_cat_s = P.dram("s_cat_s", [4, 4608], BF16, kind=("ExternalOutput" if os.environ.get("MK_DEBUG", "0") == "1" else "Internal"))
        s_cat1_s = P.dram("s_cat1_s", [4, 4608], BF16)
        s_h1s = P.dram("s_h1s", [4, D], F32, kind=("ExternalOutput" if os.environ.get("MK_DEBUG", "0") == "1" else "Internal"))
        s_h2s = P.dram("s_h2s", [4, D], F32)
        o_mc = dout("o_mc", [8, 512, 256])
        o_mn = dout("o_mn", [8, 256])
        o_mm = dout("o_mm", [1, 8])
        DEBUG = os.environ.get("MK_DEBUG", "0") == "1"
        def scratch_tiles(name, cols, dt=F32, dbg=False):
            h = nc.dram_tensor(name, [T, cols], dt, kind=("ExternalOutput" if (dbg and DEBUG) else "Internal")).ap()
            return [Buf(P, h[i * 128:(i + 1) * 128, :], f"{name}_{i}") for i in range(NT)]

        s_hg = scratch_tiles("s_hg", 8192)
        s_zb = scratch_tiles("s_zb", 2048)
        s_h1 = scratch_tiles("s_h1", D, dbg=True)
        s_bias_h = nc.dram_tensor("s_bias", [16, 4400], BF16, kind="Internal")
        s_bias = Buf(P, s_bias_h.ap(), "s_bias")
        s_cat = scratch_tiles("s_cat", 4608, BF16, dbg=True)
        s_cat1 = scratch_tiles("s_cat1", 4608, BF16, dbg=True)
        s_h2 = scratch_tiles("s_h2", D)
        s_k1 = scratch_tiles("s_k1", 2048, BF16)
        s_hraw = scratch_tiles("s_hraw", 4096)
        s_hraw_s = P.dram("s_hraw_s", [4, 4096], F32)
        s_v1 = scratch_tiles("s_v1", 4096, BF16)
        s_og = scratch_tiles("s_og", 4096)
        s_z1 = scratch_tiles("s_z1", 4096)
        s_if_h = nc.dram_tensor("s_if", [T, 16], F32, kind="Internal").ap()
        s_if = Buf(P, s_if_h, "s_if")
        q1T_h = nc.dram_tensor("s_q1T", [2048, T], BF16, kind="Internal").ap()
        s_q1T = [[Buf(P, q1T_h[cb * 128:(cb + 1) * 128, pp * 1024:(pp + 1) * 1024], f"s_q1T_{cb}_{pp}") for pp in range(2)] for cb in range(16)]
        qm1T_h = nc.dram_tensor("s_qm1T", [512, T], BF16, kind="Internal").ap()
        s_qm1T = [[Buf(P, qm1T_h[cb * 128:(cb + 1) * 128, pp * 1024:(pp + 1) * 1024], f"s_qm1T_{cb}_{pp}") for pp in range(2)] for cb in range(4)]
        dbg_o = [scratch_tiles(f"dbg_o{br}", 2048, F32, dbg=True) for br in range(3)] if DEBUG else None
        s_gb = scratch_tiles("s_gb", 48)
        qT_h = nc.dram_tensor("s_qT", [2048, T], BF16, kind="Internal").ap()
        s_qT = [[Buf(P, qT_h[cb * 128:(cb + 1) * 128, pp * 1024:(pp + 1) * 1024], f"s_qT_{cb}_{pp}") for pp in range(2)] for cb in range(16)]
        qmT_h = nc.dram_tensor("s_qmT", [512, T], BF16, kind="Internal").ap()
        s_qmT = [[Buf(P, qmT_h[cb * 128:(cb + 1) * 128, pp * 1024:(pp + 1) * 1024], f"s_qmT_{cb}_{pp}") for pp in range(2)] for cb in range(4)]

        ident_f = P.sbuf([128, 128], F32, "ident_f")
        ident_b = P.sbuf([128, 128], BF16, "ident_b")
        P.dma(ident_f[:, :], c_ident[:, :])
        P.copy(ident_b[:, :], ident_f[:, :])
        SC = 128 ** -0.5

        def proj_env():
            class E_: pass
            E = E_()
            E.after_load = None
            P.begin_phase()
            wbc = P.sbuf([128, D], F32, "wbc")
            hnT = P.sbuf([128, 32, 1024], BF16, "hnT")
            xpool = Pool(P, 2, [128, D], F32, "xt")
            ybf = Pool(P, 1, [128, D], BF16, "ybf")
            junk = P.sbuf([128, D], BF16, "junk")
            small = Pool(P, 8, [128, 8], F32, "small")
            wpool = Pool(P, 2, [128, 32, 512], BF16, "wch")
            stage = Pool(P, 3, [128, 512], F32, "stg")
            stage_b = Pool(P, 4, [128, 512], BF16, "stgb")
            psA = Pool(P, 4, [128, 512], F32, "psA", psum=True)
            psT = Pool(P, 2, [128, 4, 128], BF16, "psT", psum=True)

            def rmsnorm_to_T(x_rows, dstT, col0, nrows=128):
                xt = xpool.next()
                P.dma(xt[:nrows, :], x_rows)
                ss = small.next()
                P.memset(ss[:, 0:1], 0.0)
                P.act(junk[:nrows, :], xt[:nrows, :], AF.Square, accum_out=ss[:nrows, 0:1])
                P.ts(ss[:nrows, 1:2], ss[:nrows, 0:1], 1.0 / D, ALU.mult, EPS, ALU.add)
                P.act(ss[:nrows, 3:4], ss[:nrows, 1:2], AF.Sqrt)
                P.op("vector", "reciprocal", out=ss[:nrows, 2:3], in_=ss[:nrows, 3:4])
                yb = ybf.next()
                P.stt(yb[:nrows, :], xt[:nrows, :], ss[:nrows, 2:3], wbc[:nrows, :], ALU.mult, ALU.mult)
                for g in range(8):
                    pt = psT.next()
                    for j in range(4):
                        k = g * 4 + j
                        P.transpose(pt[:, j, :nrows], yb[:nrows, k * 128:(k + 1) * 128], ident_b[:nrows, :nrows])
                    P.copy(dstT[:, g * 4:(g + 1) * 4, col0:col0 + nrows], pt[:, :, :nrows], eng="scalar" if g % 2 else "vector")

            def load_w(wdram, c0, ncols):
                wt = wpool.next()
                wv = wdram if isinstance(wdram, V) else wdram[:, :]
                src = wv.rearrange("(k p) c -> p k c", p=128)[:, :, c0:c0 + ncols]
                P.dma(wt[:, :, :ncols], src, eng="gpsimd")
                if E.after_load is not None:
                    E.after_load(wt, c0, ncols)
                return wt

            def proj_tok(wt, ncols, ntok_tiles, sink):
                for tt in range(ntok_tiles):
                    ps = psA.next()
                    for k in range(32):
                        P.mm(ps[:, :ncols], lhsT=hnT[:, k, tt * 128:(tt + 1) * 128], rhs=wt[:, k, :ncols], start=(k == 0), stop=(k == 31))
                    sink(tt, ps)

            def proj_feat(wt, ncb, nhalves, sink):
                for cb in range(ncb):
                    for hf in range(nhalves):
                        ps = psA.next()
                        for k in range(32):
                            P.mm(ps[:, :], lhsT=wt[:, k, cb * 128:(cb + 1) * 128], rhs=hnT[:, k, hf * 512:(hf + 1) * 512], start=(k == 0), stop=(k == 31))
                        sink(cb, hf, ps)


            hnTs = P.sbuf([128, 32, 4], BF16, "hnTs")
            E.hnTs = hnTs
            def sample_proj(dst):
                def f(wt, c0, ncols):
                    ps = psA.next()
                    for k in range(32):
                        P.mm(ps[0:4, :ncols], lhsT=hnTs[:, k, 0:4], rhs=wt[:, k, :ncols], start=(k == 0), stop=(k == 31))
                    sg = stage.next()
                    P.copy(sg[0:4, :ncols], ps[0:4, :ncols])
                    P.dma(dst[0:4, c0:c0 + ncols], sg[0:4, :ncols])
                return f
            E.sample_proj = sample_proj
            E.wbc, E.hnT, E.small, E.stage, E.stage_b, E.psA = wbc, hnT, small, stage, stage_b, psA
            E.rmsnorm_to_T, E.load_w, E.proj_tok, E.proj_feat = rmsnorm_to_T, load_w, proj_tok, proj_feat
            return E

        E = proj_env()
        wbc, hnT, stage, stage_b, psA = E.wbc, E.hnT, E.stage, E.stage_b, E.psA
        rmsnorm_to_T, load_w, proj_tok, proj_feat = E.rmsnorm_to_T, E.load_w, E.proj_tok, E.proj_feat

        memT = hnT
        for l in range(2):
            P.dma(wbc[:, :], mem_norm_w[l:l + 1, :].partition_broadcast(128))
            for mt in range(2):
                rmsnorm_to_T(mem_p[mt * 128:(mt + 1) * 128, :], memT, mt * 128)
            for ch in range(2):
                wt = load_w(w_mem_kv[l], ch * 512, 512)
                for mt in range(2):
                    ps = psA.next()
                    for k in range(32):
                        P.mm(ps[:, :], lhsT=memT[:, k, mt * 128:(mt + 1) * 128], rhs=wt[:, k, :], start=(k == 0), stop=(k == 31))
                    sg = stage.next()
                    P.copy(sg[:, :], ps[:, :])
                    dst = (o_memk if ch == 0 else o_memv)
                    outs.append(P.dma(dst[l, mt * 128:(mt + 1) * 128, :], sg[:, :]))

        P.dma(wbc[:, :], norm_w[0:1, :].partition_broadcast(128))
        rmsnorm_to_T(x_s[0:4, :], E.hnTs, 0, nrows=4)
        for pp in range(2):
            E.after_load = E.sample_proj(s_ps0) if pp == 0 else None
            for tt in range(8):
                t0 = pp * 1024 + tt * 128
                rmsnorm_to_T(x_p[t0:t0 + 128, :], hnT, tt * 128)
            def sink_hg(cbase):
                def f(tt, ps):
                    sg = stage.next()
                    P.copy(sg[:, :], ps[:, :], eng="scalar" if tt % 2 else "vector")
                    P.dma(s_hg[pp * 8 + tt][:, cbase:cbase + 512], sg[:, :])
                return f
            for ci in range(16):
                wt = load_w(w_in_even, ci * 512, 512)
                proj_tok(wt, 512, 8, sink_hg(ci * 512))
            for ci in range(4):
                wt = load_w(w_in_even, 8192 + ci * 512, 512)
                def sink_q(cb, hf, ps, ci=ci):
                    sg = stage_b.next()
                    P.act(sg[:, :], ps[:, :], AF.Copy, scale=SC)
                    P.dma(s_qT[ci * 4 + cb][pp][:, hf * 512:(hf + 1) * 512], sg[:, :])
                proj_feat(wt, 4, 2, sink_q)
            for j in range(6):
                wt = load_w(w_in_even, 10240 + j * 512, 512)
                def sink_kv(tt, ps, j=j):
                    sg = stage.next()
                    P.copy(sg[:, :], ps[:, :], eng="scalar" if tt % 2 else "vector")
                    t0 = pp * 1024 + tt * 128
                    outs.append(P.dma(o_kv[j][pp * 8 + tt][:, :], sg[:, :]))
                proj_tok(wt, 512, 8, sink_kv)
            wt = load_w(w_in_even, 13312, 48)
            def sink_gb(tt, ps):
                sg = stage.next()
                P.copy(sg[:, :48], ps[:, :48])
                P.dma(s_gb[pp * 8 + tt][:, :], sg[:, :48])
            proj_tok(wt, 48, 8, sink_gb)
            for ci in range(4):
                wt = load_w(w_in_even, 13360 + ci * 512, 512)
                def sink_zb(tt, ps, ci=ci):
                    sg = stage.next()
                    P.copy(sg[:, :], ps[:, :], eng="scalar" if tt % 2 else "vector")
                    P.dma(s_zb[pp * 8 + tt][:, ci * 512:(ci + 1) * 512], sg[:, :])
                proj_tok(wt, 512, 8, sink_zb)
            wt = load_w(w_in_even, 15408, 512)
            def sink_qm(cb, hf, ps):
                sg = stage_b.next()
                P.act(sg[:, :], ps[:, :], AF.Copy, scale=SC)
                P.dma(s_qmT[cb][pp][:, hf * 512:(hf + 1) * 512], sg[:, :])
            proj_feat(wt, 4, 2, sink_qm)

        E.after_load = None
        for j in range(4):
            outs.append(P.dma(o_kvs[j][0:4, :], s_ps0[0:4, 10240 + j * 512:10240 + (j + 1) * 512]))
        for j, win_in in enumerate((win_k_in, win_v_in)):
            outs.append(P.dma(o_wins[j][0:508, :], win_in[4:512, :]))
            outs.append(P.dma(o_wins[j][508:512, :], s_ps0[0:4, 10240 + (4 + j) * 512:10240 + (5 + j) * 512]))
        P.end_phase()

        class PSB:
            def __init__(self, name):
                self.h = P.stack.enter_context(nc.psum_tensor(name, [128, 8, 512], F32))
                self.b = [Buf(P, None, f"{name}_b{i}") for i in range(8)]
            def f32(self, b0, n=1):
                return V(self.b[b0:b0 + n], self.h[:, b0:b0 + n, :].rearrange("p a b -> p (a b)"))
            def bf16(self, b0):
                return V(self.b[b0:b0 + 1], self.h[:, b0, :].bitcast(BF16))

        if STAGE >= 2:
            P.begin_phase()
            PS = PSB("psb")
            tri = P.sbuf([128, 128], F32, "tri")
            sel2 = P.sbuf([128, 2], F32, "sel2")
            P.dma(tri[:, :], c_tri[:, :])
            P.dma(sel2[:, :], c_sel2[:, :])
            lbbc = P.sbuf([128, 2048], F32, "lbbc")
            omlbc = P.sbuf([128, 2048], F32, "omlbc")
            gnbc = P.sbuf([128, 128], F32, "gnbc")
            P.dma(gnbc[:, :], hgrn_nw[0:1, :].partition_broadcast(128))
            w1 = P.sbuf([128, 2048], F32, "w1")
            w2 = P.sbuf([128, 2048], F32, "w2")
            w3 = P.sbuf([128, 2048], F32, "w3")
            w4 = P.sbuf([128, 2048], F32, "w4")
            for r, dst in enumerate((w1, w2, w3)):
                P.dma(dst[:, :], hgrn_lb[r:r + 1, :].partition_broadcast(128))
                P.act(dst[:, :], dst[:, :], AF.Exp)
            P.tt(w4[:, :], w1[:, :], w2[:, :], ALU.add)
            P.tt(w4[:, :], w4[:, :], w3[:, :], ALU.add)
            P.op("vector", "reciprocal", out=w4[:, :], in_=w4[:, :])
            P.tt(lbbc[:, :], w1[:, :], w4[:, :], ALU.mult)
            P.ts(omlbc[:, :], lbbc[:, :], -1.0, ALU.mult, 1.0, ALU.add)

            hgp = Pool(P, 2, [128, 8192], F32, "hgt")
            qgb = P.sbuf([128, 2048], BF16, "qgb")
            kgb = P.sbuf([128, 2048], BF16, "kgb")
            vb = P.sbuf([128, 2048], BF16, "vb")
            qgT = P.sbuf([128, 16, 128], BF16, "qgT")
            kgT = P.sbuf([128, 16, 128], BF16, "kgT")
            qgT_lo = P.sbuf([128, 16, 128], BF16, "qgT_lo")
            qgT_hi = P.sbuf([128, 16, 128], BF16, "qgT_hi")
            mlo = P.sbuf([128, 128], BF16, "mlo")
            mhi = P.sbuf([128, 128], BF16, "mhi")
            P.memset(mlo[:, 0:64], 1.0)
            P.memset(mlo[:, 64:128], 0.0)
            P.memset(mhi[:, 0:64], 0.0)
            P.memset(mhi[:, 64:128], 1.0)
            attb = P.sbuf([128, 16, 128], BF16, "attb")
            eBl = P.sbuf([128, 16, 2], F32, "eBl")
            S = P.sbuf([128, 16, 128], F32, "S")
            Sb = P.sbuf([128, 16, 128], BF16, "Sb")
            oab = Pool(P, 2, [128, 2048], BF16, "oab")
            sm = Pool(P, 4, [128, 64], F32, "smB")
            P.memset(S[:, :, :], 0.0)
            P.memset(Sb[:, :, :], 0.0)
            def hgrn_tile(nr, chunks, hg_src, triV, selV, S, Sb, cat_dst):
                R = slice(0, nr)
                nch = len(chunks)
                hg = hgp.next()
                P.dma(hg[R, :], hg_src)
                qa, fa, ia, za = (hg[R, j * 2048:(j + 1) * 2048] for j in range(4))
                P.act(w1[R, :], fa, AF.Sigmoid)
                P.tt(w1[R, :], w1[R, :], omlbc[R, :], ALU.mult)
                P.tt(w1[R, :], w1[R, :], lbbc[R, :], ALU.add, eng="gpsimd")
                P.act(w2[R, :], w1[R, :], AF.Ln)
                P.ts(w1[R, :], w1[R, :], -1.0, ALU.mult, 1.0, ALU.add)
                for q4 in range(4):
                    P.mm(PS.f32(q4)[R, :], lhsT=triV, rhs=w2[R, q4 * 512:(q4 + 1) * 512])
                bc = PS.f32(0, 4)[R, :]
                P.act(w3[R, :], bc, AF.Exp)
                P.act(w4[R, :], bc, AF.Exp, scale=-1.0)
                P.tt(kgb[R, :], w1[R, :], w4[R, :], ALU.mult)
                P.act(w1[R, :], qa, AF.Silu)
                P.tt(qgb[R, :], w1[R, :], w3[R, :], ALU.mult)
                P.copy(vb[R, :], ia, eng="gpsimd")
                blp = PS.f32(6)
                for h in range(16):
                    P.mm(blp[:, h * nch:(h + 1) * nch], lhsT=w2[R, h * 128:(h + 1) * 128], rhs=selV)
                P.act(eBl[:, :, 0:nch], blp[:, 0:16 * nch].rearrange("p (a b) -> p a b", a=16), AF.Exp)
                for src_t, dst_t in ((qgb, qgT), (kgb, kgT)):
                    for g in range(4):
                        pt = PS.bf16(4 + g % 2)
                        for j in range(4):
                            h = g * 4 + j
                            P.transpose(pt[:, j * 128:j * 128 + nr], src_t[R, h * 128:(h + 1) * 128], ident_b[R, R])
                        P.copy(dst_t[:, g * 4:(g + 1) * 4, 0:nr], pt[:, 0:512].rearrange("p (a b) -> p a b", a=4)[:, :, 0:nr], eng="scalar" if g % 2 else "vector")
                for g in range(4):
                    ap_ = PS.f32(g)
                    for j in range(4):
                        h = g * 4 + j
                        P.mm(ap_[R, j * 128:j * 128 + nr], lhsT=kgT[:, h, 0:nr], rhs=qgT[:, h, 0:nr])
                    P.tt(attb[R, g * 4:(g + 1) * 4, 0:nr], ap_[R, :].rearrange("p (a b) -> p a b", a=4)[:, :, 0:nr], triV.unsqueeze(1).to_broadcast([nr, 4, nr]), ALU.mult)
                for h in range(16):
                    P.mm(PS.f32(4 + h // 4)[R, (h % 4) * 128:(h % 4 + 1) * 128], lhsT=attb[R, h, 0:nr], rhs=vb[R, h * 128:(h + 1) * 128], start=(h % 4 == 0), stop=False, skip_group_check=True)
                for j, (r0, rl) in enumerate(chunks):
                    for h in range(16):
                        P.mm(PS.f32(4 + h // 4)[r0:r0 + rl, (h % 4) * 128:(h % 4 + 1) * 128], lhsT=qgT[:, h, r0:r0 + rl], rhs=Sb[:, h, :], start=False, stop=(j == nch - 1), skip_group_check=True)
                    for h in range(16):
                        P.mm(PS.f32(h // 4)[:, (h % 4) * 128:(h % 4 + 1) * 128], lhsT=kgb[r0:r0 + rl, h * 128:(h + 1) * 128], rhs=vb[r0:r0 + rl, h * 128:(h + 1) * 128])
                    S2 = S[:, :, :].rearrange("p a b -> p (a b)")
                    P.tt(w3[:, :], PS.f32(0, 4), S2, ALU.add)
                    P.tt(S[:, :, :], w3[:, :].rearrange("p (a b) -> p a b", a=16), eBl[:, :, j:j + 1].to_broadcast([128, 16, 128]), ALU.mult)
                    P.copy(Sb[:, :, :], S[:, :, :], eng="scalar")
                ops_ = PS.f32(4, 4)[R, :]
                s8 = sm.next()
                P.act(w4[R, :], ops_, AF.Square)
                P.op("vector", "tensor_reduce", out=s8[R, 0:16], in_=w4[R, :].rearrange("p (a b) -> p a b", a=16), axis=AX.X, op=ALU.add)
                P.ts(s8[R, 16:32], s8[R, 0:16], 1.0 / 128, ALU.mult, EPS, ALU.add)
                P.act(s8[R, 32:48], s8[R, 16:32], AF.Sqrt)
                P.op("vector", "reciprocal", out=s8[R, 48:64], in_=s8[R, 32:48])
                P.tt(w3[R, :].rearrange("p (a b) -> p a b", a=16), ops_.rearrange("p (a b) -> p a b", a=16), s8[R, 48:64].unsqueeze(2).to_broadcast([nr, 16, 128]), ALU.mult)
                P.tt(w3[R, :].rearrange("p (a b) -> p a b", a=16), w3[R, :].rearrange("p (a b) -> p a b", a=16), gnbc[R, :].unsqueeze(1).to_broadcast([nr, 16, 128]), ALU.mult, eng="gpsimd")
                P.act(w4[R, :], za, AF.Silu)
                ob_ = oab.next()
                P.tt(ob_[R, :], w3[R, :], w4[R, :], ALU.mult)
                P.dma(cat_dst, ob_[R, :])

            for i in range(NT):
                hgrn_tile(128, [(0, 64), (64, 64)], s_hg[i][:, :], tri[:, :], sel2[:, :], S, Sb, s_cat[i][:, 0:2048])
            outs.append(P.dma(o_hgrn[:, :, :].rearrange("h k v -> k h v"), S[:, :, :]))
            P.dma(S[:, :, :], st_hgrn[:, :, :].rearrange("h k v -> k h v"))
            P.copy(Sb[:, :, :], S[:, :, :])
            tri4 = tri[0:4, 0:4]
            ones4 = P.sbuf([4, 1], F32, "ones4")
            P.memset(ones4[:, :], 1.0)
            hgrn_tile(4, [(0, 4)], s_ps0[0:4, 0:8192], tri4, ones4[:, :], S, Sb, s_cat_s[0:4, 0:2048])
            outs.append(P.dma(o_hgrn_s[:, :, :].rearrange("h k v -> k h v"), S[:, :, :]))
            P.end_phase()

        if STAGE >= 3:
            kcmpT = P.sbuf([128, 4, 128], BF16, "kcmpT")
            vcmpA = P.sbuf([128, 4, 161], BF16, "vcmpA")
            Jb = P.sbuf([128, 128], BF16, "Jb")
            J127b = P.sbuf([128, 128], BF16, "J127b")
            P.begin_phase()
            PS = PSB("psc1")
            ldf = Pool(P, 2, [128, 512], F32, "ldf")
            ldb = Pool(P, 2, [128, 512], BF16, "ldb")
            tmpf = P.sbuf([128, 128], F32, "tmpf")
            P.dma(tmpf[:, :], c_J[:, :]); P.copy(Jb[:, :], tmpf[:, :])
            tmpf2 = P.sbuf([128, 128], F32, "tmpf2")
            P.dma(tmpf2[:, :], c_J127[:, :]); P.copy(J127b[:, :], tmpf2[:, :])
            wcf = P.sbuf([128, 32], F32, "wcf")
            P.dma(wcf[:, :], c_wc[:, :])
            w1b = P.sbuf([128, 2, 32, 128], BF16, "w1b")
            P.dma(w1b[:, :, :, :], w_cmp1[:, :, :, :].rearrange("a s d e -> d a s e"), eng="gpsimd")
            w2b = P.sbuf([128, 2, 128], BF16, "w2b")
            P.dma(w2b[:, :, :], w_cmp2[:, :, :].rearrange("a e d -> e a d"), eng="gpsimd")
            pef = P.sbuf([64, 128], F32, "pef")
            P.dma(pef[:, :], pe_cmp[:, :])
            peT = P.sbuf([128, 64], BF16, "peT")
            pp_ = PS.f32(7)
            P.transpose(pp_[:, 0:64], pef[:, :], ident_f[0:64, 0:64])
            P.copy(peT[:, :], pp_[:, 0:64])
            b1f = P.sbuf([128, 2], F32, "b1f")
            P.dma(b1f[:, :], b_cmp1[:, :].rearrange("a e -> e a"), allow_slow_non_contiguous=True)
            b1p = P.sbuf([128, 2], F32, "b1p")
            for kv in range(2):
                bp = PS.f32(6)
                for s in range(32):
                    P.mm(bp[:, kv:kv + 1], lhsT=w1b[:, kv, s, :], rhs=peT[:, kv * 32 + s:kv * 32 + s + 1], start=(s == 0), stop=(s == 31))
                P.tt(b1p[:, kv:kv + 1], bp[:, kv:kv + 1], b1f[:, kv:kv + 1], ALU.add)
            KT = [P.sbuf([128, 4, T], BF16, f"KT{j}") for j in range(2)]
            for j in range(2):
                for i in range(NT):
                    lf = ldf.next()
                    P.dma(lf[:, :], o_kv[j][i][:, :])
                    lb_ = ldb.next()
                    P.copy(lb_[:, :], lf[:, :], eng="gpsimd")
                    pt = PS.bf16(i % 2)
                    for kvh in range(4):
                        P.transpose(pt[:, kvh * 128:(kvh + 1) * 128], lb_[:, kvh * 128:(kvh + 1) * 128], ident_b[:, :])
                    P.copy(KT[j][:, :, i * 128:(i + 1) * 128], pt[:, 0:512].rearrange("p (a b) -> p a b", a=4), eng="scalar" if i % 2 else "vector")
            gT = Pool(P, 2, [128, 128], BF16, "gT")
            P.memset(vcmpA[:, :, :], 0.0)
            P.memset(kcmpT[:, :, :], 0.0)
            for kv in range(2):
                for kvh in range(4):
                    hps = PS.f32(2 + (kvh % 2))
                    for s in range(32):
                        P.mm(hps[:, 0:127], lhsT=w1b[:, kv, s, :], rhs=KT[kv][:, kvh, s:s + 2017:16], start=(s == 0), stop=(s == 31))
                    g_ = gT.next()
                    P.act(g_[:, 0:127], hps[:, 0:127], AF.Gelu_apprx_tanh, bias=b1p[:, kv:kv + 1])
                    p2 = PS.f32(4 + (kvh % 2))
                    if kv == 0:
                        P.mm(p2[:, 0:127], lhsT=w2b[:, 0, :], rhs=g_[:, 0:127])
                        P.copy(kcmpT[:, kvh, 0:127], p2[:, 0:127])
                    else:
                        P.mm(p2[0:127, 0:128], lhsT=g_[:, 0:127], rhs=w2b[:, 1, :])
                        P.copy(vcmpA[0:127, kvh, 0:128], p2[0:127, 0:128])
            for kvh in range(4):
                P.memset(vcmpA[:, kvh, 128:129], 1.0)
                P.copy(vcmpA[:, kvh, 129:161], wcf[:, :])
            P.end_phase()

        if STAGE >= 3:
            P.begin_phase()
            PS = PSB("psc2")
            rb = P.sbuf([32, 16], F32, "rb")
            ohf = P.sbuf([32, 128], F32, "ohf")
            P.dma(rb[:, :], rel_bias[:, :])
            P.dma(ohf[:, :], c_oh[:, :])
            tb = PS.f32(7)
            P.mm(tb[0:16, 0:128], lhsT=rb[:, :], rhs=ohf[:, :])
            Grow = P.sbuf([16, 4400], F32, "Grow")
            ch = P.sbuf([16, 1], F32, "ch")
            P.copy(ch[:, :], tb[0:16, 127:128])
            P.memset(Grow[:, :], 0.0)
            P.memset(Grow[:, 0:2047], NEG)
            P.copy(Grow[:, 2047:2175], tb[0:16, 0:128])
            P.ts(Grow[:, 2175:4223], Grow[:, 2175:4223], ch[:, 0:1], ALU.add)
            P.memset(Grow[:, 4223:4400], NEG)
            Gb = P.sbuf([16, 4400], BF16, "Gb")
            P.copy(Gb[:, :], Grow[:, :])
            P.dma(s_bias[:, :], Gb[:, :])

            def bias_src(h, off, pstep, np_, nfree):
                return V(s_bias, bass.AP(s_bias_h.ap().tensor, h * 4400 + off, [[pstep, np_], [1, nfree]]))

            ksT = P.sbuf([128, 4, T], BF16, "ksT")
            kwT = P.sbuf([128, 4, T], BF16, "kwT")
            vsA = P.sbuf([128, NT, 4, 129], BF16, "vsA")
            vwA = P.sbuf([128, NT, 4, 129], BF16, "vwA")
            ldf = Pool(P, 2, [128, 512], F32, "ldf2")
            ldb = Pool(P, 2, [128, 512], BF16, "ldb2")
            P.memset(vsA[:, :, :, 128:129], 1.0)
            P.memset(vwA[:, :, :, 128:129], 1.0)
            for j, dstT in ((2, ksT), (4, kwT)):
                for i in range(NT):
                    lf = ldf.next()
                    P.dma(lf[:, :], o_kv[j][i][:, :])
                    lb_ = ldb.next()
                    P.copy(lb_[:, :], lf[:, :], eng="gpsimd")
                    pt = PS.bf16(i % 2)
                    for kvh in range(4):
                        P.transpose(pt[:, kvh * 128:(kvh + 1) * 128], lb_[:, kvh * 128:(kvh + 1) * 128], ident_b[:, :])
                    P.copy(dstT[:, :, i * 128:(i + 1) * 128], pt[:, 0:512].rearrange("p (a b) -> p a b", a=4), eng="scalar" if i % 2 else "vector")
            for j, dstV in ((3, vsA), (5, vwA)):
                for i in range(NT):
                    lf = ldf.next()
                    P.dma(lf[:, :], o_kv[j][i][:, :])
                    P.copy(dstV[:, i, :, 0:128], lf[:, :].rearrange("p (a b) -> p a b", a=4), eng="scalar" if i % 2 else "vector")
            gates = P.sbuf([128, NT, 48], F32, "gates")
            bgbc = P.sbuf([128, 48], F32, "bgbc")
            P.dma(bgbc[:, :], b_gate[0:1, :].partition_broadcast(128))
            for i in range(NT):
                P.dma(gates[:, i, :], s_gb[i][:, :])
            P.tt(gates[:, :, :], gates[:, :, :], bgbc[:, :].unsqueeze(1).to_broadcast([128, NT, 48]), ALU.add)
            P.act(gates[:, :, :], gates[:, :, :], AF.Sigmoid)
            forced = P.sbuf([128, 16, 32], F32, "forced")
            P.dma(forced[:, :, :], c_forced[:, :, :])
            Ef = P.sbuf([32, 16, 128], F32, "Ef")
            P.dma(Ef[:, :, :], c_E[:, :, :])
            Eb = P.sbuf([32, 16, 128], BF16, "Eb")
            P.copy(Eb[:, :, :], Ef[:, :, :])

            Qp = Pool(P, 2, [128, 4, T], BF16, "Qk")
            CBp = P.sbuf([128, 4, T], BF16, "CBp")
            Bt = P.sbuf([128, 4, 4, 128], BF16, "Bt")
            PTp = Pool(P, 3, [128, 512], BF16, "PT")
            accp = Pool(P, 2, [128, 4, 128], F32, "acc")
            zbp = Pool(P, 2, [128, 512], F32, "zbt")
            obp = Pool(P, 2, [128, 512], BF16, "obt")
            smc = Pool(P, 6, [128, 16], F32, "smc")
            psl = P.sbuf([128, 32], F32, "psl")
            sc1 = P.sbuf([128, 32], F32, "sc1")
            sc2 = P.sbuf([128, 32], F32, "sc2")
            m8 = P.sbuf([128, 8], F32, "m8")
            nmb = P.sbuf([128, 32], BF16, "nmb")
            nmTp = Pool(P, 2, [32, 128], BF16, "nmT")
            P.memset(CBp[:, :, :], 0.0)

            def combine(acc, Obanks, width, gate_cols, first, qt, rs_keep=None):
                for hb in range(2):
                    Ov = Obanks[hb][:, 0:2 * width].rearrange("p (g c) -> p g c", g=2)
                    s_ = smc.next()
                    P.ts(s_[:, 0:2], Ov[:, :, 128:129].rearrange("p g c -> p (g c)"), 1e-30, ALU.max)
                    P.op("vector", "reciprocal", out=s_[:, 2:4], in_=s_[:, 0:2])
                    if rs_keep is not None:
                        P.copy(rs_keep[:, hb * 2:hb * 2 + 2], s_[:, 2:4])
                    if DEBUG:
                        td = smallbig.next()
                        P.tt(td[:, :, :], Ov[:, :, 0:128], s_[:, 2:4].unsqueeze(2).to_broadcast([128, 2, 128]), ALU.mult)
                        br = gate_cols // 16
                        kvh_ = (gate_cols % 16) // 4
                        P.dma(dbg_o[br][qt][:, (kvh_ * 4 + hb * 2) * 128:(kvh_ * 4 + hb * 2 + 2) * 128], td[:, :, :].rearrange("p a b -> p (a b)"))
                    P.tt(s_[:, 4:6], s_[:, 2:4], gates[:, qt, gate_cols + hb * 2:gate_cols + hb * 2 + 2], ALU.mult)
                    if first:
                        P.tt(acc[:, hb * 2:hb * 2 + 2, :], Ov[:, :, 0:128], s_[:, 4:6].unsqueeze(2).to_broadcast([128, 2, 128]), ALU.mult)
                    else:
                        tmp = smallbig.next()
                        P.tt(tmp[:, :, :], Ov[:, :, 0:128], s_[:, 4:6].unsqueeze(2).to_broadcast([128, 2, 128]), ALU.mult)
                        P.tt(acc[:, hb * 2:hb * 2 + 2, :], acc[:, hb * 2:hb * 2 + 2, :], tmp[:, :, :], ALU.add, eng="gpsimd")

            smallbig = Pool(P, 2, [128, 2, 128], F32, "sbg")
            rsk = P.sbuf([128, 4], F32, "rsk")

            for kvh in range(4):
                Q = Qp.next()
                for g in range(4):
                    h = kvh * 4 + g
                    for pp in range(2):
                        P.dma(Q[:, g, pp * 1024:(pp + 1) * 1024], s_qT[h][pp][:, :])
                for g in range(4):
                    h = kvh * 4 + g
                    P.dma(CBp[0:127, g, :], bias_src(h, 0, 16, 127, T))
                    P.dma(Bt[:, 0, g, :], bias_src(h, 1920, 1, 128, 128))
                    P.dma(Bt[:, 1, g, :], bias_src(h, 2048, 1, 128, 128))
                    P.dma(Bt[:, 2, g, :], bias_src(h, 2300, 1, 128, 128))
                    P.dma(Bt[:, 3, g, :], bias_src(h, 4096, 1, 128, 128))
                def cmp_part(qt):
                    q0 = qt * 128
                    Qt = Q[:, :, q0:q0 + 128]
                    acc = accp.next()
                    sp = PS.f32(qt % 2)
                    P.mm(sp[0:127, :], lhsT=kcmpT[:, kvh, 0:127], rhs=Qt, start=True, stop=False)
                    P.mm(sp[0:127, :], lhsT=J127b[0:127, 0:127], rhs=CBp[0:127, :, q0:q0 + 128], start=False, stop=True)
                    pt_ = PTp.next()
                    P.act(pt_[0:127, :], sp[0:127, :], AF.Exp)
                    Oc = [PS.f32(2), PS.f32(3)]
                    for g in range(4):
                        P.mm(Oc[g // 2][:, (g % 2) * 161:(g % 2 + 1) * 161], lhsT=pt_[0:127, g * 128:(g + 1) * 128], rhs=vcmpA[0:127, kvh, :])
                    combine(acc, Oc, 161, 0 * 16 + kvh * 4, True, qt, rs_keep=rsk)
                    use_sel = qt >= 8
                    nmT = nmTp.next() if use_sel else None
                    if use_sel:
                        for g in range(4):
                            Og = Oc[g // 2][:, (g % 2) * 161 + 129:(g % 2) * 161 + 161]
                            if g == 0:
                                P.ts(psl[:, :], Og, rsk[:, 0:1], ALU.mult)
                            else:
                                P.stt(psl[:, :], Og, rsk[:, g:g + 1], psl[:, :], ALU.mult, ALU.add)
                        P.tt(sc1[:, :], psl[:, :], forced[:, qt, :], ALU.add)
                        P.op("vector", "max", out=m8[:, :], in_=sc1[:, :])
                        P.op("vector", "match_replace", out=sc2[:, :], in_to_replace=m8[:, :], in_values=sc1[:, :], imm_value=-1e30)
                        P.op("vector", "max", out=m8[:, :], in_=sc2[:, :])
                        P.ts(sc2[:, :], sc1[:, :], m8[:, 7:8], ALU.is_ge)
                        P.ts(nmb[:, :], sc2[:, :], -NEG, ALU.mult, NEG, ALU.add)
                        ptn = PS.bf16(2)
                        P.transpose(ptn[0:32, 0:128], nmb[:, :], ident_b[:, :])
                        P.copy(nmT[:, :], ptn[0:32, 0:128])
                    return (acc, nmT if use_sel else None, use_sel)
                def rest_part(qt, st_):
                    acc, nmT, use_sel = st_
                    q0 = qt * 128
                    Qt = Q[:, :, q0:q0 + 128]
                    Os = [PS.f32(4), PS.f32(5)]
                    for kt in range(qt + 1):
                        sp = PS.f32(kt % 2)
                        P.mm(sp[:, :], lhsT=ksT[:, kvh, kt * 128:(kt + 1) * 128], rhs=Qt, start=True, stop=False)
                        btype = 0 if kt == qt else (1 if kt == qt - 1 else 2)
                        P.mm(sp[:, :], lhsT=Jb[:, :], rhs=Bt[:, btype, :, :], start=False, stop=not use_sel)
                        if use_sel:
                            P.mm(sp[:, :], lhsT=Eb[:, kt, :], rhs=nmT[:, :].unsqueeze(1).to_broadcast([32, 4, 128]), start=False, stop=True)
                        pt_ = PTp.next()
                        P.act(pt_[:, :], sp[:, :], AF.Exp)
                        for g in range(4):
                            P.mm(Os[g // 2][:, (g % 2) * 129:(g % 2 + 1) * 129], lhsT=pt_[:, g * 128:(g + 1) * 128], rhs=vsA[:, kt, kvh, :], start=(kt == 0 and g % 2 == 0), stop=(kt == qt), skip_group_check=True)
                    combine(acc, Os, 129, 1 * 16 + kvh * 4, False, qt)
                    Ow = [PS.f32(6), PS.f32(7)]
                    k0 = max(0, qt - 4)
                    for kt in range(k0, qt + 1):
                        sp = PS.f32(kt % 2)
                        P.mm(sp[:, :], lhsT=kwT[:, kvh, kt * 128:(kt + 1) * 128], rhs=Qt, start=True, stop=False)
                        dlt = qt - kt
                        btype = 0 if dlt == 0 else (1 if dlt == 1 else (3 if dlt == 4 else 2))
                        P.mm(sp[:, :], lhsT=Jb[:, :], rhs=Bt[:, btype, :, :], start=False, stop=True)
                        pt_ = PTp.next()
                        P.act(pt_[:, :], sp[:, :], AF.Exp)
                        for g in range(4):
                            P.mm(Ow[g // 2][:, (g % 2) * 129:(g % 2 + 1) * 129], lhsT=pt_[:, g * 128:(g + 1) * 128], rhs=vwA[:, kt, kvh, :], start=(kt == k0 and g % 2 == 0), stop=(kt == qt), skip_group_check=True)
                    combine(acc, Ow, 129, 2 * 16 + kvh * 4, False, qt)
                    zt = zbp.next()
                    P.dma(zt[:, :], s_zb[qt][:, kvh * 512:(kvh + 1) * 512])
                    P.act(zt[:, :], zt[:, :], AF.Silu)
                    ot = obp.next()
                    P.tt(ot[:, :], acc[:, :, :].rearrange("p a b -> p (a b)"), zt[:, :], ALU.mult)
                    P.dma(s_cat[qt][:, 2048 + kvh * 512:2048 + (kvh + 1) * 512], ot[:, :])
                st_ = cmp_part(0)
                for qt in range(NT):
                    nxt_ = cmp_part(qt + 1) if qt + 1 < NT else None
                    rest_part(qt, st_)
                    st_ = nxt_
            P.end_phase()

        def mem_attn_phase(l, qmT_bufs, cat_tiles, col0, smp=None):
            P.begin_phase()
            PS = PSB(f"psm{l}")
            mkT = P.sbuf([128, 4, 256], BF16, "mkT")
            mvA = P.sbuf([128, 2, 4, 129], BF16, "mvA")
            ldf = Pool(P, 2, [128, 512], F32, "ldfm")
            ldb = Pool(P, 2, [128, 512], BF16, "ldbm")
            P.memset(mvA[:, :, :, 128:129], 1.0)
            for mt in range(2):
                lf = ldf.next()
                P.dma(lf[:, :], o_memk[l, mt * 128:(mt + 1) * 128, :])
                lb_ = ldb.next()
                P.copy(lb_[:, :], lf[:, :])
                pt = PS.bf16(mt)
                for h in range(4):
                    P.transpose(pt[:, h * 128:(h + 1) * 128], lb_[:, h * 128:(h + 1) * 128], ident_b[:, :])
                P.copy(mkT[:, :, mt * 128:(mt + 1) * 128], pt[:, 0:512].rearrange("p (a b) -> p a b", a=4))
                lf = ldf.next()
                P.dma(lf[:, :], o_memv[l, mt * 128:(mt + 1) * 128, :])
                P.copy(mvA[:, mt, :, 0:128], lf[:, :].rearrange("p (a b) -> p a b", a=4))
            qm = P.sbuf([128, 4, T], BF16, "qm")
            for h in range(4):
                for pp in range(2):
                    P.dma(qm[:, h, pp * 1024:(pp + 1) * 1024], qmT_bufs[h][pp][:, :])
            PTm = [[P.sbuf([128, 512], BF16, f"PTm{h}{mt}") for mt in range(2)] for h in range(4)]
            omp = Pool(P, 2, [128, 4, 128], BF16, "omt")
            smm = Pool(P, 4, [128, 4], F32, "smm")
            for qg in range(4):
                for h in range(4):
                    for mt in range(2):
                        sp = PS.f32(mt)
                        P.mm(sp[:, :], lhsT=mkT[:, h, mt * 128:(mt + 1) * 128], rhs=qm[:, h, qg * 512:(qg + 1) * 512])
                        P.act(PTm[h][mt][:, :], sp[:, :], AF.Exp)
                for qs in range(4):
                    om_ = omp.next()
                    for h in range(4):
                        O = PS.f32(2 + h)
                        for mt in range(2):
                            P.mm(O[:, 0:129], lhsT=PTm[h][mt][:, qs * 128:(qs + 1) * 128], rhs=mvA[:, mt, h, :], start=(mt == 0), stop=(mt == 1))
                        s_ = smm.next()
                        P.op("vector", "reciprocal", out=s_[:, 0:1], in_=O[:, 128:129])
                        P.ts(om_[:, h, :], O[:, 0:128], s_[:, 0:1], ALU.mult)
                    P.dma(cat_tiles[qg * 4 + qs][:, col0:col0 + 512], om_[:, :, :].rearrange("p a b -> p (a b)"))
            if smp is not None:
                q_src, cat_s_dst = smp
                for mt in range(2):
                    lf = ldf.next()
                    P.dma(lf[:, :], memk_s[l, mt * 128:(mt + 1) * 128, :])
                    lb_ = ldb.next()
                    P.copy(lb_[:, :], lf[:, :])
                    pt = PS.bf16(mt)
                    for h in range(4):
                        P.transpose(pt[:, h * 128:(h + 1) * 128], lb_[:, h * 128:(h + 1) * 128], ident_b[:, :])
                    P.copy(mkT[:, :, mt * 128:(mt + 1) * 128], pt[:, 0:512].rearrange("p (a b) -> p a b", a=4))
                    lf = ldf.next()
                    P.dma(lf[:, :], memv_s[l, mt * 128:(mt + 1) * 128, :])
                    P.copy(mvA[:, mt, :, 0:128], lf[:, :].rearrange("p (a b) -> p a b", a=4))
                qf = ldf.next()
                P.dma(qf[0:4, :], q_src)
                qb_ = ldb.next()
                P.act(qb_[0:4, :], qf[0:4, :], AF.Copy, scale=SC)
                ptq = PS.bf16(0)
                for h in range(4):
                    P.transpose(ptq[:, h * 4:(h + 1) * 4], qb_[0:4, h * 128:(h + 1) * 128], ident_b[0:4, 0:4])
                qmTs = P.sbuf([128, 16], BF16, "qmTs")
                P.copy(qmTs[:, :], ptq[:, 0:16])
                om_ = omp.next()
                for h in range(4):
                    O = PS.f32(2 + h)
                    for mt in range(2):
                        sp = PS.f32(6 + mt)
                        P.mm(sp[:, 0:4], lhsT=mkT[:, h, mt * 128:(mt + 1) * 128], rhs=qmTs[:, h * 4:(h + 1) * 4])
                        P.act(PTm[h][mt][:, 0:4], sp[:, 0:4], AF.Exp)
                    for mt in range(2):
                        P.mm(O[0:4, 0:129], lhsT=PTm[h][mt][:, 0:4], rhs=mvA[:, mt, h, :], start=(mt == 0), stop=(mt == 1))
                    s_ = smm.next()
                    P.op("vector", "reciprocal", out=s_[0:4, 0:1], in_=O[0:4, 128:129])
                    P.ts(om_[0:4, h, :], O[0:4, 0:128], s_[0:4, 0:1], ALU.mult)
                P.dma(cat_s_dst, om_[0:4, :, :].rearrange("p a b -> p (a b)"))
            P.end_phase()

        def out_proj_phase(l, w_out, cat_tiles, res_src, dst_tiles, smp=None):
            P.begin_phase()
            PS = PSB(f"pso{l}")
            catT = P.sbuf([128, 36, 1024], BF16, "catT")
            wpo = Pool(P, 2, [128, 36, 512], BF16, "wpo")
            ldc = Pool(P, 2, [128, 4608], BF16, "ldc")
            xres = Pool(P, 3, [128, 512], F32, "xres")
            if smp is not None:
                cat_s_src, res_s, dst_s = smp
                lc = ldc.next()
                P.dma(lc[0:4, :], cat_s_src[0:4, :])
                catTs = P.sbuf([128, 36, 4], BF16, "catTs")
                for g in range(9):
                    pt = PS.bf16(g % 2)
                    for j in range(4):
                        k = g * 4 + j
                        P.transpose(pt[:, j * 4:(j + 1) * 4], lc[0:4, k * 128:(k + 1) * 128], ident_b[0:4, 0:4])
                    P.copy(catTs[:, g * 4:(g + 1) * 4, :].rearrange("p a b -> p (a b)"), pt[:, 0:16])
            for pp in range(2):
                for tt in range(8):
                    lc = ldc.next()
                    P.dma(lc[:, :], cat_tiles[pp * 8 + tt][:, :])
                    for g in range(9):
                        pt = PS.bf16(g % 2)
                        for j in range(4):
                            k = g * 4 + j
                            P.transpose(pt[:, j * 128:(j + 1) * 128], lc[:, k * 128:(k + 1) * 128], ident_b[:, :])
                        P.copy(catT[:, g * 4:(g + 1) * 4, tt * 128:(tt + 1) * 128], pt[:, 0:512].rearrange("p (a b) -> p a b", a=4), eng="scalar" if g % 2 else "vector")
                for cc in range(8):
                    wt = wpo.next()
                    P.dma(wt[:, :, :], w_out[:, :].rearrange("(k p) c -> p k c", p=128)[:, :, cc * 512:(cc + 1) * 512], eng="gpsimd")
                    if smp is not None and pp == 0:
                        ps = PS.f32(6)
                        for k in range(36):
                            P.mm(ps[0:4, :], lhsT=catTs[:, k, :], rhs=wt[:, k, :], start=(k == 0), stop=(k == 35))
                        xr = xres.next()
                        P.dma(xr[0:4, :], res_s[0:4, cc * 512:(cc + 1) * 512])
                        P.tt(xr[0:4, :], xr[0:4, :], ps[0:4, :], ALU.add)
                        P.dma(dst_s[0:4, cc * 512:(cc + 1) * 512], xr[0:4, :])
                    for tt in range(8):
                        ps = PS.f32(2 + tt % 4)
                        for k in range(36):
                            P.mm(ps[:, :], lhsT=catT[:, k, tt * 128:(tt + 1) * 128], rhs=wt[:, k, :], start=(k == 0), stop=(k == 35))
                        xr = xres.next()
                        P.dma(xr[:, :], res_src(pp * 8 + tt, cc * 512))
                        P.tt(xr[:, :], xr[:, :], ps[:, :], ALU.add)
                        P.dma(dst_tiles[pp * 8 + tt][:, cc * 512:(cc + 1) * 512], xr[:, :])
            P.end_phase()


        if STAGE >= 8:
            sup_ = ExitStack()
            P.stack = sup_
            kcmpTs = P.sbuf([128, 4, 1024], BF16, "kcmpTs")
            vcmpAs = P.sbuf([128, 8, 4, 129], BF16, "vcmpAs")
            offs_i = P.sbuf([128, 128], I32, "offs_i")
            P.stack = P.gstack
            P.begin_phase()
            PS = PSB("pssc1")
            ptb_i = P.sbuf([128, 128], I32, "ptb_i")
            P.dma(ptb_i[:, :], page_tab[0:1, :].partition_broadcast(128))
            ptb_f = P.sbuf([128, 128], F32, "ptb_f")
            P.copy(ptb_f[:, :], ptb_i[:, :])
            iot = P.sbuf([128, 1], F32, "iot")
            P.dma(iot[:, :], c_iota[:, :])
            P.ts(ptb_f[:, :], ptb_f[:, :], 128.0, ALU.mult, iot[:, 0:1], ALU.add)
            P.copy(offs_i[:, :], ptb_f[:, :])
            w1b = P.sbuf([128, 2, 32, 128], BF16, "w1bs")
            P.dma(w1b[:, :, :, :], w_cmp1[:, :, :, :].rearrange("a s d e -> d a s e"), eng="gpsimd")
            w2b = P.sbuf([128, 2, 128], BF16, "w2bs")
            P.dma(w2b[:, :, :], w_cmp2[:, :, :].rearrange("a e d -> e a d"), eng="gpsimd")
            pef = P.sbuf([64, 128], F32, "pefs")
            P.dma(pef[:, :], pe_cmp[:, :])
            peT = P.sbuf([128, 64], BF16, "peTs")
            pp_ = PS.f32(7)
            P.transpose(pp_[:, 0:64], pef[:, :], ident_f[0:64, 0:64])
            P.copy(peT[:, :], pp_[:, 0:64])
            b1f = P.sbuf([128, 2], F32, "b1fs")
            P.dma(b1f[:, :], b_cmp1[:, :].rearrange("a e -> e a"), allow_slow_non_contiguous=True)
            b1p = P.sbuf([128, 2], F32, "b1ps")
            for kv in range(2):
                bp = PS.f32(6)
                for s in range(32):
                    P.mm(bp[:, kv:kv + 1], lhsT=w1b[:, kv, s, :], rhs=peT[:, kv * 32 + s:kv * 32 + s + 1], start=(s == 0), stop=(s == 31))
                P.tt(b1p[:, kv:kv + 1], bp[:, kv:kv + 1], b1f[:, kv:kv + 1], ALU.add)
            RT = P.sbuf([128, 4, 16384], BF16, "RT")
            pgf = Pool(P, 3, [128, 512], F32, "pgf")
            pgb = Pool(P, 2, [128, 512], BF16, "pgb")
            gTs = Pool(P, 2, [128, 512], BF16, "gTs")
            P.memset(vcmpAs[:, :, :, :], 0.0)
            P.memset(kcmpTs[:, :, :], 0.0)
            for kv, pool_ in ((0, pool_ck), (1, pool_cv)):
                for p in range(128):
                    pf = pgf.next()
                    P.idma(pf[:, :], pool_[:, :], offs_i[:, p:p + 1])
                    pb_ = pgb.next()
                    P.copy(pb_[:, :], pf[:, :], eng="vector")
                    pt = PS.bf16(p % 2)
                    for kvh in range(4):
                        P.transpose(pt[:, kvh * 128:(kvh + 1) * 128], pb_[:, kvh * 128:(kvh + 1) * 128], ident_b[:, :])
                    P.copy(RT[:, :, p * 128:(p + 1) * 128], pt[:, 0:512].rearrange("p (a b) -> p a b", a=4), eng="scalar")
                for kvh in range(4):
                    for n0, ncnt in ((0, 512), (512, 511)):
                        hps = PS.f32(2 + (n0 // 512))
                        for s in range(32):
                            st_ = 16 * n0 + s
                            P.mm(hps[:, 0:ncnt], lhsT=w1b[:, kv, s, :], rhs=RT[:, kvh, st_:st_ + 16 * (ncnt - 1) + 1:16], start=(s == 0), stop=(s == 31))
                        g_ = gTs.next()
                        P.act(g_[:, 0:ncnt], hps[:, 0:ncnt], AF.Gelu_apprx_tanh, bias=b1p[:, kv:kv + 1])
                        if kv == 0:
                            p2 = PS.f32(4 + (n0 // 512))
                            P.mm(p2[:, 0:ncnt], lhsT=w2b[:, 0, :], rhs=g_[:, 0:ncnt])
                            P.copy(kcmpTs[:, kvh, n0:n0 + ncnt], p2[:, 0:ncnt])
                        else:
                            for blk in range(4):
                                nb = min(128, ncnt - blk * 128)
                                p2 = PS.f32(4 + blk % 2)
                                P.mm(p2[0:nb, 0:128], lhsT=g_[:, blk * 128:blk * 128 + nb], rhs=w2b[:, 1, :])
                                P.copy(vcmpAs[0:nb, n0 // 128 + blk, kvh, 0:128], p2[0:nb, 0:128])
            P.memset(vcmpAs[:, :, :, 128:129], 1.0)
            P.end_phase()

            P.begin_phase()
            PS = PSB("pssc2")
            def sb_src(off, pstep, np_):
                return V(s_bias, bass.AP(s_bias_h.ap().tensor, off, [[pstep, np_], [4400, 16], [1, 4]]))
            cb_all = P.sbuf([128, 16, 4], BF16, "cb_all"); P.dma(cb_all[:, :, :], sb_src(2300, 0, 128))
            tz_all = P.sbuf([128, 16, 4], BF16, "tz_all"); P.dma(tz_all[:, :, :], sb_src(2048, 1, 128))
            cbl = P.sbuf([128, 16, 4], BF16, "cbl"); P.dma(cbl[0:127, :, :], sb_src(2048, 16, 127))
            nb4 = P.sbuf([4, 16, 4], BF16, "nb4"); P.dma(nb4[:, :, :], sb_src(2044, 1, 4))
            w0b = P.sbuf([128, 16, 4], BF16, "w0b"); P.dma(w0b[:, :, :], sb_src(4096, 1, 128))
            q4f = P.sbuf([4, 2048], F32, "q4f")
            P.dma(q4f[:, :], s_ps0[0:4, 8192:10240])
            q4b = P.sbuf([4, 2048], BF16, "q4b")
            P.act(q4b[:, :], q4f[:, :], AF.Copy, scale=SC)
            qTs = P.sbuf([128, 16, 4], BF16, "qTs4")
            ptq = PS.bf16(7)
            for h in range(16):
                P.transpose(ptq[:, h * 4:(h + 1) * 4], q4b[:, h * 128:(h + 1) * 128], ident_b[0:4, 0:4])
            P.copy(qTs[:, :, :].rearrange("p a b -> p (a b)"), ptq[:, 0:64])
            wcsf = P.sbuf([128, 8, 257], F32, "wcsf"); P.dma(wcsf[:, :, :], c_wcs[:, :, :])
            wcsb = P.sbuf([128, 8, 257], BF16, "wcsb"); P.copy(wcsb[:, :, :], wcsf[:, :, :])
            forced_s = P.sbuf([4, 257], F32, "forced_s"); P.dma(forced_s[:, :], c_forced_s[:, :])
            sumg = P.sbuf([16, 4], F32, "sumg"); P.dma(sumg[:, :], c_sumg[:, :])
            hmf = P.sbuf([2, 128], F32, "hmf"); P.dma(hmf[:, :], c_hm[:, :])
            hmb = P.sbuf([2, 128], BF16, "hmb"); P.copy(hmb[:, :], hmf[:, :])
            g4 = P.sbuf([4, 48], F32, "g4")
            P.dma(g4[:, :], s_ps0[0:4, 13312:13360])
            bg4 = P.sbuf([4, 48], F32, "bg4")
            P.dma(bg4[:, :], b_gate[0:1, :].partition_broadcast(4))
            P.tt(g4[:, :], g4[:, :], bg4[:, :], ALU.add)
            P.act(g4[:, :], g4[:, :], AF.Sigmoid)
            P.dma(s_gs[:, :], g4[:, :])
            gsr = P.sbuf([16, 3, 4], F32, "gsr")
            zbr = P.sbuf([16, 4, 128], F32, "zbr")
            for g in range(4):
                P.dma(gsr[g * 4:(g + 1) * 4, :, :], V(s_gs, bass.AP(s_gs.h.tensor, g, [[48, 4], [16, 3], [4, 4]])), allow_slow_non_contiguous=True)
                P.dma(zbr[g * 4:(g + 1) * 4, :, :], V(s_ps0, bass.AP(s_ps0.h.tensor, 13360 + g * 128, [[EVEN_IN, 4], [512, 4], [1, 128]])))
            P.act(zbr[:, :, :], zbr[:, :, :], AF.Silu)
            acc = P.sbuf([16, 4, 128], F32, "acc16")
            sms = Pool(P, 6, [16, 8], F32, "sms")
            PTs = Pool(P, 3, [128, 64], BF16, "PTs")
            pn = P.sbuf([16, 257], F32, "pn")
            sc1 = P.sbuf([4, 258], F32, "sc1s")
            sc2 = P.sbuf([4, 258], F32, "sc2s")
            m8 = P.sbuf([4, 8], F32, "m8s")

            for kvh in range(4):
                O1 = PS.f32(2)[0:16, 0:129]
                O2 = PS.f32(3)[0:16, 0:257]
                qk = qTs[:, kvh * 4:(kvh + 1) * 4, :]
                for nt in range(8):
                    rows = 127 if nt == 7 else 128
                    sp = PS.f32(nt % 2)[0:rows, 0:16]
                    P.mm(sp, lhsT=kcmpTs[:, kvh, nt * 128:nt * 128 + rows], rhs=qk, start=True, stop=False)
                    if nt == 7:
                        P.mm(sp, lhsT=J127b[0:127, 0:127], rhs=cbl[0:127, kvh * 4:(kvh + 1) * 4, :], start=False, stop=True)
                    else:
                        P.mm(sp, lhsT=ident_b[:, :], rhs=cb_all[:, kvh * 4:(kvh + 1) * 4, :], start=False, stop=True)
                    pt_ = PTs.next()
                    P.act(pt_[0:rows, 0:16], sp, AF.Exp)
                    P.mm(O1, lhsT=pt_[0:rows, 0:16], rhs=vcmpAs[0:rows, nt, kvh, :], start=(nt == 0), stop=(nt == 7))
                    P.mm(O2, lhsT=pt_[0:rows, 0:16], rhs=wcsb[0:rows, nt, :], start=(nt == 0), stop=(nt == 7))
                s_ = sms.next()
                P.op("vector", "reciprocal", out=s_[:, 0:1], in_=O1[:, 128:129])
                P.tt(s_[:, 1:2], s_[:, 0:1], gsr[:, 0, kvh:kvh + 1], ALU.mult)
                P.ts(acc[:, kvh, :], O1[:, 0:128], s_[:, 1:2], ALU.mult)
                P.ts(pn[:, :], O2, s_[:, 0:1], ALU.mult)
                psl = PS.f32(6)[0:4, 0:257]
                P.mm(psl, lhsT=sumg[:, :], rhs=pn[:, :])
                P.memset(sc1[:, 256:258], -1e30)
                P.tt(sc1[:, 0:257], psl, forced_s[:, :], ALU.add)
                P.op("vector", "max", out=m8[:, :], in_=sc1[:, :])
                P.op("vector", "match_replace", out=sc2[:, :], in_to_replace=m8[:, :], in_values=sc1[:, :], imm_value=-1e30)
                P.op("vector", "max", out=m8[:, :], in_=sc2[:, :])
                P.ts(sc2[:, :], sc1[:, :], m8[:, 7:8], ALU.is_ge)
                P.ts(sc2[:, :], sc2[:, :], -NEG, ALU.mult, NEG, ALU.add)
                P.dma(s_nm[kvh, :, :], sc2[:, :])
            nmPf = P.sbuf([2, 4, 4, 129], F32, "nmPf")
            for kvh in range(4):
                for t_ in range(4):
                    P.dma(nmPf[:, kvh, t_, :], V(s_nm, bass.AP(s_nm.h.tensor, (kvh * 4 + t_) * 258, [[1, 2], [2, 129]])), allow_slow_non_contiguous=True)
            nmP = P.sbuf([2, 4, 4, 129], BF16, "nmP")
            P.copy(nmP[:, :, :, :], nmPf[:, :, :, :])

            kTp = Pool(P, 2, [128, 4, 128], BF16, "kTp")
            vAp = Pool(P, 2, [128, 4, 129], BF16, "vAp")
            for b_ in vAp.bufs:
                P.memset(b_[:, :, 128:129], 1.0)
            pgf = Pool(P, 4, [128, 512], F32, "pgf2")
            pgb = Pool(P, 2, [128, 512], BF16, "pgb2")

            def attend(tiles, Obanks, br):
                nt_ = len(tiles)
                for ti, tl in enumerate(tiles):
                    rows = tl["rows"]
                    kf = pgf.next(); tl["kload"](kf)
                    vf = pgf.next(); tl["vload"](vf)
                    kb = pgb.next()
                    P.copy(kb[0:rows, :], kf[0:rows, :], eng="vector")
                    ptk = PS.bf16(6 + ti % 2)
                    for kvh in range(4):
                        P.transpose(ptk[:, kvh * 128:kvh * 128 + rows], kb[0:rows, kvh * 128:(kvh + 1) * 128], ident_b[0:rows, 0:rows])
                    kT_ = kTp.next()
                    P.copy(kT_[:, :, 0:rows], ptk[:, 0:512].rearrange("p (a b) -> p a b", a=4)[:, :, 0:rows], eng="scalar")
                    vA_ = vAp.next()
                    P.copy(vA_[0:rows, :, 0:128], vf[0:rows, :].rearrange("p (a b) -> p a b", a=4))
                    sp = PS.f32(ti % 2)
                    blhs, brhs = tl["bias"]
                    for kvh in range(4):
                        spk = sp[0:rows, kvh * 16:(kvh + 1) * 16]
                        P.mm(spk, lhsT=kT_[:, kvh, 0:rows], rhs=qTs[:, kvh * 4:(kvh + 1) * 4, :], start=True, stop=False)
                        has_m = tl.get("mask") is not None
                        P.mm(spk, lhsT=blhs, rhs=brhs[0:rows, kvh * 4:(kvh + 1) * 4, :], start=False, stop=not has_m)
                        if has_m:
                            P.mm(spk, lhsT=hmb[:, :], rhs=nmP[:, kvh, :, tl["mask"]].unsqueeze(1).to_broadcast([2, 4, 4]), start=False, stop=True)
                    pt_ = PTs.next()
                    P.act(pt_[0:rows, :], sp[0:rows, 0:64], AF.Exp)
                    for kvh in range(4):
                        Ob = Obanks[0][0:16, kvh * 129:(kvh + 1) * 129] if kvh < 3 else Obanks[1][0:16, 0:129]
                        P.mm(Ob, lhsT=pt_[0:rows, kvh * 16:(kvh + 1) * 16], rhs=vA_[0:rows, kvh, :], start=(ti == 0 and kvh in (0, 3)), stop=(ti == nt_ - 1), skip_group_check=True)
                for kvh in range(4):
                    Ob = Obanks[0][0:16, kvh * 129:(kvh + 1) * 129] if kvh < 3 else Obanks[1][0:16, 0:129]
                    s_ = sms.next()
                    P.op("vector", "reciprocal", out=s_[:, 0:1], in_=Ob[:, 128:129])
                    P.tt(s_[:, 1:2], s_[:, 0:1], gsr[:, br, kvh:kvh + 1], ALU.mult)
                    P.stt(acc[:, kvh, :], Ob[:, 0:128], s_[:, 1:2], acc[:, kvh, :], ALU.mult, ALU.add)

            def page_loader(pool_, p):
                return lambda dst: P.idma(dst[:, :], pool_[:, :], offs_i[:, p:p + 1])
            def new_loader(j):
                return lambda dst: P.dma(dst[0:4, :], s_ps0[0:4, 10240 + j * 512:10240 + (j + 1) * 512])
            def win_loader(src_, i):
                return lambda dst: P.dma(dst[:, :], src_[i * 128:(i + 1) * 128, :])

            J4 = Jb[0:4, 124:128]
            sel_tiles = []
            for p in range(128):
                sel_tiles.append(dict(kload=page_loader(pool_sk, p), vload=page_loader(pool_sv, p), rows=128,
                                      bias=((Jb[:, :], tz_all) if p == 127 else (ident_b[:, :], cb_all)), mask=p))
            sel_tiles.append(dict(kload=new_loader(2), vload=new_loader(3), rows=4, bias=(J4, nb4), mask=None))
            attend(sel_tiles, [PS.f32(2), PS.f32(3)], 1)
            win_tiles = []
            for i in range(4):
                bias = (Jb[:, :], w0b) if i == 0 else ((Jb[:, :], tz_all) if i == 3 else (ident_b[:, :], cb_all))
                win_tiles.append(dict(kload=win_loader(win_k_in, i), vload=win_loader(win_v_in, i), rows=128, bias=bias, mask=None))
            win_tiles.append(dict(kload=new_loader(4), vload=new_loader(5), rows=4, bias=(J4, nb4), mask=None))
            attend(win_tiles, [PS.f32(4), PS.f32(5)], 2)
            ob16 = P.sbuf([16, 4, 128], BF16, "ob16")
            P.tt(ob16[:, :, :], acc[:, :, :], zbr[:, :, :], ALU.mult)
            for g in range(4):
                P.dma(V(s_cat_s, bass.AP(s_cat_s.h.tensor, 2048 + g * 128, [[4608, 4], [512, 4], [1, 128]])), ob16[g * 4:(g + 1) * 4, :, :])
            P.end_phase()
            sup_.close()

        if STAGE >= 4:
            SMP = STAGE >= 8
            mem_attn_phase(0, s_qmT, s_cat, 4096, smp=(s_ps0[0:4, 15408:15920], s_cat_s[0:4, 4096:4608]) if SMP else None)
            out_proj_phase(0, w_out_even, s_cat, lambda i, c0: x_p[i * 128:(i + 1) * 128, c0:c0 + 512], s_h1,
                           smp=(s_cat_s, x_s, s_h1s) if SMP else None)

        if STAGE >= 5:
            E = proj_env()
            wbc, hnT, stage, stage_b = E.wbc, E.hnT, E.stage, E.stage_b
            P.dma(wbc[:, :], norm_w[1:2, :].partition_broadcast(128))
            if STAGE >= 8:
                E.rmsnorm_to_T(s_h1s[0:4, :], E.hnTs, 0, nrows=4)
            for pp in range(2):
                E.after_load = E.sample_proj(s_ps1) if (pp == 0 and STAGE >= 8) else None
                for tt in range(8):
                    E.rmsnorm_to_T(s_h1[pp * 8 + tt][:, :], hnT, tt * 128)
                def feat_sink(dst_bufs, cbase, scale):
                    def f(cb, hf, ps):
                        sg = stage_b.next()
                        P.act(sg[:, :], ps[:, :], AF.Copy, scale=scale)
                        P.dma(dst_bufs[cbase + cb][pp][:, hf * 512:(hf + 1) * 512], sg[:, :])
                    return f
                def tok_sink(dst_tiles, cbase, bf, scale=1.0):
                    def f(tt, ps):
                        sg = (stage_b if bf else stage).next()
                        if scale != 1.0 or tt % 2:
                            P.act(sg[:, :], ps[:, :], AF.Copy, scale=scale)
                        else:
                            P.copy(sg[:, :], ps[:, :])
                        P.dma(dst_tiles[pp * 8 + tt][:, cbase:cbase + 512], sg[:, :])
                    return f
                for ci in range(4):
                    wt = E.load_w(w_in_odd, ci * 512, 512)
                    E.proj_feat(wt, 4, 2, feat_sink(s_q1T, ci * 4, 1.0))
                for ci in range(4):
                    wt = E.load_w(w_in_odd, 2048 + ci * 512, 512)
                    E.proj_tok(wt, 512, 8, tok_sink(s_k1, ci * 512, True, 1.0 / 16))
                for ci in range(8):
                    wt = E.load_w(w_in_odd, 4096 + ci * 512, 512)
                    E.proj_tok(wt, 512, 8, tok_sink(s_v1, ci * 512, True))
                for ci in range(8):
                    wt = E.load_w(w_in_odd, 8192 + ci * 512, 512)
                    E.proj_tok(wt, 512, 8, tok_sink(s_og, ci * 512, False))
                wt = E.load_w(w_in_odd, 12288, 16)
                def sink_if(tt, ps):
                    sg = stage.next()
                    P.copy(sg[:, :16], ps[:, :16])
                    t0 = pp * 1024 + tt * 128
                    P.dma(s_if[t0:t0 + 128, :], sg[:, :16])
                E.proj_tok(wt, 16, 8, sink_if)
                for ci in range(8):
                    wt = E.load_w(w_in_odd, 12304 + ci * 512, 512)
                    E.proj_tok(wt, 512, 8, tok_sink(s_z1, ci * 512, False))
                wt = E.load_w(w_in_odd, 16400, 512)
                E.proj_feat(wt, 4, 2, feat_sink(s_qm1T, 0, SC))
            P.end_phase()

        if STAGE >= 6:
            P.begin_phase()
            PS = PSB("psf")
            NCH = T // 64
            tri64 = P.sbuf([64, 64], F32, "tri64"); P.dma(tri64[:, :], c_tri64[:, :])
            sellast = P.sbuf([64, 128], F32, "sellast"); P.dma(sellast[:, :], c_sellast[:, :])
            cmask = P.sbuf([64, 512], F32, "cmask"); P.dma(cmask[:, :], c_cmask[:, :])
            cmaskT = P.sbuf([64, 512], F32, "cmaskT"); P.dma(cmaskT[:, :], c_cmaskT[:, :])
            ones_f = P.sbuf([64, 128], F32, "ones_f"); P.memset(ones_f[:, :], 1.0)
            ones_b = P.sbuf([64, 1], BF16, "ones_b"); P.memset(ones_b[:, :], 1.0)
            id64 = ident_f[0:64, 0:64]
            gts = P.sbuf([64, NCH, 16], F32, "gts")
            P.dma(gts[:, :, :], s_if[:, :].rearrange("(c t) g -> t c g", t=64))
            bifb = P.sbuf([64, 16], F32, "bifb")
            P.dma(bifb[:, :], b_if[:, :].rearrange("a h -> (a h)").unsqueeze(0).partition_broadcast(64) if False else V(b_if, bass.AP(b_if.h.tensor, 0, [[0, 64], [1, 16]])))
            P.tt(gts[:, :, :], gts[:, :, :], bifb[:, :].unsqueeze(1).to_broadcast([64, NCH, 16]), ALU.add)
            logf = P.sbuf([64, NCH, 8], F32, "logf")
            P.act(logf[:, :, :], gts[:, :, 8:16], AF.Exp, scale=-1.0)
            P.act(logf[:, :, :], logf[:, :, :], AF.Ln, bias=1.0)
            P.ts(logf[:, :, :], logf[:, :, :], -1.0, ALU.mult)
            bcs = P.sbuf([64, NCH, 8], F32, "bcs")
            pb = PS.f32(0)
            P.mm(pb[0:64, 0:NCH * 8], lhsT=tri64[:, :], rhs=logf[:, :, :].rearrange("p a b -> p (a b)"))
            P.copy(bcs[:, :, :].rearrange("p a b -> p (a b)"), pb[0:64, 0:NCH * 8])
            av = P.sbuf([64, NCH, 8], F32, "av")
            P.tt(av[:, :, :], gts[:, :, 0:8], bcs[:, :, :], ALU.subtract)
            Blb = P.sbuf([128, NCH, 8], F32, "Blb")
            pb2 = PS.f32(1)
            P.mm(pb2[:, 0:NCH * 8], lhsT=sellast[:, :], rhs=bcs[:, :, :].rearrange("p a b -> p (a b)"))
            P.copy(Blb[:, :, :].rearrange("p a b -> p (a b)"), pb2[:, 0:NCH * 8])
            CT = P.sbuf([128, 8, 2, 512], F32, "CT")
            CTb = P.sbuf([128, 8, 2, 512], BF16, "CTb")
            nT = P.sbuf([128, 8, 2], F32, "nT")
            nTb = P.sbuf([128, 8, 2], BF16, "nTb")
            mprev = P.sbuf([128, 8], F32, "mprev")
            for t_ in (CT, CTb):
                P.memset(t_[:, :, :, :], 0.0)
            P.memset(nT[:, :, :], 0.0); P.memset(nTb[:, :, :], 0.0); P.memset(mprev[:, :], 0.0)
            nwb = P.sbuf([64, 4096], F32, "nwb")
            P.dma(nwb[:, :], ml_nw[0:1, :].partition_broadcast(64))
            qTp = Pool(P, 2, [128, 16, 128], BF16, "q1Tt")
            vp_ = Pool(P, 2, [64, 4096], BF16, "vv1")
            kp_ = Pool(P, 2, [64, 2048], BF16, "kk1")
            ogp = Pool(P, 1, [64, 4096], F32, "ogt")
            zp = ogp
            hob = Pool(P, 1, [64, 4096], BF16, "hob")
            kT = P.sbuf([128, 16, 64], BF16, "kT1")
            dg = Pool(P, 2, [64, 8, 64], F32, "dg")
            Dm = P.sbuf([64, 8, 64], F32, "Dm")
            winT = P.sbuf([64, 8, 64], F32, "winT")
            sf = Pool(P, 4, [128, 64], F32, "sf")
            qT = None
            hsP = Pool(P, 2, [64, 8, 512], F32, "hsP")
            qTsP = Pool(P, 2, [128, 16, 64], BF16, "qTsP")
            swTP = Pool(P, 2, [64, 8, 64], BF16, "swTP")
            kwP = Pool(P, 2, [64, 8, 256], BF16, "kwP")
            rmx = P.sbuf([64, NCH, 8], F32, "rmx")
            mtA = P.sbuf([64, NCH, 8], F32, "mtA")
            mpA = P.sbuf([128, NCH + 1, 8], F32, "mpA")
            wxA = P.sbuf([64, NCH, 8], F32, "wxA")
            c2A = P.sbuf([64, NCH, 8], F32, "c2A")
            emtA = P.sbuf([64, NCH, 8], F32, "emtA")
            BmL = P.sbuf([128, NCH, 8], F32, "BmL")
            wendA = P.sbuf([64, NCH, 8], F32, "wendA")
            dCA = P.sbuf([128, NCH, 8], F32, "dCA")

            def mlstm_run(L, nch, bT, aT, BlT, sell_, cm_, cmT_, load_fn, og_src_fn, z_src_fn, cat_dst_fn):
                idL = ident_f[0:L, 0:L]
                RL = slice(0, L)
                for c in range(nch):
                    d1 = dg.next(); d1c = d1[:, :, :].rearrange("p a b -> p (a b)")
                    P.tt(d1c[RL, 0:8 * L].rearrange("p (a b) -> p a b", a=8), idL.unsqueeze(1).to_broadcast([L, 8, L]), aT[:, c, :].unsqueeze(2).to_broadcast([L, 8, L]), ALU.mult)
                    pA = PS.f32(c % 2)
                    P.mm(pA[RL, 0:8 * L], lhsT=ones_f[RL, RL], rhs=d1c[RL, 0:8 * L], start=True, stop=False)
                    P.mm(pA[RL, 0:8 * L], lhsT=idL, rhs=cm_, start=False, stop=True)
                    P.tt(Dm[RL, :, RL], pA[RL, 0:8 * L].rearrange("p (a b) -> p a b", a=8), bT[:, c, :].unsqueeze(2).to_broadcast([L, 8, L]), ALU.add)
                    P.op("vector", "tensor_reduce", out=rmx[RL, c, :], in_=Dm[RL, :, RL], axis=AX.X, op=ALU.max)
                for c in range(nch):
                    s_ = sf.next()
                    P.tt(s_[RL, 0:8], bT[:, c, :], mpA[RL, c, :], ALU.add)
                    P.tt(mtA[RL, c, :], rmx[RL, c, :], s_[RL, 0:8], ALU.max)
                    pM = PS.f32(2 + c % 2)
                    P.mm(pM[:, 0:8], lhsT=sell_, rhs=mtA[RL, c, :])
                    P.copy(mpA[:, c + 1, :], pM[:, 0:8])
                C_ = slice(0, nch)
                P.tt(wxA[RL, C_, :], bT, mpA[RL, 0:nch, :], ALU.add)
                P.tt(wxA[RL, C_, :], wxA[RL, C_, :], mtA[RL, C_, :], ALU.subtract)
                P.act(wxA[RL, C_, :], wxA[RL, C_, :], AF.Exp)
                P.tt(c2A[RL, C_, :], bT, mtA[RL, C_, :], ALU.subtract)
                P.act(emtA[RL, C_, :], mtA[RL, C_, :], AF.Exp, scale=-1.0)
                P.tt(BmL[:, C_, :], BlT, mpA[:, 1:nch + 1, :], ALU.subtract)
                P.tt(wendA[RL, C_, :], aT, BmL[RL, C_, :], ALU.add)
                P.act(wendA[RL, C_, :], wendA[RL, C_, :], AF.Exp)
                P.tt(dCA[:, C_, :], BmL[:, C_, :], mpA[:, 0:nch, :], ALU.add)
                P.act(dCA[:, C_, :], dCA[:, C_, :], AF.Exp)

                def stage1(c):
                    qc, vv, kk = load_fn(c)
                    ptk = PS.bf16(7)
                    for blk in range(16):
                        P.transpose(ptk[:, blk * 64:blk * 64 + L], kk[:, blk * 128:(blk + 1) * 128], ident_b[RL, RL])
                    P.copy(kT[:, :, RL], ptk[:, 0:1024].rearrange("p (a b) -> p a b", a=16)[:, :, RL])
                    d2 = dg.next(); d2c = d2[:, :, :].rearrange("p a b -> p (a b)")
                    P.tt(d2c[RL, 0:8 * L].rearrange("p (a b) -> p a b", a=8), idL.unsqueeze(1).to_broadcast([L, 8, L]), c2A[RL, c, :].unsqueeze(2).to_broadcast([L, 8, L]), ALU.mult)
                    pC = PS.f32(1)
                    P.mm(pC[RL, 0:8 * L], lhsT=ones_f[RL, RL], rhs=d2c[RL, 0:8 * L], start=True, stop=False)
                    P.mm(pC[RL, 0:8 * L], lhsT=idL, rhs=cmT_, start=False, stop=True)
                    P.tt(winT[RL, :, RL], pC[RL, 0:8 * L].rearrange("p (a b) -> p a b", a=8), aT[:, c, :].unsqueeze(2).to_broadcast([L, 8, L]), ALU.add)
                    P.act(winT[RL, :, RL], winT[RL, :, RL], AF.Exp)
                    d3 = dg.next(); d3c = d3[:, :, :].rearrange("p a b -> p (a b)")
                    P.tt(d3c[RL, 0:8 * L].rearrange("p (a b) -> p a b", a=8), idL.unsqueeze(1).to_broadcast([L, 8, L]), wxA[RL, c, :].unsqueeze(2).to_broadcast([L, 8, L]), ALU.mult)
                    pW = PS.f32(2)
                    P.mm(pW[:, 0:8 * L], lhsT=ones_f[RL, :], rhs=d3c[RL, 0:8 * L])
                    qTs_ = qTsP.next()
                    for kc in range(2):
                        P.tt(qTs_[:, kc::2, RL], qc[:, kc::2, :], pW[:, 0:8 * L].rearrange("p (a b) -> p a b", a=8), ALU.mult)
                    pS = PS.f32(3)
                    for h in range(8):
                        for kc in range(2):
                            P.mm(pS[RL, h * L:(h + 1) * L], lhsT=kT[:, h * 2 + kc, RL], rhs=qc[:, h * 2 + kc, :], start=(kc == 0), stop=(kc == 1))
                    swT_ = swTP.next()
                    P.tt(swT_[RL, :, RL], pS[RL, 0:8 * L].rearrange("p (a b) -> p a b", a=8), winT[RL, :, RL], ALU.mult)
                    kw__ = kwP.next()
                    P.tt(kw__[RL, :, :], kk.rearrange("p (a b) -> p a b", a=8), wendA[RL, c, :].unsqueeze(2).to_broadcast([L, 8, 256]), ALU.mult, eng="gpsimd")
                    return (qTs_, swT_, kw__, vv)

                def stage2(c, st):
                    qTs_, swT_, kw__, vv = st
                    s_ = sf.next()
                    pD = PS.f32(0)
                    for h in range(8):
                        P.mm(pD[RL, h:h + 1], lhsT=swT_[RL, h, RL], rhs=ones_b[RL, :], start=True, stop=False)
                        for kc in range(2):
                            P.mm(pD[RL, h:h + 1], lhsT=qTs_[:, h * 2 + kc, RL], rhs=nTb[:, h, kc:kc + 1], start=False, stop=(kc == 1))
                    P.act(s_[RL, 40:48], pD[RL, 0:8], AF.Abs)
                    P.tt(s_[RL, 40:48], s_[RL, 40:48], emtA[RL, c, :], ALU.max)
                    P.op("vector", "reciprocal", out=s_[RL, 56:64], in_=s_[RL, 40:48])
                    hs_ = hsP.next()
                    for h in range(8):
                        pN = PS.f32(4 + h % 2)
                        P.mm(pN[RL, :], lhsT=swT_[RL, h, RL], rhs=vv[:, h * 512:(h + 1) * 512], start=True, stop=False)
                        for kc in range(2):
                            P.mm(pN[RL, :], lhsT=qTs_[:, h * 2 + kc, RL], rhs=CTb[:, h, kc, :], start=False, stop=(kc == 1))
                        P.act(hs_[RL, h, :], pN[RL, :], AF.Copy, scale=s_[RL, 56 + h:57 + h])
                    for h in range(8):
                        for kc in range(2):
                            pU = PS.f32(4 + (h * 2 + kc) % 2)
                            P.mm(pU[:, :], lhsT=kw__[RL, h, kc * 128:(kc + 1) * 128], rhs=vv[:, h * 512:(h + 1) * 512])
                            P.stt(CT[:, h, kc, :], CT[:, h, kc, :], dCA[:, c, h:h + 1], pU[:, :], ALU.mult, ALU.add)
                            P.copy(CTb[:, h, kc, :], CT[:, h, kc, :], eng="scalar")
                    pn = PS.f32(6)
                    for h in range(8):
                        for kc in range(2):
                            P.mm(pn[:, 16 + h * 2 + kc:17 + h * 2 + kc], lhsT=kw__[RL, h, kc * 128:(kc + 1) * 128], rhs=ones_b[RL, :])
                    P.tt(nT[:, :, :], nT[:, :, :], dCA[:, c, :].unsqueeze(2).to_broadcast([128, 8, 2]), ALU.mult)
                    P.tt(nT[:, :, :], nT[:, :, :], pn[:, 16:32].rearrange("p (a b) -> p a b", a=8), ALU.add)
                    P.copy(nTb[:, :, :], nT[:, :, :])
                    P.dma(cat_dst_fn(c), hs_[RL, :, :].rearrange("p a b -> p (a b)"))

                st = stage1(0)
                for c in range(nch):
                    nxt = stage1(c + 1) if c + 1 < nch else None
                    stage2(c, st)
                    st = nxt
                P.copy(mprev[:, :], mpA[:, nch, :])

            qT_state = {"qT": None}
            def load_prompt(c):
                t0 = c * 64
                if c % 2 == 0:
                    qT_state["qT"] = qTp.next()
                    pp, off = divmod(t0, 1024)
                    for blk in range(16):
                        P.dma(qT_state["qT"][:, blk, :], s_q1T[blk][pp][:, off:off + 128])
                qc = qT_state["qT"][:, :, (c % 2) * 64:(c % 2 + 1) * 64]
                vt = vp_.next()
                kt_ = kp_.next()
                ti, r0 = divmod(t0, 128)
                P.dma(vt[:, :], s_v1[ti][r0:r0 + 64, :])
                P.dma(kt_[:, :], s_k1[ti][r0:r0 + 64, :])
                return qc, vt[:, :], kt_[:, :]
            def tr_(c):
                ti, r0 = divmod(c * 64, 128)
                return ti, r0
            P.memset(mpA[:, 0, :], 0.0)
            mlstm_run(64, NCH, bcs[:, :, :], av[:, :, :], Blb[:, :, :], sellast[:, :], cmask[:, :], cmaskT[:, :], load_prompt,
                      lambda c: s_og[tr_(c)[0]][tr_(c)[1]:tr_(c)[1] + 64, :], lambda c: s_z1[tr_(c)[0]][tr_(c)[1]:tr_(c)[1] + 64, :],
                      lambda c: s_hraw[tr_(c)[0]][tr_(c)[1]:tr_(c)[1] + 64, :])
            trp = Pool(P, 1, [128, 4, 128], F32, "trp")
            def emit_states(o_mc_, o_mn_, o_mm_):
                for h in range(8):
                    for kc in range(2):
                        pt = PS.f32(kc)
                        for dvb in range(4):
                            P.transpose(pt[:, dvb * 128:(dvb + 1) * 128], CT[:, h, kc, dvb * 128:(dvb + 1) * 128], ident_f[:, :])
                        tr = trp.next()
                        P.copy(tr[:, :, :].rearrange("p a b -> p (a b)"), pt[:, :])
                        outs.append(P.dma(o_mc_[h, :, kc * 128:(kc + 1) * 128].rearrange("(a p) k -> p a k", p=128), tr[:, :, :]))
                outs.append(P.dma(o_mn_[:, :].rearrange("h (c p) -> p h c", p=128), nT[:, :, :], allow_slow_non_contiguous=True))
                outs.append(P.dma(o_mm_[0:1, :], mprev[0:1, :]))
            emit_states(o_mc, o_mn, o_mm)
            if STAGE >= 8:
                sell4 = P.sbuf([4, 128], F32, "sell4"); P.dma(sell4[:, :], c_sellast4[:, :])
                cm4 = P.sbuf([4, 32], F32, "cm4"); P.dma(cm4[:, :], c_cmask4[:, :])
                cmT4 = P.sbuf([4, 32], F32, "cmT4"); P.dma(cmT4[:, :], c_cmaskT4[:, :])
                g4s = P.sbuf([4, 16], F32, "g4s")
                P.dma(g4s[:, :], s_ps1[0:4, 12288:12304])
                P.tt(g4s[:, :], g4s[:, :], bifb[0:4, :], ALU.add)
                lf4 = P.sbuf([4, 8], F32, "lf4")
                P.act(lf4[:, :], g4s[:, 8:16], AF.Exp, scale=-1.0)
                P.act(lf4[:, :], lf4[:, :], AF.Ln, bias=1.0)
                P.ts(lf4[:, :], lf4[:, :], -1.0, ALU.mult)
                b4 = P.sbuf([4, 8], F32, "b4")
                pb = PS.f32(0)
                P.mm(pb[0:4, 0:8], lhsT=tri64[0:4, 0:4], rhs=lf4[:, :])
                P.copy(b4[:, :], pb[0:4, 0:8])
                a4 = P.sbuf([4, 8], F32, "a4")
                P.tt(a4[:, :], g4s[:, 0:8], b4[:, :], ALU.subtract)
                Bl4 = P.sbuf([128, 8], F32, "Bl4")
                pb2 = PS.f32(1)
                P.mm(pb2[:, 0:8], lhsT=sell4[:, :], rhs=b4[:, :])
                P.copy(Bl4[:, :], pb2[:, 0:8])
                cin = Pool(P, 2, [128, 256], F32, "cin")
                for h in range(8):
                    for dvb in range(4):
                        ci_ = cin.next()
                        P.dma(ci_[:, :], st_c[h, dvb * 128:(dvb + 1) * 128, :])
                        pt = PS.f32(2 + dvb % 2)
                        for kc in range(2):
                            P.transpose(pt[:, kc * 128:(kc + 1) * 128], ci_[:, kc * 128:(kc + 1) * 128], ident_f[:, :])
                        P.copy(CT[:, h, :, dvb * 128:(dvb + 1) * 128], pt[:, 0:256].rearrange("p (a b) -> p a b", a=2))
                P.copy(CTb[:, :, :, :], CT[:, :, :, :])
                P.dma(nT[:, :, :], st_n[:, :].rearrange("h (c p) -> p h c", p=128), allow_slow_non_contiguous=True)
                P.copy(nTb[:, :, :], nT[:, :, :])
                P.dma(mprev[:, :], st_m[0:1, :].partition_broadcast(128))
                vt4 = vp_.next()
                kt4 = kp_.next()
                qkf = ogp.next()
                P.dma(qkf[0:4, :], s_ps1[0:4, 0:4096])
                P.act(kt4[0:4, :], qkf[0:4, 2048:4096], AF.Copy, scale=1.0 / 16)
                q4b = hob.next()
                P.copy(q4b[0:4, 0:2048], qkf[0:4, 0:2048])
                ptq = PS.bf16(7)
                for blk in range(16):
                    P.transpose(ptq[:, blk * 4:(blk + 1) * 4], q4b[0:4, blk * 128:(blk + 1) * 128], ident_b[0:4, 0:4])
                q4T = P.sbuf([128, 16, 4], BF16, "q4T")
                P.copy(q4T[:, :, :].rearrange("p a b -> p (a b)"), ptq[:, 0:64])
                vf4 = ogp.next()
                P.dma(vf4[0:4, :], s_ps1[0:4, 4096:8192])
                P.copy(vt4[0:4, :], vf4[0:4, :])
                P.copy(mpA[:, 0, :], mprev[:, :])
                mlstm_run(4, 1, b4[:, :].unsqueeze(1), a4[:, :].unsqueeze(1), Bl4[:, :].unsqueeze(1), sell4[:, :], cm4[:, :], cmT4[:, :],
                          lambda c: (q4T[:, :, :], vt4[0:4, :], kt4[0:4, :]),
                          lambda c: s_ps1[0:4, 8192:12288], lambda c: s_ps1[0:4, 12304:16400], lambda c: s_hraw_s[0:4, :])
                emit_states(o_mc_s, o_mn_s, o_mm_s)
            P.end_phase()


        if STAGE >= 6:
            P.begin_phase()
            nwb2 = P.sbuf([128, 4096], F32, "nwb2")
            P.dma(nwb2[:, :], ml_nw[0:1, :].partition_broadcast(128))
            hp2 = Pool(P, 2, [128, 4096], F32, "hp2")
            op2 = Pool(P, 2, [128, 4096], F32, "op2")
            zp2 = Pool(P, 2, [128, 4096], F32, "zp2")
            sq2 = P.sbuf([128, 4096], BF16, "sq2")
            ob2 = Pool(P, 2, [128, 4096], BF16, "ob2")
            sm3 = Pool(P, 4, [128, 32], F32, "sm3")
            def post_tile(nr, h_src, og_src, z_src, dst):
                R = slice(0, nr)
                ht = hp2.next(); P.dma(ht[R, :], h_src)
                ot = op2.next(); P.dma(ot[R, :], og_src)
                zt = zp2.next(); P.dma(zt[R, :], z_src)
                s2 = sm3.next()
                P.act(sq2[R, :], ht[R, :], AF.Square)
                P.op("vector", "tensor_reduce", out=s2[R, 0:8], in_=sq2[R, :].rearrange("p (a b) -> p a b", a=8), axis=AX.X, op=ALU.add)
                P.ts(s2[R, 0:8], s2[R, 0:8], 1.0 / 512, ALU.mult, EPS, ALU.add)
                P.act(s2[R, 8:16], s2[R, 0:8], AF.Sqrt)
                P.op("vector", "reciprocal", out=s2[R, 16:24], in_=s2[R, 8:16])
                P.tt(ht[R, :].rearrange("p (a b) -> p a b", a=8), ht[R, :].rearrange("p (a b) -> p a b", a=8), s2[R, 16:24].unsqueeze(2).to_broadcast([nr, 8, 512]), ALU.mult)
                P.tt(ht[R, :], ht[R, :], nwb2[R, :], ALU.mult, eng="gpsimd")
                P.act(ot[R, :], ot[R, :], AF.Sigmoid)
                P.tt(ht[R, :], ht[R, :], ot[R, :], ALU.mult)
                P.act(zt[R, :], zt[R, :], AF.Silu)
                o_ = ob2.next()
                P.tt(o_[R, :], ht[R, :], zt[R, :], ALU.mult, eng="gpsimd")
                P.dma(dst, o_[R, :])
            for i in range(NT):
                post_tile(128, s_hraw[i][:, :], s_og[i][:, :], s_z1[i][:, :], s_cat1[i][:, 0:4096])
            if STAGE >= 8:
                post_tile(4, s_hraw_s[0:4, :], s_ps1[0:4, 8192:12288], s_ps1[0:4, 12304:16400], s_cat1_s[0:4, 0:4096])
            P.end_phase()

        if STAGE >= 7:
            SMP = STAGE >= 8
            mem_attn_phase(1, s_qm1T, s_cat1, 4096, smp=(s_ps1[0:4, 16400:16912], s_cat1_s[0:4, 4096:4608]) if SMP else None)
            out_proj_phase(1, w_out_odd, s_cat1, lambda i, c0: s_h1[i][:, c0:c0 + 512], s_h2,
                           smp=(s_cat1_s, s_h1s, s_h2s) if SMP else None)
            P.begin_phase()
            fwb = P.sbuf([128, D], F32, "fwb")
            P.dma(fwb[:, :], final_norm_w[0:1, :].partition_broadcast(128))
            xp2 = Pool(P, 2, [128, D], F32, "xp2")
            yp2 = Pool(P, 2, [128, D], F32, "yp2")
            jk = P.sbuf([128, D], BF16, "jk2")
            sm2 = Pool(P, 4, [128, 8], F32, "sm2")
            for i in range(NT + (1 if SMP else 0)):
                if i == NT:
                    xt = xp2.next()
                    P.dma(xt[0:4, :], s_h2s[0:4, :])
                    ss = sm2.next()
                    P.memset(ss[:, 0:1], 0.0)
                    P.act(jk[0:4, :], xt[0:4, :], AF.Square, accum_out=ss[0:4, 0:1])
                    P.ts(ss[0:4, 1:2], ss[0:4, 0:1], 1.0 / D, ALU.mult, EPS, ALU.add)
                    P.act(ss[0:4, 3:4], ss[0:4, 1:2], AF.Sqrt)
                    P.op("vector", "reciprocal", out=ss[0:4, 2:3], in_=ss[0:4, 3:4])
                    yt = yp2.next()
                    P.stt(yt[0:4, :], xt[0:4, :], ss[0:4, 2:3], fwb[0:4, :], ALU.mult, ALU.mult)
                    outs.append(P.dma(o_ys[0:4, :], yt[0:4, :]))
                    continue
                xt = xp2.next()
                P.dma(xt[:, :], s_h2[i][:, :])
                ss = sm2.next()
                P.memset(ss[:, 0:1], 0.0)
                P.act(jk[:, :], xt[:, :], AF.Square, accum_out=ss[:, 0:1])
                P.ts(ss[:, 1:2], ss[:, 0:1], 1.0 / D, ALU.mult, EPS, ALU.add)
                P.act(ss[:, 3:4], ss[:, 1:2], AF.Sqrt)
                P.op("vector", "reciprocal", out=ss[:, 2:3], in_=ss[:, 3:4])
                yt = yp2.next()
                P.stt(yt[:, :], xt[:, :], ss[:, 2:3], fwb[:, :], ALU.mult, ALU.mult)
                outs.append(P.dma(o_y[i * 128:(i + 1) * 128, :], yt[:, :]))
            P.end_phase()

        P.emit(outs)
    return nc


_NC_CACHE = {}
DBG_SINK = None


NCORES = int(os.environ.get("MK_CORES", "8"))


def kernel(**inp):
    n = NCORES
    if "nc" not in _NC_CACHE:
        _NC_CACHE["nc"] = build_program()
    nc = _NC_CACHE["nc"]
    consts = host_consts()
    in_maps = []
    f32 = lambda a: np.ascontiguousarray(np.asarray(a, dtype=np.float32))
    shared = {
        "norm_w": f32(inp["norm_w"]), "mem_norm_w": f32(inp["mem_norm_w"]),
        "final_norm_w": f32(inp["final_norm_w"]).reshape(1, D),
        "w_mem_kv": f32(inp["w_mem_kv"]), "w_in_even": f32(inp["w_in_even"][0]),
        "hgrn_lb": f32(inp["hgrn_lb_logits"]), "hgrn_nw": f32(inp["hgrn_norm_w"]),
        "rel_bias": f32(inp["rel_bias"]), "b_gate": f32(inp["b_nsa_gate"]), "w_cmp1": f32(inp["w_cmp1"][0]),
        "b_cmp1": f32(inp["b_cmp1"][0]), "w_cmp2": f32(inp["w_cmp2"][0]), "pe_cmp": f32(inp["pe_cmp"][0]).reshape(64, 128),
        "w_out_even": f32(inp["w_out_even"][0]),
        "w_in_odd": f32(inp["w_in_odd"][0]), "w_out_odd": f32(inp["w_out_odd"][0]), "b_if": f32(inp["b_mlstm_if"][0]),
        "ml_nw": f32(inp["mlstm_norm_w"][0]).reshape(1, 4096),
    }
    for nm_, key in (("pool_ck", "cache_cmp_k"), ("pool_cv", "cache_cmp_v"), ("pool_sk", "cache_sel_k"), ("pool_sv", "cache_sel_v")):
        shared[nm_] = f32(inp[key][0]).reshape(1280 * 128, 512)
    shared.update(consts)
    for c in range(n):
        b = c % 4
        m = dict(shared)
        m["x_p"] = f32(inp["x_prompt"][b])
        m["x_s"] = f32(inp["x_sample"][c])
        m["st_c"] = f32(inp["state_mlstm_c"][0, c])
        m["st_n"] = f32(inp["state_mlstm_n"][0, c])
        m["st_m"] = f32(inp["state_mlstm_m"][0, c]).reshape(1, 8)
        m["page_tab"] = np.ascontiguousarray(np.asarray(inp["page_table"][c], dtype=np.int32).reshape(1, 128))
        m["memk_s"] = f32(inp["cache_mem_k"][:, c]).reshape(2, 256, 512)
        m["memv_s"] = f32(inp["cache_mem_v"][:, c]).reshape(2, 256, 512)
        m["st_hgrn"] = f32(inp["state_hgrn"][0, c])
        m["win_k_in"] = f32(inp["cache_win_k"][0, c]).reshape(512, 512)
        m["win_v_in"] = f32(inp["cache_win_v"][0, c]).reshape(512, 512)
        m["mem_p"] = f32(inp["mem_prompt"][b])
        in_maps.append(m)
    res = run_bass_kernel_spmd(nc, in_maps, core_ids=list(range(n)))
    R = list(res.results)
    while len(R) < 8:
        R.append(R[0])
    B = 4
    mem_k = np.stack([R[b]["o_memk"] for b in range(B)], axis=1).reshape(2, B, 256, 4, 128)
    mem_v = np.stack([R[b]["o_memv"] for b in range(B)], axis=1).reshape(2, B, 256, 4, 128)
    kv = [np.stack([R[b][f"o_kv{j}"] for b in range(B)], axis=0).reshape(1, B, T, 4, 128) for j in range(6)]
    if DBG_SINK is not None:
        DBG_SINK(R)
    z = lambda *s: np.zeros(s, np.float32)
    yp = np.stack([R[b]["o_y"] for b in range(B)], axis=0) if STAGE >= 7 else z(B, T, D)
    ys = np.stack([R[c]["o_ys"] for c in range(8)], axis=0) if STAGE >= 8 else z(8, 4, D)
    outs = (yp, ys, mem_k, mem_v, kv[0], kv[1], kv[2], kv[3],
            np.ascontiguousarray(kv[4][:, :, -512:]), np.ascontiguousarray(kv[5][:, :, -512:]),
            (np.stack([R[b]["o_hgrn"] for b in range(B)], axis=0)[None] if STAGE >= 2 else z(1, B, 16, 128, 128)), (np.stack([R[b]["o_mc"] for b in range(B)], axis=0)[None] if STAGE >= 6 else z(1, B, 8, 512, 256)),
            (np.stack([R[b]["o_mn"] for b in range(B)], axis=0)[None] if STAGE >= 6 else z(1, B, 8, 256)),
            (np.stack([R[b]["o_mm"].reshape(8) for b in range(B)], axis=0)[None] if STAGE >= 6 else z(1, B, 8)),
            *[np.stack([R[c][f"o_kvs{j}"] for c in range(8)], axis=0).reshape(1, 8, 4, 4, 128) for j in range(4)],
            *[np.stack([R[c][f"o_wins{j}"] for c in range(8)], axis=0).reshape(1, 8, 512, 4, 128) for j in range(2)],
            (np.stack([R[c]["o_hgrn_s"] for c in range(8)], axis=0)[None] if STAGE >= 2 else z(1, 8, 16, 128, 128)),
            (np.stack([R[c]["o_mc_s"] for c in range(8)], axis=0)[None] if STAGE >= 8 else z(1, 8, 8, 512, 256)),
            (np.stack([R[c]["o_mn_s"] for c in range(8)], axis=0)[None] if STAGE >= 8 else z(1, 8, 8, 256)),
            (np.stack([R[c]["o_mm_s"].reshape(8) for c in range(8)], axis=0)[None] if STAGE >= 8 else z(1, 8, 8)))
    return outs
```
